# Optimizing a Trainium2 kernel written in Bass

```python
import math
import jax, jax.numpy as jnp
from jax import lax
import numpy as np

D_MODEL = 4096
BATCH = 2
SEQ = 8192
DEPTH = 1
DEC_BATCH = 1
DEC_SEQ = 8192
PAST_LEN = 128

RMS_EPS = 1e-6
POOL_WIDTH = D_MODEL // 2
POOL_WINDOWS = (2, 4, 8, 16)
N_POOL_GROUPS = len(POOL_WINDOWS)
POOL_GROUP = POOL_WIDTH // N_POOL_GROUPS
N_HEADS = 16
NOPE_DIM = 128
ROPE_DIM = 64
V_DIM = 128
QK_DIM = NOPE_DIM + ROPE_DIM
Q_LORA = D_MODEL // 4
KV_LORA = D_MODEL // 8
MLA_WIDTH = N_HEADS * V_DIM
ROPE_THETA = 10000.0
Q_BLOCK = 128
D_FF = 11008
CONV_W = 3
OFF_POOL = 0
OFF_CQ = OFF_POOL + POOL_WIDTH
OFF_CKV = OFF_CQ + Q_LORA
OFF_KR = OFF_CKV + KV_LORA
OFF_GP = OFF_KR + ROPE_DIM
OFF_GM = OFF_GP + D_MODEL
IN_COLS = OFF_GM + D_MODEL

kernel_name = "gated_pool_mla_convglu_encoder"


def _rmsnorm(x, g):
    xf = x.astype(jnp.float32)
    y = xf * lax.rsqrt(jnp.mean(xf * xf, axis=-1, keepdims=True) + RMS_EPS)
    return (y * g.astype(jnp.float32)).astype(x.dtype)


def _rope_tables(S):
    inv = 1.0 / (ROPE_THETA ** (jnp.arange(0, ROPE_DIM, 2, dtype=jnp.float32) / ROPE_DIM))
    ang = jnp.arange(S, dtype=jnp.float32)[:, None] * inv[None, :]
    return jnp.cos(ang), jnp.sin(ang)


def _apply_rope(x, cos, sin):
    xf = x.astype(jnp.float32)
    x1, x2 = jnp.split(xf, 2, axis=-1)
    c = cos[None, :, None, :]
    s = sin[None, :, None, :]
    out = jnp.concatenate([x1 * c - x2 * s, x2 * c + x1 * s], axis=-1)
    return out.astype(x.dtype)


def _multiscale_pool(u):
    B, S, _ = u.shape
    ug = u.reshape(B, S, N_POOL_GROUPS, POOL_GROUP).astype(jnp.float32)
    cs = jnp.pad(jnp.cumsum(ug, axis=1), ((0, 0), (1, 0), (0, 0), (0, 0)))
    t = jnp.arange(S)
    means = []
    for gi, w in enumerate(POOL_WINDOWS):
        lo = jnp.clip(t - w // 2, 0, S)
        hi = jnp.clip(t + (w - w // 2), 0, S)
        csg = cs[:, :, gi]
        cnt = (hi - lo).astype(jnp.float32)[None, :, None]
        means.append((csg[:, hi] - csg[:, lo]) / cnt)
    mean = jnp.stack(means, axis=2)
    return (mean - ug).astype(u.dtype)


def _attention(q, k, v):
    B, S, H, Dq = q.shape
    nblk = S // Q_BLOCK
    scale = 1.0 / math.sqrt(Dq)
    qb = q.reshape(B, nblk, Q_BLOCK, H, Dq).transpose(1, 0, 2, 3, 4)

    def one_block(qi):
        s = jnp.einsum('bqhd,bkhd->bhqk', qi, k, preferred_element_type=jnp.float32) * scale
        p = jax.nn.softmax(s, axis=-1)
        return jnp.einsum('bhqk,bkhd->bqhd', p.astype(v.dtype), v)

    o = lax.map(one_block, qb)
    return o.transpose(1, 0, 2, 3, 4).reshape(B, S, H * V_DIM)


def _dwconv3(a, w, b):
    ap = jnp.pad(a, ((0, 0), (1, 1), (0, 0)))
    return ap[:, :-2] * w[0] + ap[:, 1:-1] * w[1] + ap[:, 2:] * w[2] + b


def _layer(x, norm_mix_gain, w_in, pool_w, pool_scale, q_a_norm_gain, w_uq, kv_a_norm_gain,
           w_ukv, q_norm_gain, k_norm_gain, w_branch_pool, w_branch_mla, w_o, norm_ffn_gain,
           w_up, conv_w, conv_b, w_down):
    B, S, _ = x.shape
    h = _rmsnorm(x, norm_mix_gain)
    z = h @ w_in
    u_pool = z[..., OFF_POOL:OFF_CQ]
    c_q = z[..., OFF_CQ:OFF_CKV]
    c_kv = z[..., OFF_CKV:OFF_KR]
    k_rope = z[..., OFF_KR:OFF_GP]
    g_pool = z[..., OFF_GP:OFF_GM]
    g_mla = z[..., OFF_GM:IN_COLS]

    pooled = _multiscale_pool(u_pool)
    a_out = jnp.einsum('bsgc,gcd->bsgd', pooled, pool_w).reshape(B, S, POOL_WIDTH) * pool_scale

    q = (_rmsnorm(c_q, q_a_norm_gain) @ w_uq).reshape(B, S, N_HEADS, QK_DIM)
    kv = (_rmsnorm(c_kv, kv_a_norm_gain) @ w_ukv).reshape(B, S, N_HEADS, NOPE_DIM + V_DIM)
    k_nope, v = kv[..., :NOPE_DIM], kv[..., NOPE_DIM:]
    k_r = jnp.broadcast_to(k_rope[:, :, None, :], (B, S, N_HEADS, ROPE_DIM))
    k = jnp.concatenate([k_nope, k_r], axis=-1)
    q = _rmsnorm(q, q_norm_gain)
    k = _rmsnorm(k, k_norm_gain)
    cos, sin = _rope_tables(S)
    q = jnp.concatenate([q[..., :NOPE_DIM], _apply_rope(q[..., NOPE_DIM:], cos, sin)], axis=-1)
    k = jnp.concatenate([k[..., :NOPE_DIM], _apply_rope(k[..., NOPE_DIM:], cos, sin)], axis=-1)
    b_out = _attention(q, k, v)

    m = jax.nn.sigmoid(g_pool) * (a_out @ w_branch_pool) + jax.nn.sigmoid(g_mla) * (b_out @ w_branch_mla)
    x = x + m @ w_o

    h2 = _rmsnorm(x, norm_ffn_gain)
    up = _dwconv3(h2 @ w_up, conv_w, conv_b)
    gate, val = up[..., :D_FF], up[..., D_FF:]
    return x + (jax.nn.silu(gate) * val) @ w_down


def _trunk(x, norm_mix_gain, w_in, pool_w, pool_scale, q_a_norm_gain, w_uq, kv_a_norm_gain,
           w_ukv, q_norm_gain, k_norm_gain, w_branch_pool, w_branch_mla, w_o, norm_ffn_gain,
           w_up, conv_w, conv_b, w_down):
    for l in range(DEPTH):
        x = _layer(x, norm_mix_gain[l], w_in[l], pool_w[l], pool_scale[l], q_a_norm_gain[l],
                   w_uq[l], kv_a_norm_gain[l], w_ukv[l], q_norm_gain[l], k_norm_gain[l],
                   w_branch_pool[l], w_branch_mla[l], w_o[l], norm_ffn_gain[l], w_up[l],
                   conv_w[l], conv_b[l], w_down[l])
    return x


def setup_inputs(seed: int = 0) -> dict:
    key = jax.random.key(seed)
    ks = jax.random.split(key, 20)
    f32 = jnp.float32
    L = DEPTH

    def nrm(k, shape, fan_in):
        return jax.random.normal(k, shape, f32) * (fan_in ** -0.5)

    def gain(k, n):
        return 1.0 + 0.02 * jax.random.normal(k, (L, n), f32)

    return {
        "x_prompt": jax.random.normal(ks[0], (BATCH, SEQ, D_MODEL), f32),
        "x_sample": jax.random.normal(ks[1], (DEC_BATCH, DEC_SEQ, D_MODEL), f32),
        "norm_mix_gain": gain(ks[2], D_MODEL),
        "w_in": nrm(ks[3], (L, D_MODEL, IN_COLS), D_MODEL),
        "pool_w": nrm(ks[4], (L, N_POOL_GROUPS, POOL_GROUP, POOL_GROUP), POOL_GROUP),
        "pool_scale": 1.0 + 0.1 * jax.random.normal(ks[5], (L, POOL_WIDTH), f32),
        "q_a_norm_gain": gain(ks[6], Q_LORA),
        "w_uq": nrm(ks[7], (L, Q_LORA, N_HEADS * QK_DIM), Q_LORA),
        "kv_a_norm_gain": gain(ks[8], KV_LORA),
        "w_ukv": nrm(ks[9], (L, KV_LORA, N_HEADS * (NOPE_DIM + V_DIM)), KV_LORA),
        "q_norm_gain": gain(ks[10], QK_DIM),
        "k_norm_gain": gain(ks[11], QK_DIM),
        "w_branch_pool": nrm(ks[12], (L, POOL_WIDTH, D_MODEL), POOL_WIDTH),
        "w_branch_mla": nrm(ks[13], (L, MLA_WIDTH, D_MODEL), MLA_WIDTH),
        "w_o": nrm(ks[14], (L, D_MODEL, D_MODEL), D_MODEL),
        "norm_ffn_gain": gain(ks[15], D_MODEL),
        "w_up": nrm(ks[16], (L, D_MODEL, 2 * D_FF), D_MODEL),
        "conv_w": nrm(ks[17], (L, CONV_W, 2 * D_FF), CONV_W),
        "conv_b": 0.01 * jax.random.normal(ks[18], (L, 2 * D_FF), f32),
        "w_down": nrm(ks[19], (L, D_FF, D_MODEL), D_FF),
    }


def reference(x_prompt, x_sample, norm_mix_gain, w_in, pool_w, pool_scale, q_a_norm_gain, w_uq,
              kv_a_norm_gain, w_ukv, q_norm_gain, k_norm_gain, w_branch_pool, w_branch_mla, w_o,
              norm_ffn_gain, w_up, conv_w, conv_b, w_down):
    y_prompt = _trunk(x_prompt, norm_mix_gain, w_in, pool_w, pool_scale, q_a_norm_gain, w_uq,
                      kv_a_norm_gain, w_ukv, q_norm_gain, k_norm_gain, w_branch_pool,
                      w_branch_mla, w_o, norm_ffn_gain, w_up, conv_w, conv_b, w_down)
    y_sample = _trunk(x_sample, norm_mix_gain, w_in, pool_w, pool_scale, q_a_norm_gain, w_uq,
                      kv_a_norm_gain, w_ukv, q_norm_gain, k_norm_gain, w_branch_pool,
                      w_branch_mla, w_o, norm_ffn_gain, w_up, conv_w, conv_b, w_down)
    return (y_prompt, y_sample)
```

```python
import math
from contextlib import ExitStack

import numpy as np
import concourse.bass as bass
import concourse.mybir as mybir
from concourse.bass_utils import run_bass_kernel_spmd

F32 = mybir.dt.float32
BF16 = mybir.dt.bfloat16
AF = mybir.ActivationFunctionType
ALU = mybir.AluOpType
EPS = 1e-6
NSEQ = 3
ROPE = 64
NOPE = 128
VD = 128
QK = 192
WINDOWS = (2, 4, 8, 16)
HALO = 9


class Cfg:
    def __init__(self, D=4096, S=8192, H=16, DFF=11008, NCORE=8):
        self.D, self.S, self.H, self.DFF, self.NCORE = D, S, H, DFF, NCORE
        self.KC = D // 128
        self.PW = D // 2
        self.PG = self.PW // 4
        self.PC = self.PW // 128
        self.PGC = self.PG // 128
        self.QL = D // 4
        self.QC = self.QL // 128
        self.KVL = D // 8
        self.KVC = self.KVL // 128
        self.FC = DFF // 128
        self.TQ = S // NCORE
        self.TU = self.TQ + 2 * HALO
        self.TE = self.TQ + 2
        self.OFF_CQ = self.PW
        self.OFF_CKV = self.OFF_CQ + self.QL
        self.OFF_KR = self.OFF_CKV + self.KVL
        self.OFF_GP = self.OFF_KR + ROPE
        self.OFF_GM = self.OFF_GP + D
        self.IN_COLS = self.OFF_GM + D
        self.SBT = min(256, S)
        self.TF = min(512, self.TQ)
        self.KCH = min(4096, S)
        self.MA = H * VD
        self.MAC = self.MA // 128


class Buf:
    __slots__ = ("name", "w", "r", "dw", "dr")

    def __init__(self, name):
        self.name = name
        self.w = {}
        self.r = {}
        self.dw = []
        self.dr = []


class Op:
    __slots__ = ("eng", "meth", "args", "kw", "dma", "deps", "inc", "sem", "val")

    def __init__(self, eng, meth, args, kw, dma):
        self.eng, self.meth, self.args, self.kw, self.dma = eng, meth, args, kw, dma
        self.deps = []
        self.inc = False
        self.sem = None
        self.val = 0


SEM_LIMIT = 30000
import os
P1_STOP = int(os.environ.get("P1_STOP", "0"))
MIX_STOP = int(os.environ.get("MIX_STOP", "0"))
FFN_STOP = int(os.environ.get("FFN_STOP", "0"))
RING = 12


class Prog:
    ENGS = ("pe", "act", "dve", "pool", "sp")

    def __init__(self, sems):
        self.ops = {e: [] for e in self.ENGS}
        self.order = []
        self.free_sems = list(sems)
        self.phase_bufs = []
        self.cell_buf = Buf("cell")

    def buf(self, name, persistent=False):
        b = Buf(name)
        if not persistent:
            self.phase_bufs.append(b)
        return b

    def bufs(self, name, n, persistent=False):
        return [self.buf(f"{name}{i}", persistent) for i in range(n)]

    def _add(self, eng, meth, args, kw, r, w, dma):
        op = Op(eng, meth, args, kw, dma)
        deps = op.deps
        for b in r:
            for e, p in b.w.items():
                if e == eng and not dma and eng == "pe":
                    continue
                deps.append(p)
            deps.extend(b.dw)
        for b in w:
            for e, p in b.r.items():
                if e == eng and not dma and eng == "pe":
                    continue
                deps.append(p)
            for e, p in b.w.items():
                if e == eng and not dma and eng == "pe":
                    continue
                deps.append(p)
            deps.extend(b.dr)
            deps.extend(b.dw)
        for b in r:
            if dma:
                b.dr.append(op)
            else:
                b.r[eng] = op
        for b in w:
            if dma:
                b.w = {}
                b.r = {}
                b.dr = []
                b.dw = [op]
            else:
                b.w = {eng: op}
                b.r = {}
                b.dr = []
                b.dw = []
        for p in deps:
            p.inc = True
        if dma:
            op.inc = True
        self.ops[eng].append(op)
        self.order.append(op)
        return op

    def op(self, eng, meth, *args, r=(), w=(), **kw):
        return self._add(eng, meth, args, kw, r, w, False)

    def dma(self, q, out, in_, r=(), w=(), wa=(), **kw):
        kw = dict(kw)
        kw["out"] = out
        kw["in_"] = in_
        op = self._add(q, "dma_start", (), kw, r, w, True)
        for b in wa:
            b.dw.append(op)
        return op

    def barrier(self, cells):
        bl = list(self.phase_bufs) + [self.cell_buf]
        self._add("act", "memzero", (cells["act"],), {}, [], bl, False)
        for eng in ("dve", "pool"):
            self._add(eng, "memset", (cells[eng], 0.0), {}, [], bl, False)
        self._add("sp", "dma_start", (), dict(out=cells["sp_dst"], in_=cells["sp_src"]), [], bl, True)

    def drop_bufs(self):
        self.phase_bufs = []

    def assign(self):
        cur = {}
        ring = {q: [dict(sem=None, val=0, last=None) for _ in range(RING)] for q in ("sp", "pool")}
        rcnt = {"sp": 0, "pool": 0}
        for op in self.order:
            if not op.inc:
                continue
            if op.dma:
                slots = ring[op.eng]
                sl = slots[rcnt[op.eng] % len(slots)]
                rcnt[op.eng] += 1
                if sl["last"] is not None:
                    op.deps.append(sl["last"])
                if sl["sem"] is None or sl["val"] + 16 > SEM_LIMIT:
                    sl["sem"] = self.free_sems.pop()
                    sl["val"] = 0
                sl["val"] += 16
                sl["last"] = op
                op.sem, op.val = sl["sem"], sl["val"]
            else:
                cc = cur.get(op.eng)
                if cc is None or cc[1] + 1 > SEM_LIMIT:
                    cc = [self.free_sems.pop(), 0]
                    cur[op.eng] = cc
                cc[1] += 1
                op.sem, op.val = cc[0], cc[1]

    def emit(self, eng_name, eng):
        known = {}
        for op in self.ops[eng_name]:
            need = {}
            for p in op.deps:
                k = id(p.sem)
                if known.get(k, 0) >= p.val:
                    continue
                if k not in need or need[k][1] < p.val:
                    need[k] = (p.sem, p.val)
            for k, (sem, val) in need.items():
                eng.wait_ge(sem, val)
                known[k] = val
            ins = getattr(eng, op.meth)(*op.args, **op.kw)
            if op.inc:
                ins.then_inc(op.sem, 16 if op.dma else 1)


class Arena:
    def __init__(self, ap_f32, nbytes):
        self.ap = ap_f32
        self.nbytes = nbytes
        self.off = 0

    def seek(self, kb):
        self.off = int(kb * 1024)

    def check(self, kb):
        assert self.off <= kb * 1024, f"arena region overflow: {self.off} > {kb}KB"

    def alloc(self, shape, dtype, parts=128):
        esz = 4 if dtype == F32 else 2
        n = int(np.prod(shape))
        nb = (n * esz + 3) // 4 * 4
        assert self.off + nb <= self.nbytes, f"arena overflow: {self.off}+{nb} > {self.nbytes}"
        a = self.ap[0:parts, self.off // 4:(self.off + nb) // 4]
        self.off += nb
        if dtype != F32:
            a = a.bitcast(dtype)
            if a.shape[1] != n:
                a = a[:, 0:n]
        if len(shape) == 2:
            a = a.rearrange("p (a b) -> p a b", a=shape[0])
        elif len(shape) == 3:
            a = a.rearrange("p (a b c) -> p a b c", a=shape[0], b=shape[1])
        return a


def tiles_of(n, t):
    return [(i, min(t, n - i)) for i in range(0, n, t)]


def blocks_of(n):
    bl = [(i * 128, 128) for i in range(n // 128)]
    if n % 128:
        bl.append((n - 128, 128))
    return bl


class Rot:
    def __init__(self, items):
        self.items = list(items)
        self.i = 0

    def next(self):
        v = self.items[self.i % len(self.items)]
        self.i += 1
        return v


def build(cfg, debug=False, stages=("p1", "mix", "ffn")):
    c = cfg
    D, S, H, KC = c.D, c.S, c.H, c.KC
    TE, TU, TQ = c.TE, c.TU, c.TQ
    nc = bass.Bass("TRN2", target_bir_lowering=False)

    def din(name, shape, dt=F32):
        return nc.dram_tensor(name, list(shape), dt, kind="ExternalInput").ap()

    def dscr(name, shape, dt):
        return nc.dram_tensor(name, list(shape), dt, kind="ExternalOutput" if debug else "Internal").ap()

    xall = din("xall", [NSEQ * S, D])
    xe = din("xe", [NSEQ, TU, D])
    cosk = din("cosk", [S, 32]); sink = din("sink", [S, 32])
    cosq = din("cosq", [TE, 32]); sinq = din("sinq", [TE, 32])
    invcnt = din("invcnt", [128, 4, TE])
    valid = din("valid", [TE, 1])
    g_mix = din("g_mix", [128, KC]); g_ffn = din("g_ffn", [128, KC])
    g_qa = din("g_qa", [128, c.QC]); g_kva = din("g_kva", [128, c.KVC])
    g_q_rep = din("g_q_rep", [128, 2 * QK]); g_kr_rep = din("g_kr_rep", [128, ROPE]); g_kn = din("g_kn", [128, 1])
    pscale = din("pscale", [128, c.PC])
    convw = din("convw", [128, 3, 2 * c.FC]); convb = din("convb", [128, 2 * c.FC])
    ident_in = din("ident", [128, 128])
    w_in = din("w_in", [D, c.IN_COLS])
    pool_w = din("pool_w", [c.PW, c.PG])
    w_uq = din("w_uq", [c.QL, H * QK])
    w_ukv = din("w_ukv", [c.KVL, H * 256])
    w_bp = din("w_bp", [c.PW, D]); w_bm = din("w_bm", [c.MA, D])
    w_o = din("w_o", [D, D])
    w_up = din("w_up", [D, 2 * c.DFF]); w_down = din("w_down", [c.DFF, D])
    y = nc.dram_tensor("yout", [NSEQ, TQ, D], F32, kind="ExternalOutput").ap()
    KT = dscr("KT", [NSEQ, H, 128, S], BF16)
    KRT = dscr("KRT", [NSEQ, ROPE, S], BF16)
    VV = dscr("VV", [NSEQ, S, H * VD], BF16)
    GATE = dscr("GATE", [NSEQ, 2, KC, 128, TE], BF16)
    X1 = dscr("X1", [NSEQ, TE, D], F32)
    dummy_d = nc.dram_tensor("dummy_d", [1, 16], F32, kind="Internal").ap()

    NBLK_ALL = NSEQ * S // 128
    NBE = (TE + 127) // 128
    ARENA_BYTES = 190 * 1024
    pers_sizes = dict(ones_f=128, ident=64, ones_b=64, rsk=NBLK_ALL * H, cells=32, gmix=KC, gffn=KC, gqa=c.QC,
                      gkva=c.KVC, gkr=ROPE, gkn=1, pscale=c.PC, convw=6 * c.FC, convb=2 * c.FC)
    PERS_F32 = sum(pers_sizes.values()) + 16

    with ExitStack() as es:
        pers = es.enter_context(nc.sbuf_tensor("pers", [128, PERS_F32], F32))
        arena_t = es.enter_context(nc.sbuf_tensor("arena", [128, ARENA_BYTES // 4], F32))
        psum = [es.enter_context(nc.psum_tensor(f"ps{i}", [128, 512], F32)) for i in range(8)]
        sems = [es.enter_context(nc.semaphore(f"sm{i}")) for i in range(96)]
        block = es.enter_context(nc.Block())

        P = Prog(sems)
        A = Arena(arena_t, ARENA_BYTES)
        pv = {}
        o = 0
        for k, n in pers_sizes.items():
            pv[k] = pers[:, o:o + n]
            o += n
        ones_f = pv["ones_f"]
        ident = pv["ident"].bitcast(BF16)
        ones_b = pv["ones_b"].bitcast(BF16)
        RSK = pv["rsk"].rearrange("p (b h) -> p b h", h=H)
        cl = pv["cells"]
        cells = {"act": cl[:, 0:1], "dve": cl[:, 1:2], "pool": cl[:, 2:3], "sp_dst": cl[0:1, 8:24], "sp_src": ident_in[0:1, 0:16]}
        gmix, gffn, gqa, gkva = pv["gmix"], pv["gffn"], pv["gqa"], pv["gkva"]
        gkr, gkn, psc = pv["gkr"], pv["gkn"], pv["pscale"]
        cw = pv["convw"].rearrange("p (t j) -> p t j", t=3)
        cb = pv["convb"]
        ps = [p[:] for p in psum]
        psb = [p[:].bitcast(BF16) for p in psum]
        PB = P.bufs("psum", 8, persistent=True)
        B_const = P.buf("const", persistent=True)
        B_rsk = P.buf("rsk", persistent=True)
        B_y = P.buf("y", persistent=True)
        B_kv = P.bufs("kvscr", NSEQ, persistent=True)
        B_gate = P.bufs("gate", NSEQ, persistent=True)
        B_x1 = P.bufs("x1", NSEQ, persistent=True)

        A.seek(189)
        t_id = A.alloc([128], F32)
        bt = P.buf("t_id")
        P.dma("sp", t_id, ident_in, w=[bt])
        P.op("dve", "tensor_copy", ident, t_id, r=[bt], w=[B_const])
        P.op("dve", "memset", ones_f, 1.0, w=[B_const])
        P.op("dve", "memset", ones_b, 1.0, w=[B_const])
        P.op("dve", "memset", cl, 0.0, w=[B_const, P.cell_buf])
        for dst, src in ((gmix, g_mix), (gffn, g_ffn), (gqa, g_qa), (gkva, g_kva), (gkr, g_kr_rep),
                         (gkn, g_kn), (psc, pscale), (cw, convw), (cb, convb)):
            P.dma("sp", dst, src, w=[B_const])
        P.barrier(cells)
        P.drop_bufs()

        def alloc_norm_tmp():
            tm = []
            for i in range(2):
                xn_ = A.alloc([D], BF16); Bn_ = P.buf("xn")
                tm.append(dict(xb=A.alloc([D], F32), junk=xn_, xn=xn_, st=A.alloc([4], F32),
                               Bx=P.buf("xb"), Bj=Bn_, Bn=Bn_, Bs=P.buf("st"), Bm=P.buf("msk")))
            return tm

        def norm_transpose(src, r, gain, hT, col0, t, BhT, rsrc=(), mask_ap=None):
            xb, junk, xn, st = t["xb"], t["junk"], t["xn"], t["st"]
            Bx, Bj, Bn, Bs, Bm = t["Bx"], t["Bj"], t["Bn"], t["Bs"], t["Bm"]
            P.dma("sp", xb[0:r, :], src, r=list(rsrc), w=[Bx])
            if mask_ap is not None:
                P.dma("sp", st[0:r, 3:4], mask_ap, w=[Bm])
            P.op("act", "activation", out=junk[0:r, :], in_=xb[0:r, :], func=AF.Square, accum_out=st[0:r, 0:1], r=[Bx], w=[Bj, Bs])
            P.op("act", "activation", out=st[0:r, 1:2], in_=st[0:r, 0:1], func=AF.Sqrt, bias=EPS, scale=1.0 / D, r=[Bs], w=[Bs])
            P.op("dve", "reciprocal", st[0:r, 2:3], st[0:r, 1:2], r=[Bs], w=[Bs])
            if mask_ap is not None:
                P.op("dve", "tensor_tensor", st[0:r, 2:3], st[0:r, 2:3], st[0:r, 3:4], ALU.mult, r=[Bs, Bm], w=[Bs])
            P.op("dve", "tensor_scalar", xn[0:r, :], xb[0:r, :], st[0:r, 2:3], None, ALU.mult, r=[Bx, Bs], w=[Bn])
            for g4 in range(0, KC, 4):
                bank = (g4 // 4) % 2
                n4 = min(4, KC - g4)
                for i in range(n4):
                    kc = g4 + i
                    P.op("pe", "transpose", psb[bank][:, i * 128:i * 128 + r], xn[0:r, kc * 128:(kc + 1) * 128], ident[0:r, 0:r],
                         r=[Bn, B_const], w=[PB[bank]])
                for i in range(n4):
                    kc = g4 + i
                    if kc % 2 == 0:
                        P.op("act", "activation", out=hT[:, kc, col0:col0 + r], in_=psb[bank][:, i * 128:i * 128 + r], func=AF.Copy,
                             scale=gain[:, kc:kc + 1], r=[PB[bank], B_const], w=[BhT])
                    else:
                        P.op("dve", "tensor_scalar", hT[:, kc, col0:col0 + r], psb[bank][:, i * 128:i * 128 + r], gain[:, kc:kc + 1],
                             None, ALU.mult, r=[PB[bank], B_const], w=[BhT])

        def wdma(dst, src_rows, col0, ncols, B, row0=0, nrows=None):
            nrows = src_rows.shape[0] - row0 if nrows is None else nrows
            src = src_rows[row0:row0 + nrows, col0:col0 + ncols].rearrange("(k p) c -> p k c", p=128)
            P.dma("pool", dst, src, w=[B], max_dma_last_dim=4096)

        def phase1():
            SBT = c.SBT
            NB = SBT // 128
            KVC = c.KVC
            A.seek(0)
            Wckv = A.alloc([KC, c.KVL], BF16); Wkr = A.alloc([KC, ROPE], BF16); Wukv = A.alloc([KVC, H * 256], BF16)
            BW = P.buf("p1w")
            wdma(Wckv, w_in, c.OFF_CKV, c.KVL, BW)
            wdma(Wkr, w_in, c.OFF_KR, ROPE, BW)
            for h0 in range(0, H * 256, 1024):
                n = min(1024, H * 256 - h0)
                wdma(Wukv[:, :, h0:h0 + n], w_ukv, h0, n, BW)
            tmps = alloc_norm_tmp()
            hT = [A.alloc([KC, SBT], BF16) for _ in range(2)]
            BhT = P.bufs("hT", 2)
            ckvf = A.alloc([KVC, SBT], F32); Bckvf = P.bufs("ckvf", KVC)
            sq = [A.alloc([SBT], F32) for _ in range(2)]; Bsq = P.bufs("sq", 2)
            rstdkv = A.alloc([SBT], F32); Brkv = P.buf("rstdkv")
            ckvn = A.alloc([KVC, SBT], BF16); Bckvn = P.buf("ckvn")
            krs = A.alloc([NB, 8], F32); Bkrs = P.buf("krs")
            krg = A.alloc([ROPE], F32); Bkrg = P.buf("krg")
            cs = [A.alloc([2, 32], F32) for _ in range(2)]; Bcs = P.bufs("cs", 2)
            rt = A.alloc([4, 32], F32); Brt = P.buf("rt")
            krb = A.alloc([ROPE], BF16); Bkrb = P.buf("krb")
            ssqk = A.alloc([NB, H], F32); Bssqk = P.buf("ssqk")
            Vst = A.alloc([NB, H * VD], BF16); BVst = P.buf("Vst")
            KTst = A.alloc([H, SBT], BF16); BKT = P.buf("KTst")
            KRst = A.alloc([SBT], BF16); BKR = P.buf("KRst")
            junk2s = [A.alloc([QK], BF16) for _ in range(4)]; Bj2s = P.bufs("junk2", 4); jr = Rot([0, 1, 2, 3])
            nsb = S // SBT
            Bser = P.buf("ser")
            blk_i = 0
            for s in range(NSEQ):
                Bscr = B_kv[s]
                for sb in range(nsb):
                    it = s * nsb + sb
                    hTc = hT[it % 2]; BhTc = BhT[it % 2]
                    t0 = sb * SBT
                    for b in range(NB):
                        t = tmps[blk_i % 2]; blk_i += 1
                        row0 = s * S + t0 + b * 128
                        norm_transpose(xall[row0:row0 + 128, :], 128, gmix, hTc, b * 128, t, BhTc)
                    if P1_STOP == 1:
                        continue
                    for j in range(KVC):
                        bank = 2 + (j % 2)
                        for kc in range(KC):
                            P.op("pe", "matmul", ps[bank][:, 0:SBT], Wckv[:, kc, j * 128:(j + 1) * 128], hTc[:, kc, :],
                                 start=(kc == 0), stop=(kc == KC - 1), r=[BW, BhTc], w=[PB[bank]])
                        P.op("act", "activation", out=ckvf[:, j, :], in_=ps[bank][:, 0:SBT], func=AF.Copy, r=[PB[bank]], w=[Bckvf[j]])
                        P.op("dve", "tensor_tensor", sq[j % 2], ps[bank][:, 0:SBT], ckvf[:, j, :], ALU.mult,
                             r=[PB[bank], Bckvf[j]], w=[Bsq[j % 2]])
                        P.op("pe", "matmul", ps[4][:, 0:SBT], ones_f, sq[j % 2], start=(j == 0), stop=(j == KVC - 1),
                             r=[Bsq[j % 2], B_const], w=[PB[4]])
                    P.op("act", "activation", out=rstdkv, in_=ps[4][:, 0:SBT], func=AF.Sqrt, bias=EPS, scale=1.0 / c.KVL,
                         r=[PB[4]], w=[Brkv])
                    P.op("dve", "reciprocal", rstdkv, rstdkv, r=[Brkv], w=[Brkv])
                    for j in range(KVC):
                        P.op("dve", "scalar_tensor_tensor", ckvn[:, j, :], ckvf[:, j, :], gkva[:, j:j + 1], rstdkv, ALU.mult, ALU.mult,
                             r=[Bckvf[j], Brkv, B_const], w=[Bckvn])
                    if P1_STOP == 2:
                        continue
                    for b in range(NB):
                        for kc in range(KC):
                            P.op("pe", "matmul", ps[5][:, b * ROPE:(b + 1) * ROPE], hTc[:, kc, b * 128:(b + 1) * 128], Wkr[:, kc, :],
                                 start=(kc == 0), stop=(kc == KC - 1), r=[BW, BhTc], w=[PB[5]])
                    for b in range(NB):
                        pk = ps[5][:, b * ROPE:(b + 1) * ROPE]
                        pos0 = t0 + b * 128
                        csb = cs[b % 2]; Bcsb = Bcs[b % 2]
                        P.dma("sp", csb[:, 0, :], cosk[pos0:pos0 + 128, :], w=[Bcsb])
                        P.dma("sp", csb[:, 1, :], sink[pos0:pos0 + 128, :], w=[Bcsb])
                        ji = jr.next()
                        P.op("act", "activation", out=junk2s[ji][:, 0:ROPE], in_=pk, func=AF.Square, accum_out=krs[:, b, 0:1],
                             r=[PB[5]], w=[Bj2s[ji], Bkrs])
                        P.op("dve", "tensor_tensor", krg, pk, gkr, ALU.mult, r=[PB[5], B_const], w=[Bkrg])
                        x1 = krg[:, 0:32]; x2 = krg[:, 32:64]
                        P.op("dve", "tensor_tensor", rt[:, 0, :], x1, csb[:, 0, :], ALU.mult, r=[Bkrg, Bcsb], w=[Brt])
                        P.op("dve", "tensor_tensor", rt[:, 1, :], x2, csb[:, 1, :], ALU.mult, r=[Bkrg, Bcsb], w=[Brt])
                        P.op("dve", "tensor_tensor", rt[:, 2, :], x2, csb[:, 0, :], ALU.mult, r=[Bkrg, Bcsb], w=[Brt])
                        P.op("dve", "tensor_tensor", rt[:, 3, :], x1, csb[:, 1, :], ALU.mult, r=[Bkrg, Bcsb], w=[Brt])
                        P.op("dve", "tensor_tensor", krb[:, 0:32], rt[:, 0, :], rt[:, 1, :], ALU.subtract, r=[Brt], w=[Bkrb])
                        P.op("dve", "tensor_tensor", krb[:, 32:64], rt[:, 2, :], rt[:, 3, :], ALU.add, r=[Brt], w=[Bkrb])
                        P.op("pe", "transpose", psb[6][0:ROPE, b * 128:(b + 1) * 128], krb, ident, r=[Bkrb, B_const], w=[PB[6]])
                    P.op("act", "activation", out=KRst[0:ROPE, :], in_=psb[6][0:ROPE, 0:SBT], func=AF.Copy, r=[PB[6]], w=[BKR])
                    P.dma("sp", KRT[s, :, t0:t0 + SBT], KRst[0:ROPE, :], r=[BKR], wa=[Bscr])
                    if P1_STOP == 3:
                        continue
                    for b in range(NB):
                        for n in range(H // 2):
                            bank = (n % 2)
                            for kc in range(KVC):
                                P.op("pe", "matmul", ps[bank][:, :], ckvn[:, kc, b * 128:(b + 1) * 128], Wukv[:, kc, n * 512:(n + 1) * 512],
                                     start=(kc == 0), stop=(kc == KVC - 1), r=[Bckvn, BW], w=[PB[bank]])
                            pvw = ps[bank][:, :].rearrange("p (h t d) -> p h t d", h=2, t=2)
                            if P1_STOP == 31:
                                continue
                            P.op("dve", "tensor_copy", Vst[:, b, n * 256:(n + 1) * 256].rearrange("p (h d) -> p h d", h=2), pvw[:, :, 1, :],
                                 r=[PB[bank]], w=[BVst, Bser])
                            if P1_STOP == 32:
                                continue
                            for hh in range(2):
                                ji = jr.next()
                                if P1_STOP in (35, 36):
                                    P.op("act", "activation", out=junk2s[ji][:, 0:128], in_=ps[bank][:, hh * 256:hh * 256 + 128], func=AF.Square,
                                         r=[PB[bank]], w=[Bj2s[ji], Bssqk])
                                    continue
                                P.op("act", "activation", out=junk2s[ji][:, 0:128], in_=ps[bank][:, hh * 256:hh * 256 + 128], func=AF.Square,
                                     accum_out=ssqk[:, b, 2 * n + hh:2 * n + hh + 1], r=[PB[bank], Bser], w=[Bj2s[ji], Bssqk])
                    if P1_STOP in (31, 32, 33, 35, 36):
                        continue
                    for b in range(NB):
                        gb = (s * S + t0) // 128 + b
                        P.op("dve", "tensor_scalar", ssqk[:, b, :], ssqk[:, b, :], krs[:, b, 0:1], EPS * QK, ALU.add, ALU.add, r=[Bssqk, Bkrs], w=[Bssqk])
                        P.op("act", "activation", out=RSK[:, gb, :], in_=ssqk[:, b, :], func=AF.Sqrt,
                             r=[Bssqk], w=[B_rsk])
                        P.op("dve", "reciprocal", RSK[:, gb, :], RSK[:, gb, :], r=[B_rsk], w=[B_rsk])
                    if P1_STOP == 34:
                        continue
                    P.dma("sp", VV[s, t0:t0 + SBT, :].rearrange("(b p) f -> p b f", p=128), Vst, r=[BVst], wa=[Bscr])
                    if P1_STOP == 4:
                        continue
                    for h in range(H):
                        bank = 6 + (h % 2)
                        for kc in range(KVC):
                            P.op("pe", "matmul", ps[bank][:, 0:SBT], Wukv[:, kc, h * 256:h * 256 + 128], ckvn[:, kc, :],
                                 start=(kc == 0), stop=(kc == KVC - 1), r=[BW, Bckvn], w=[PB[bank]])
                        if h % 2 == 0:
                            P.op("act", "activation", out=KTst[:, h, :], in_=ps[bank][:, 0:SBT], func=AF.Copy, scale=gkn[:, 0:1],
                                 r=[PB[bank], B_const], w=[BKT])
                        else:
                            P.op("dve", "tensor_scalar", KTst[:, h, :], ps[bank][:, 0:SBT], gkn[:, 0:1], None, ALU.mult,
                                 r=[PB[bank], B_const], w=[BKT])
                    P.dma("sp", KT[s, :, :, t0:t0 + SBT].rearrange("h p t -> p h t"), KTst, r=[BKT], wa=[Bscr])

        tilesE = tiles_of(TE, 512)
        tilesU = tiles_of(TU, 512)
        blocksE = blocks_of(TE)

        def mixer(s):
            PC, PGC, QC, MAC = c.PC, c.PGC, c.QC, c.MAC
            A.seek(0)
            aT = A.alloc([PC, TE], BF16); BaT = P.buf("aT")
            A.check(33)
            A.seek(33)
            hT = A.alloc([KC, TU], BF16); BhT = P.buf("hT")
            A.check(99)
            A.seek(99)
            tmps = alloc_norm_tmp()
            for bi, (i0, r) in enumerate(blocks_of(TU)):
                norm_transpose(xe[s, i0:i0 + r, :], r, gmix, hT, i0, tmps[bi % 2], BhT)
            P.barrier(cells)
            if MIX_STOP == 1:
                P.drop_bufs()
                return
            A.seek(99)
            Wg = [A.alloc([KC, 512], BF16) for _ in range(2)]; BWg = P.bufs("Wg", 2)
            sg = [A.alloc([TE], BF16) for _ in range(2)]; Bsg = P.bufs("sg", 2)
            banks = Rot([0, 1, 2, 3, 4, 5])
            cnt = 0
            for gi, off in enumerate((c.OFF_GP, c.OFF_GM)):
                for pi in range(D // 512):
                    W = Wg[cnt % 2]; BW_ = BWg[cnt % 2]; cnt += 1
                    wdma(W, w_in, off + pi * 512, 512, BW_)
                    for jj in range(4):
                        j = pi * 4 + jj
                        sgt = sg[j % 2]; Bs_ = Bsg[j % 2]
                        for (q0, n) in tilesE:
                            bank = banks.next()
                            for kc in range(KC):
                                P.op("pe", "matmul", ps[bank][:, 0:n], W[:, kc, jj * 128:(jj + 1) * 128], hT[:, kc, 8 + q0:8 + q0 + n],
                                     start=(kc == 0), stop=(kc == KC - 1), r=[BW_, BhT], w=[PB[bank]])
                            P.op("act", "activation", out=sgt[:, q0:q0 + n], in_=ps[bank][:, 0:n], func=AF.Sigmoid, r=[PB[bank]], w=[Bs_])
                        P.dma("sp", GATE[s, gi, j], sgt, r=[Bs_], wa=[B_gate[s]])
            P.barrier(cells)
            if MIX_STOP == 2:
                P.drop_bufs()
                return
            A.seek(99)
            Wp = [A.alloc([KC, 256], BF16) for _ in range(2)]; BWp = P.bufs("Wp", 2)
            usb = [A.alloc([TU], F32) for _ in range(2)]; Bu = P.bufs("usb", 2)
            ta = A.alloc([TU], F32); tb = A.alloc([TU], F32); tc = A.alloc([TU], F32); Bta = P.buf("ta"); Btb = P.buf("tb"); Btc = P.buf("tc")
            pooled = A.alloc([PGC, TE], BF16); Bpl = P.buf("pooled")
            plw = [A.alloc([PGC, c.PG], BF16) for _ in range(2)]; Bplw = P.bufs("plw", 2)
            icn = [A.alloc([TE], F32) for _ in range(2)]; Bicn = P.bufs("icn", 2)
            banks = Rot([0, 1, 2, 3, 4, 5])
            cnt = 0
            for g in range(4):
                w = WINDOWS[g]
                wdma(plw[g % 2], pool_w, 0, c.PG, Bplw[g % 2], row0=g * c.PG, nrows=c.PG)
                P.dma("sp", icn[g % 2], invcnt[:, g, :], w=[Bicn[g % 2]])
                for cc in range(PGC):
                    ch = g * PGC + cc
                    if ch % 2 == 0:
                        W = Wp[cnt % 2]; BW_ = BWp[cnt % 2]; cnt += 1
                        ncol = min(256, c.PW - ch * 128)
                        wdma(W[:, :, 0:ncol], w_in, ch * 128, ncol, BW_)
                    u = usb[ch % 2]; Bu_ = Bu[ch % 2]
                    for (u0, n) in tilesU:
                        bank = banks.next()
                        for kc in range(KC):
                            P.op("pe", "matmul", ps[bank][:, 0:n], W[:, kc, (ch % 2) * 128:(ch % 2) * 128 + 128], hT[:, kc, u0:u0 + n],
                                 start=(kc == 0), stop=(kc == KC - 1), r=[BW_, BhT], w=[PB[bank]])
                        P.op("act", "activation", out=u[:, u0:u0 + n], in_=ps[bank][:, 0:n], func=AF.Copy, r=[PB[bank]], w=[Bu_])
                    if w == 2:
                        P.op("dve", "tensor_tensor", tc[:, 8:8 + TE], u[:, 7:7 + TE], u[:, 8:8 + TE], ALU.add, r=[Bu_], w=[Btc])
                    else:
                        P.op("dve", "tensor_tensor", ta[:, 0:TU - 1], u[:, 0:TU - 1], u[:, 1:TU], ALU.add, r=[Bu_], w=[Bta])
                        if w == 4:
                            P.op("dve", "tensor_tensor", tc[:, 8:8 + TE], ta[:, 6:6 + TE], ta[:, 8:8 + TE], ALU.add, r=[Bta], w=[Btc])
                        else:
                            P.op("dve", "tensor_tensor", tb[:, 0:TU - 3], ta[:, 0:TU - 3], ta[:, 2:TU - 1], ALU.add, r=[Bta], w=[Btb])
                            if w == 8:
                                P.op("dve", "tensor_tensor", tc[:, 8:8 + TE], tb[:, 4:4 + TE], tb[:, 8:8 + TE], ALU.add, r=[Btb], w=[Btc])
                            else:
                                P.op("dve", "tensor_tensor", ta[:, 0:TU - 7], tb[:, 0:TU - 7], tb[:, 4:TU - 3], ALU.add, r=[Btb], w=[Bta])
                                P.op("dve", "tensor_tensor", tc[:, 8:8 + TE], ta[:, 0:TE], ta[:, 8:8 + TE], ALU.add, r=[Bta], w=[Btc])
                    P.op("dve", "tensor_tensor", tc[:, 8:8 + TE], tc[:, 8:8 + TE], icn[g % 2], ALU.mult, r=[Btc, Bicn[g % 2]], w=[Btc])
                    P.op("dve", "tensor_tensor", pooled[:, cc, :], tc[:, 8:8 + TE], u[:, 8:8 + TE], ALU.subtract, r=[Btc, Bu_], w=[Bpl])
                for oc in range(PGC):
                    ch = g * PGC + oc
                    for (q0, n) in tilesE:
                        bank = banks.next()
                        for kc in range(PGC):
                            P.op("pe", "matmul", ps[bank][:, 0:n], plw[g % 2][:, kc, oc * 128:(oc + 1) * 128], pooled[:, kc, q0:q0 + n],
                                 start=(kc == 0), stop=(kc == PGC - 1), r=[Bplw[g % 2], Bpl], w=[PB[bank]])
                        P.op("act", "activation", out=aT[:, ch, q0:q0 + n], in_=ps[bank][:, 0:n], func=AF.Copy, scale=psc[:, ch:ch + 1],
                             r=[PB[bank], B_const], w=[BaT])
            P.barrier(cells)
            if MIX_STOP == 3:
                P.drop_bufs()
                return
            A.seek(99)
            cqnT = A.alloc([QC, TE], BF16); Bcqn = P.buf("cqnT")
            Wcq = [A.alloc([KC, 128], BF16) for _ in range(2)]; BWcq = P.bufs("Wcq", 2)
            cqf = A.alloc([QC, TE], F32); Bcqf = P.bufs("cqf", QC)
            sq = [A.alloc([512], F32) for _ in range(2)]; Bsq = P.bufs("sqq", 2)
            rstd = A.alloc([TE], F32); Brs = P.buf("rstdq")
            banks = Rot([0, 1, 2, 3])
            sqi = 0
            for j in range(QC):
                W = Wcq[j % 2]; BW_ = BWcq[j % 2]
                wdma(W, w_in, c.OFF_CQ + j * 128, 128, BW_)
                for ti, (q0, n) in enumerate(tilesE):
                    bank = banks.next()
                    for kc in range(KC):
                        P.op("pe", "matmul", ps[bank][:, 0:n], W[:, kc, :], hT[:, kc, 8 + q0:8 + q0 + n],
                             start=(kc == 0), stop=(kc == KC - 1), r=[BW_, BhT], w=[PB[bank]])
                    P.op("act", "activation", out=cqf[:, j, q0:q0 + n], in_=ps[bank][:, 0:n], func=AF.Copy, r=[PB[bank]], w=[Bcqf[j]])
                    sqt = sq[sqi % 2]; Bsq_ = Bsq[sqi % 2]; sqi += 1
                    P.op("dve", "tensor_tensor", sqt[:, 0:n], ps[bank][:, 0:n], cqf[:, j, q0:q0 + n], ALU.mult, r=[PB[bank], Bcqf[j]], w=[Bsq_])
                    P.op("pe", "matmul", ps[5 + ti][:, 0:n], ones_f, sqt[:, 0:n], start=(j == 0), stop=(j == QC - 1),
                         r=[Bsq_, B_const], w=[PB[5 + ti]])
            for ti, (q0, n) in enumerate(tilesE):
                P.op("act", "activation", out=rstd[:, q0:q0 + n], in_=ps[5 + ti][:, 0:n], func=AF.Sqrt, bias=EPS, scale=1.0 / c.QL,
                     r=[PB[5 + ti]], w=[Brs])
            P.op("dve", "reciprocal", rstd, rstd, r=[Brs], w=[Brs])
            for j in range(QC):
                P.op("dve", "scalar_tensor_tensor", cqnT[:, j, :], cqf[:, j, :], gqa[:, j:j + 1], rstd, ALU.mult, ALU.mult,
                     r=[Bcqf[j], Brs, B_const], w=[Bcqn])
            P.barrier(cells)
            if MIX_STOP == 4:
                P.drop_bufs()
                return
            A.seek(33)
            QTn = A.alloc([H, TE], BF16); QTr = A.alloc([H, TE], BF16); BQT = P.buf("QT")
            A.check(98)
            A.seek(99 + 2 * QC * TE / 1024 + 1)
            NP5 = 2 if H >= 4 else 1
            HH = H // NP5
            Wuq = A.alloc([QC, HH * QK], BF16); BWuq = P.buf("Wuq")
            gq = A.alloc([2 * QK], F32); csq = A.alloc([NBE, 2, 32], F32); Bc5 = P.buf("c5")
            P.dma("sp", gq, g_q_rep, w=[Bc5])
            for bi, (e0, r) in enumerate(blocksE):
                P.dma("sp", csq[0:r, bi, 0, :], cosq[e0:e0 + r, :], w=[Bc5])
                P.dma("sp", csq[0:r, bi, 1, :], sinq[e0:e0 + r, :], w=[Bc5])
            qsbf = A.alloc([HH * QK], F32); qsb = qsbf.rearrange("p (h d) -> p h d", h=HH); Bqsb = P.buf("qsb")
            ssqq = A.alloc([HH], F32); rsq = A.alloc([HH], F32); Bssq = P.buf("ssqq"); Brq = P.buf("rsq")
            qn = A.alloc([HH, NOPE], BF16); Bqn = P.buf("qn")
            rtq = A.alloc([4, HH, 32], F32); Brtq = P.buf("rtq")
            qrf = A.alloc([HH, ROPE], F32); Bqrf = P.buf("qrf")
            qr = A.alloc([HH, ROPE], BF16); Bqr = P.buf("qr")
            junk3s = [A.alloc([QK], BF16) for _ in range(4)]; Bj3s = P.bufs("junk3", 4); jr3 = Rot([0, 1, 2, 3])
            banks = Rot([0, 1, 2, 3])
            tb_ = Rot([4, 5, 6, 7])
            for hf in range(NP5):
                hb = hf * HH
                for h0 in range(0, HH * QK, 768):
                    n = min(768, HH * QK - h0)
                    wdma(Wuq[:, :, h0:h0 + n], w_uq, hb * QK + h0, n, BWuq)
                for bi, (e0, r) in enumerate(blocksE):
                    for hp in range(HH // 2):
                        bank = banks.next()
                        for kc in range(QC):
                            P.op("pe", "matmul", ps[bank][0:r, 0:2 * QK], cqnT[:, kc, e0:e0 + r], Wuq[:, kc, hp * 2 * QK:(hp + 1) * 2 * QK],
                                 start=(kc == 0), stop=(kc == QC - 1), r=[Bcqn, BWuq], w=[PB[bank]])
                        for hh in range(2):
                            ji = jr3.next()
                            P.op("act", "activation", out=junk3s[ji][0:r, :], in_=ps[bank][0:r, hh * QK:(hh + 1) * QK], func=AF.Square,
                                 accum_out=ssqq[0:r, 2 * hp + hh:2 * hp + hh + 1], r=[PB[bank]], w=[Bj3s[ji], Bssq])
                        P.op("dve", "tensor_tensor", qsbf[0:r, hp * 2 * QK:(hp + 1) * 2 * QK], ps[bank][0:r, 0:2 * QK], gq[0:r, :],
                             ALU.mult, r=[PB[bank], Bc5, Bssq], w=[Bqsb])
                    P.op("act", "activation", out=rsq[0:r, :], in_=ssqq[0:r, :], func=AF.Sqrt, bias=EPS, scale=1.0 / QK, r=[Bssq], w=[Brq])
                    P.op("dve", "reciprocal", rsq[0:r, :], rsq[0:r, :], r=[Brq], w=[Brq])
                    P.op("dve", "tensor_tensor", qn[0:r], qsb[0:r, :, 0:NOPE], rsq[0:r, :].unsqueeze(2).broadcast_to([r, HH, NOPE]), ALU.mult,
                         r=[Bqsb, Brq], w=[Bqn])
                    x1 = qsb[0:r, :, NOPE:NOPE + 32]; x2 = qsb[0:r, :, NOPE + 32:QK]
                    cb_ = csq[0:r, bi, 0, :].unsqueeze(1).broadcast_to([r, HH, 32])
                    sb_ = csq[0:r, bi, 1, :].unsqueeze(1).broadcast_to([r, HH, 32])
                    P.op("dve", "tensor_tensor", rtq[0:r, 0], x1, cb_, ALU.mult, r=[Bqsb, Bc5], w=[Brtq])
                    P.op("dve", "tensor_tensor", rtq[0:r, 1], x2, sb_, ALU.mult, r=[Bqsb, Bc5], w=[Brtq])
                    P.op("dve", "tensor_tensor", rtq[0:r, 2], x2, cb_, ALU.mult, r=[Bqsb, Bc5], w=[Brtq])
                    P.op("dve", "tensor_tensor", rtq[0:r, 3], x1, sb_, ALU.mult, r=[Bqsb, Bc5], w=[Brtq])
                    P.op("dve", "tensor_tensor", qrf[0:r, :, 0:32], rtq[0:r, 0], rtq[0:r, 1], ALU.subtract, r=[Brtq], w=[Bqrf])
                    P.op("dve", "tensor_tensor", qrf[0:r, :, 32:64], rtq[0:r, 2], rtq[0:r, 3], ALU.add, r=[Brtq], w=[Bqrf])
                    P.op("dve", "tensor_tensor", qr[0:r], qrf[0:r], rsq[0:r, :].unsqueeze(2).broadcast_to([r, HH, ROPE]), ALU.mult,
                         r=[Bqrf, Brq], w=[Bqr])
                    for h0 in range(0, HH, 4):
                        nh = min(4, HH - h0)
                        bank = tb_.next()
                        for i in range(nh):
                            P.op("pe", "transpose", psb[bank][:, i * 128:i * 128 + r], qn[0:r, h0 + i, :], ident[0:r, 0:r], r=[Bqn, B_const], w=[PB[bank]])
                        P.op("act", "activation", out=QTn[:, hb + h0:hb + h0 + nh, e0:e0 + r],
                             in_=psb[bank][:, 0:nh * 128].rearrange("p (h t) -> p h t", h=nh)[:, :, 0:r], func=AF.Copy, r=[PB[bank]], w=[BQT])
                    for h0 in range(0, HH, 8):
                        nh = min(8, HH - h0)
                        bank = tb_.next()
                        for i in range(nh):
                            P.op("pe", "transpose", psb[bank][0:ROPE, i * 128:i * 128 + r], qr[0:r, h0 + i, :], ident[0:r, 0:r], r=[Bqr, B_const], w=[PB[bank]])
                        P.op("dve", "tensor_copy", QTr[0:ROPE, hb + h0:hb + h0 + nh, e0:e0 + r],
                             psb[bank][0:ROPE, 0:nh * 128].rearrange("p (h t) -> p h t", h=nh)[:, :, 0:r], r=[PB[bank]], w=[BQT])
            P.barrier(cells)
            if MIX_STOP == 5:
                P.drop_bufs()
                return
            A.seek(98)
            bT = A.alloc([H, TE], BF16); BbT = P.buf("bT")
            A.check(131)
            KCH = c.KCH
            NKB = KCH // 128
            Kn = [A.alloc([KCH], BF16) for _ in range(2)]; BKn = P.bufs("Kn", 2)
            Vh = [A.alloc([NKB, VD], BF16) for _ in range(2)]; BVh = P.bufs("Vh", 2)
            Kr = A.alloc([S], BF16); BKr = P.buf("Kr")
            PT = [A.alloc([512], BF16) for _ in range(4)]; BPT = P.bufs("PT", 4)
            rc = A.alloc([512], F32); Brc = P.buf("rc")
            P.dma("sp", Kr[0:ROPE, :], KRT[s], r=[B_kv[s]], w=[BKr])
            NQT = len(tilesE)
            assert NQT <= 3
            ci = 0
            pti = 0
            sbk = Rot([0, 1])
            for h in range(H):
                for k0 in range(0, S, KCH):
                    Kn_, BKn_ = Kn[ci % 2], BKn[ci % 2]
                    Vh_, BVh_ = Vh[ci % 2], BVh[ci % 2]
                    ci += 1
                    P.dma("sp", Kn_, KT[s, h, :, k0:k0 + KCH], r=[B_kv[s]], w=[BKn_])
                    P.dma("sp", Vh_, VV[s, k0:k0 + KCH, h * VD:(h + 1) * VD].rearrange("(b p) d -> p b d", p=128), r=[B_kv[s]], w=[BVh_])
                    for kb in range(NKB):
                        kg = (k0 // 128) + kb
                        first = (kg == 0)
                        last = (kg == S // 128 - 1)
                        for qi, (q0, n) in enumerate(tilesE):
                            sbank = sbk.next()
                            P.op("pe", "matmul", ps[sbank][:, 0:n], Kn_[:, kb * 128:(kb + 1) * 128], QTn[:, h, q0:q0 + n], start=True, stop=False,
                                 r=[BKn_, BQT], w=[PB[sbank]])
                            P.op("pe", "matmul", ps[sbank][:, 0:n], Kr[0:ROPE, kg * 128:(kg + 1) * 128], QTr[0:ROPE, h, q0:q0 + n], start=False, stop=True,
                                 r=[BKr, BQT], w=[PB[sbank]])
                            pt = PT[pti % 4]; Bpt = BPT[pti % 4]; pti += 1
                            P.op("act", "activation", out=pt[:, 0:n], in_=ps[sbank][:, 0:n], func=AF.Exp,
                                 scale=RSK[:, s * (S // 128) + kg, h:h + 1], r=[PB[sbank], B_rsk], w=[Bpt])
                            P.op("pe", "matmul", ps[2 + 2 * qi][:, 0:n], Vh_[:, kb, :], pt[:, 0:n], start=first, stop=last, r=[BVh_, Bpt], w=[PB[2 + 2 * qi]])
                            P.op("pe", "matmul", ps[3 + 2 * qi][:, 0:n], ones_b, pt[:, 0:n], start=first, stop=last, r=[B_const, Bpt], w=[PB[3 + 2 * qi]])
                for qi, (q0, n) in enumerate(tilesE):
                    P.op("dve", "reciprocal", rc[:, 0:n], ps[3 + 2 * qi][:, 0:n], r=[PB[3 + 2 * qi]], w=[Brc])
                    P.op("dve", "tensor_tensor", bT[:, h, q0:q0 + n], ps[2 + 2 * qi][:, 0:n], rc[:, 0:n], ALU.mult, r=[PB[2 + 2 * qi], Brc], w=[BbT])
            P.barrier(cells)
            if MIX_STOP == 6:
                P.drop_bufs()
                return
            A.seek(33)
            mT = A.alloc([KC, TE], BF16); BmT = P.buf("mT")
            A.check(98)
            A.seek(131)
            Wbp = [A.alloc([PC, 256], BF16) for _ in range(2)]; BWbp = P.bufs("Wbp", 2)
            Wbm = [A.alloc([MAC, 256], BF16) for _ in range(2)]; BWbm = P.bufs("Wbm", 2)
            sgp = [A.alloc([TE], BF16) for _ in range(2)]; sgm = [A.alloc([TE], BF16) for _ in range(2)]
            Bsgp = P.bufs("sgp", 2); Bsgm = P.bufs("sgm", 2)
            t1 = [A.alloc([512], F32) for _ in range(2)]; t2 = [A.alloc([512], F32) for _ in range(2)]
            Bt1 = P.bufs("t1", 2); Bt2 = P.bufs("t2", 2)
            pr = Rot([0, 1, 2, 3])
            ti_ = 0
            for j in range(KC):
                if j % 2 == 0:
                    k2 = (j // 2) % 2
                    ncol = min(256, D - j * 128)
                    wdma(Wbp[k2][:, :, 0:ncol], w_bp, j * 128, ncol, BWbp[k2])
                    wdma(Wbm[k2][:, :, 0:ncol], w_bm, j * 128, ncol, BWbm[k2])
                P.dma("sp", sgp[j % 2], GATE[s, 0, j], r=[B_gate[s]], w=[Bsgp[j % 2]])
                P.dma("sp", sgm[j % 2], GATE[s, 1, j], r=[B_gate[s]], w=[Bsgm[j % 2]])
                jc = (j % 2) * 128
                for (q0, n) in tilesE:
                    pi_ = pr.next()
                    b1, b2 = 2 * pi_, 2 * pi_ + 1
                    for kc in range(PC):
                        P.op("pe", "matmul", ps[b1][:, 0:n], Wbp[k2][:, kc, jc:jc + 128], aT[:, kc, q0:q0 + n], start=(kc == 0), stop=(kc == PC - 1),
                             r=[BWbp[k2], BaT], w=[PB[b1]])
                    for kc in range(MAC):
                        P.op("pe", "matmul", ps[b2][:, 0:n], Wbm[k2][:, kc, jc:jc + 128], bT[:, kc, q0:q0 + n], start=(kc == 0), stop=(kc == MAC - 1),
                             r=[BWbm[k2], BbT], w=[PB[b2]])
                    a1 = t1[ti_ % 2]; a2 = t2[ti_ % 2]; Ba1 = Bt1[ti_ % 2]; Ba2 = Bt2[ti_ % 2]; ti_ += 1
                    P.op("dve", "tensor_tensor", a1[:, 0:n], ps[b1][:, 0:n], sgp[j % 2][:, q0:q0 + n], ALU.mult, r=[PB[b1], Bsgp[j % 2]], w=[Ba1])
                    P.op("dve", "tensor_tensor", a2[:, 0:n], ps[b2][:, 0:n], sgm[j % 2][:, q0:q0 + n], ALU.mult, r=[PB[b2], Bsgm[j % 2]], w=[Ba2])
                    P.op("pool", "tensor_tensor", mT[:, j, q0:q0 + n], a1[:, 0:n], a2[:, 0:n], ALU.add, r=[Ba1, Ba2], w=[BmT])
            P.barrier(cells)
            if MIX_STOP == 7:
                P.drop_bufs()
                return
            A.seek(98)
            Wo = [A.alloc([KC, 512], BF16) for _ in range(2)]; BWo = P.bufs("Wo", 2)
            xs = [A.alloc([512], F32) for _ in range(3)]; Bxs = P.bufs("xs", 3)
            ost = [A.alloc([512], F32) for _ in range(3)]; Bost = P.bufs("ost", 3)
            banks = Rot(list(range(8)))
            xi = 0
            for n_ in range(D // 512):
                W = Wo[n_ % 2]; BW_ = BWo[n_ % 2]
                wdma(W, w_o, n_ * 512, 512, BW_)
                for (e0, r) in blocksE:
                    bank = banks.next()
                    x_ = xs[xi % 3]; Bx_ = Bxs[xi % 3]; o_ = ost[xi % 3]; Bo_ = Bost[xi % 3]; xi += 1
                    P.dma("sp", x_[0:r, :], xe[s, 8 + e0:8 + e0 + r, n_ * 512:(n_ + 1) * 512], w=[Bx_])
                    for kc in range(KC):
                        P.op("pe", "matmul", ps[bank][0:r, :], mT[:, kc, e0:e0 + r], W[:, kc, :], start=(kc == 0), stop=(kc == KC - 1),
                             r=[BmT, BW_], w=[PB[bank]])
                    P.op("dve", "tensor_tensor", o_[0:r, :], ps[bank][0:r, :], x_[0:r, :], ALU.add, r=[PB[bank], Bx_], w=[Bo_])
                    lo = (128 - TE % 128) if (e0 % 128 != 0) else 0
                    P.dma("sp", X1[s, e0 + lo:e0 + r, n_ * 512:(n_ + 1) * 512], o_[lo:r, :], r=[Bo_], wa=[B_x1[s]])
            P.barrier(cells)
            P.drop_bufs()

        def ffn(s, a0):
            TF, FC = c.TF, c.FC
            HF = TF // 2
            N = HF + 2
            A.seek(0)
            gT = A.alloc([FC, TF], BF16); BgT = P.buf("gT")
            mark0 = A.off
            h2T = A.alloc([KC, TF + 2], BF16); Bh2 = P.buf("h2T")
            mark = A.off / 1024
            tmps = alloc_norm_tmp()
            for bi, (i0, r) in enumerate(blocks_of(TF + 2)):
                norm_transpose(X1[s, a0 + i0:a0 + i0 + r, :], r, gffn, h2T, i0, tmps[bi % 2], Bh2, rsrc=[B_x1[s]],
                               mask_ap=valid[a0 + i0:a0 + i0 + r, :])
            P.barrier(cells)
            if FFN_STOP == 1:
                P.drop_bufs()
                return
            A.seek(mark)
            Wg = [A.alloc([KC, 256], BF16) for _ in range(2)]; Wv = [A.alloc([KC, 256], BF16) for _ in range(2)]
            BWg = P.bufs("Wug", 2); BWv = P.bufs("Wuv", 2)
            tg = [A.alloc([HF], F32) for _ in range(2)]; tv = [A.alloc([HF], F32) for _ in range(2)]; sg = [A.alloc([HF], F32) for _ in range(2)]
            Btg = P.bufs("tg", 2); Btv = P.bufs("tv", 2); Bsg = P.bufs("sgl", 2)
            pr = Rot([0, 1, 2, 3])
            ei = 0
            for f in range(FC):
                if f % 2 == 0:
                    k2 = (f // 2) % 2
                    ncol = min(256, c.DFF - f * 128)
                    wdma(Wg[k2][:, :, 0:ncol], w_up, f * 128, ncol, BWg[k2])
                    wdma(Wv[k2][:, :, 0:ncol], w_up, c.DFF + f * 128, ncol, BWv[k2])
                fc0 = (f % 2) * 128
                for half in range(2):
                    c0 = half * HF
                    pi_ = pr.next()
                    bg, bv = 2 * pi_, 2 * pi_ + 1
                    for kc in range(KC):
                        P.op("pe", "matmul", ps[bg][:, 0:N], Wg[k2][:, kc, fc0:fc0 + 128], h2T[:, kc, c0:c0 + N], start=(kc == 0), stop=(kc == KC - 1),
                             r=[BWg[k2], Bh2], w=[PB[bg]])
                    for kc in range(KC):
                        P.op("pe", "matmul", ps[bv][:, 0:N], Wv[k2][:, kc, fc0:fc0 + 128], h2T[:, kc, c0:c0 + N], start=(kc == 0), stop=(kc == KC - 1),
                             r=[BWv[k2], Bh2], w=[PB[bv]])
                    tg_, tv_, sg_ = tg[ei % 2], tv[ei % 2], sg[ei % 2]
                    Btg_, Btv_, Bsg_ = Btg[ei % 2], Btv[ei % 2], Bsg[ei % 2]
                    ei += 1
                    for (bank, t_, Bt_, ch) in ((bg, tg_, Btg_, f), (bv, tv_, Btv_, FC + f)):
                        P.op("act", "activation", out=t_, in_=ps[bank][:, 1:1 + HF], func=AF.Identity, bias=cb[:, ch:ch + 1], scale=cw[:, 1, ch:ch + 1],
                             r=[PB[bank], B_const], w=[Bt_])
                        P.op("dve", "scalar_tensor_tensor", t_, ps[bank][:, 0:HF], cw[:, 0, ch:ch + 1], t_, ALU.mult, ALU.add,
                             r=[PB[bank], B_const, Bt_], w=[Bt_])
                        P.op("dve", "scalar_tensor_tensor", t_, ps[bank][:, 2:2 + HF], cw[:, 2, ch:ch + 1], t_, ALU.mult, ALU.add,
                             r=[PB[bank], B_const, Bt_], w=[Bt_])
                    P.op("act", "activation", out=sg_, in_=tg_, func=AF.Silu, r=[Btg_], w=[Bsg_])
                    P.op("dve", "tensor_tensor", gT[:, f, c0:c0 + HF], sg_, tv_, ALU.mult, r=[Bsg_, Btv_], w=[BgT])
            P.barrier(cells)
            if FFN_STOP == 2:
                P.drop_bufs()
                return
            A.off = mark0
            KG = 22
            groups = tiles_of(FC, KG)
            Wd = [A.alloc([KG, 512], BF16) for _ in range(2)]; BWd = P.bufs("Wd", 2)
            xs = [A.alloc([512], F32) for _ in range(4)]; Bxs = P.bufs("xs2", 4)
            ost = [A.alloc([512], F32) for _ in range(4)]; Bost = P.bufs("ost2", 4)
            NBk = TF // 128
            wi = 0
            xi = 0
            for n_ in range(D // 512):
                for (f0, nf) in groups:
                    W = Wd[wi % 2]; BW_ = BWd[wi % 2]; wi += 1
                    wdma(W[:, 0:nf, :], w_down, n_ * 512, 512, BW_, row0=f0 * 128, nrows=nf * 128)
                    for b in range(NBk):
                        bank = b + NBk * (n_ % (8 // NBk))
                        for fi in range(nf):
                            f = f0 + fi
                            P.op("pe", "matmul", ps[bank][:, :], gT[:, f, b * 128:(b + 1) * 128], W[:, fi, :], start=(f == 0), stop=(f == FC - 1),
                                 r=[BgT, BW_], w=[PB[bank]])
                for b in range(NBk):
                    bank = b + NBk * (n_ % (8 // NBk))
                    x_ = xs[xi % 4]; Bx_ = Bxs[xi % 4]; o_ = ost[xi % 4]; Bo_ = Bost[xi % 4]; xi += 1
                    if FFN_STOP == 3:
                        continue
                    P.dma("sp", x_, X1[s, a0 + 1 + b * 128:a0 + 1 + (b + 1) * 128, n_ * 512:(n_ + 1) * 512], r=[B_x1[s]], w=[Bx_])
                    P.op("dve", "tensor_tensor", o_, ps[bank][:, :], x_, ALU.add, r=[PB[bank], Bx_], w=[Bo_])
                    if FFN_STOP == 4:
                        continue
                    ydst = X1 if FFN_STOP == 5 else y
                    P.dma("sp", ydst[s, a0 + b * 128:a0 + (b + 1) * 128, n_ * 512:(n_ + 1) * 512], o_, r=[Bo_], wa=[B_y])
            P.barrier(cells)
            P.drop_bufs()

        if "p1" in stages:
            phase1()
            P.barrier(cells)
            P.drop_bufs()
        for s in range(NSEQ):
            if "mix" in stages:
                mixer(s)
            if "ffn" in stages:
                for a0 in range(0, TQ, c.TF):
                    ffn(s, a0)
        last = P._add("sp", "dma_start", (), dict(out=cells["sp_dst"], in_=cells["sp_src"]), [],
                      P.phase_bufs + B_kv + B_gate + B_x1 + [B_y, B_const, B_rsk, P.cell_buf], True)
        P.assign()
        n_ops = {e: len(v) for e, v in P.ops.items()}
        print("[build] ops per engine:", n_ops, "sems left", len(P.free_sems))

        @block.tensor
        def _(e):
            P.emit("pe", e)

        @block.scalar
        def _(e):
            P.emit("act", e)

        @block.vector
        def _(e):
            P.emit("dve", e)

        @block.gpsimd
        def _(e):
            P.emit("pool", e)

        @block.sync
        def _(e):
            P.emit("sp", e)
            e.wait_ge(last.sem, last.val)
    return nc


def rope_tables_np(n_pos):
    inv = (1.0 / (np.float32(10000.0) ** (np.arange(0, ROPE, 2, dtype=np.float32) / np.float32(ROPE)))).astype(np.float32)
    ang = (np.arange(n_pos, dtype=np.float32)[:, None] * inv[None, :]).astype(np.float32)
    return np.cos(ang).astype(np.float32), np.sin(ang).astype(np.float32)


def host_inputs(cfg, inp):
    c = cfg
    D, S, H = c.D, c.S, c.H
    f32 = np.float32
    xs = np.concatenate([np.asarray(inp["x_prompt"], f32), np.asarray(inp["x_sample"], f32)], 0)
    xall = np.ascontiguousarray(xs.reshape(NSEQ * S, D))

    def col(v, n):
        return np.ascontiguousarray(np.asarray(v, f32).reshape(n, 128).T)

    cosk, sink = rope_tables_np(S)
    qg = np.asarray(inp["q_norm_gain"], f32).reshape(QK)
    kg = np.asarray(inp["k_norm_gain"], f32).reshape(QK)
    cw = np.asarray(inp["conv_w"], f32).reshape(3, 2 * c.DFF)
    common = {
        "xall": xall, "cosk": cosk, "sink": sink,
        "g_mix": col(inp["norm_mix_gain"], c.KC), "g_ffn": col(inp["norm_ffn_gain"], c.KC),
        "g_qa": col(inp["q_a_norm_gain"], c.QC), "g_kva": col(inp["kv_a_norm_gain"], c.KVC),
        "g_q_rep": np.ascontiguousarray(np.broadcast_to(np.concatenate([qg, qg])[None, :], (128, 2 * QK))),
        "g_kr_rep": np.ascontiguousarray(np.broadcast_to(kg[None, NOPE:], (128, ROPE))),
        "g_kn": np.ascontiguousarray(kg[:NOPE].reshape(128, 1)),
        "pscale": col(inp["pool_scale"], c.PC),
        "convw": np.ascontiguousarray(cw.reshape(3, 2 * c.FC, 128).transpose(2, 0, 1)),
        "convb": col(inp["conv_b"], 2 * c.FC),
        "ident": np.eye(128, dtype=f32),
        "w_in": np.asarray(inp["w_in"], f32).reshape(D, c.IN_COLS),
        "pool_w": np.asarray(inp["pool_w"], f32).reshape(c.PW, c.PG),
        "w_uq": np.asarray(inp["w_uq"], f32).reshape(c.QL, H * QK),
        "w_ukv": np.asarray(inp["w_ukv"], f32).reshape(c.KVL, H * 256),
        "w_bp": np.asarray(inp["w_branch_pool"], f32).reshape(c.PW, D),
        "w_bm": np.asarray(inp["w_branch_mla"], f32).reshape(c.MA, D),
        "w_o": np.asarray(inp["w_o"], f32).reshape(D, D),
        "w_up": np.asarray(inp["w_up"], f32).reshape(D, 2 * c.DFF),
        "w_down": np.asarray(inp["w_down"], f32).reshape(c.DFF, D),
    }
    maps = []
    for core in range(c.NCORE):
        start = core * c.TQ
        lo = start - HALO
        xe_ = np.zeros((NSEQ, c.TU, D), f32)
        a, b = max(lo, 0), min(lo + c.TU, S)
        xe_[:, a - lo:b - lo, :] = xs[:, a:b, :]
        pos = np.arange(c.TE) + start - 1
        ok = (pos >= 0) & (pos < S)
        pc = np.clip(pos, 0, S - 1)
        inv = np.zeros((4, c.TE), f32)
        for gi, w in enumerate(WINDOWS):
            lo_w = np.clip(pos - w // 2, 0, S)
            hi_w = np.clip(pos + (w - w // 2), 0, S)
            cnt = np.maximum(hi_w - lo_w, 1)
            inv[gi] = (1.0 / cnt.astype(f32)).astype(f32)
        m = dict(common)
        m.update({
            "xe": xe_, "cosq": np.ascontiguousarray(cosk[pc]), "sinq": np.ascontiguousarray(sink[pc]),
            "invcnt": np.ascontiguousarray(np.broadcast_to(inv[None], (128, 4, c.TE))),
            "valid": ok.astype(f32).reshape(c.TE, 1),
        })
        maps.append(m)
    return maps


_FULL = None


def run(cfg, inp, debug=False):
    nc = build(cfg, debug)
    maps = host_inputs(cfg, inp)
    res = run_bass_kernel_spmd(nc, maps, core_ids=list(range(cfg.NCORE)))
    return res


def kernel(**inputs):
    cfg = Cfg()
    res = run(cfg, inputs)
    ys = np.stack([r["yout"] for r in res.results], 0)
    full = ys.transpose(1, 0, 2, 3).reshape(NSEQ, cfg.S, cfg.D)
    return (np.ascontiguousarray(full[:2]), np.ascontiguousarray(full[2:3]))
```

```python
import math
from contextlib import ExitStack

import numpy as np
import concourse.bass as bass
import concourse.mybir as mybir
from concourse.bass_utils import run_bass_kernel_spmd

F32 = mybir.dt.float32
BF16 = mybir.dt.bfloat16
AF = mybir.ActivationFunctionType
ALU = mybir.AluOpType
EPS = 1e-6
NSEQ = 3
ROPE = 64
NOPE = 128
VD = 128
QK = 192
WINDOWS = (2, 4, 8, 16)
HALO = 9


class Cfg:
    def __init__(self, D=4096, S=8192, H=16, DFF=11008, NCORE=8):
        self.D, self.S, self.H, self.DFF, self.NCORE = D, S, H, DFF, NCORE
        self.KC = D // 128
        self.PW = D // 2
        self.PG = self.PW // 4
        self.PC = self.PW // 128
        self.PGC = self.PG // 128
        self.QL = D // 4
        self.QC = self.QL // 128
        self.KVL = D // 8
        self.KVC = self.KVL // 128
        self.FC = DFF // 128
        self.TQ = S // NCORE
        self.TU = self.TQ + 2 * HALO
        self.TE = self.TQ + 2
        self.OFF_CQ = self.PW
        self.OFF_CKV = self.OFF_CQ + self.QL
        self.OFF_KR = self.OFF_CKV + self.KVL
        self.OFF_GP = self.OFF_KR + ROPE
        self.OFF_GM = self.OFF_GP + D
        self.IN_COLS = self.OFF_GM + D
        self.SBT = min(256, S)
        self.TF = min(512, self.TQ)
        self.KCH = min(4096, S)
        self.MA = H * VD
        self.MAC = self.MA // 128


class Buf:
    __slots__ = ("name", "w", "r", "dw", "dr")

    def __init__(self, name):
        self.name = name
        self.w = {}
        self.r = {}
        self.dw = []
        self.dr = []


class Op:
    __slots__ = ("eng", "meth", "args", "kw", "dma", "deps", "inc", "sem", "val")

    def __init__(self, eng, meth, args, kw, dma):
        self.eng, self.meth, self.args, self.kw, self.dma = eng, meth, args, kw, dma
        self.deps = []
        self.inc = False
        self.sem = None
        self.val = 0


SEM_LIMIT = 30000
import os
P1_STOP = int(os.environ.get("P1_STOP", "0"))
MIX_STOP = int(os.environ.get("MIX_STOP", "0"))
FFN_STOP = int(os.environ.get("FFN_STOP", "0"))
RING = 12


class Prog:
    ENGS = ("pe", "act", "dve", "pool", "sp")

    def __init__(self, sems):
        self.ops = {e: [] for e in self.ENGS}
        self.order = []
        self.free_sems = list(sems)
        self.phase_bufs = []
        self.cell_buf = Buf("cell")

    def buf(self, name, persistent=False):
        b = Buf(name)
        if not persistent:
            self.phase_bufs.append(b)
        return b

    def bufs(self, name, n, persistent=False):
        return [self.buf(f"{name}{i}", persistent) for i in range(n)]

    def _add(self, eng, meth, args, kw, r, w, dma):
        op = Op(eng, meth, args, kw, dma)
        deps = op.deps
        for b in r:
            for e, p in b.w.items():
                if e == eng and not dma and eng == "pe":
                    continue
                deps.append(p)
            deps.extend(b.dw)
        for b in w:
            for e, p in b.r.items():
                if e == eng and not dma and eng == "pe":
                    continue
                deps.append(p)
            for e, p in b.w.items():
                if e == eng and not dma and eng == "pe":
                    continue
                deps.append(p)
            deps.extend(b.dr)
            deps.extend(b.dw)
        for b in r:
            if dma:
                b.dr.append(op)
            else:
                b.r[eng] = op
        for b in w:
            if dma:
                b.w = {}
                b.r = {}
                b.dr = []
                b.dw = [op]
            else:
                b.w = {eng: op}
                b.r = {}
                b.dr = []
                b.dw = []
        for p in deps:
            p.inc = True
        if dma:
            op.inc = True
        self.ops[eng].append(op)
        self.order.append(op)
        return op

    def op(self, eng, meth, *args, r=(), w=(), **kw):
        return self._add(eng, meth, args, kw, r, w, False)

    def dma(self, q, out, in_, r=(), w=(), wa=(), **kw):
        kw = dict(kw)
        kw["out"] = out
        kw["in_"] = in_
        op = self._add(q, "dma_start", (), kw, r, w, True)
        for b in wa:
            b.dw.append(op)
        return op

    def barrier(self, cells):
        bl = list(self.phase_bufs) + [self.cell_buf]
        self._add("act", "memzero", (cells["act"],), {}, [], bl, False)
        for eng in ("dve", "pool"):
            self._add(eng, "memset", (cells[eng], 0.0), {}, [], bl, False)
        self._add("sp", "dma_start", (), dict(out=cells["sp_dst"], in_=cells["sp_src"]), [], bl, True)

    def drop_bufs(self):
        self.phase_bufs = []

    def assign(self):
        cur = {}
        ring = {q: [dict(sem=None, val=0, last=None) for _ in range(RING)] for q in ("sp", "pool")}
        rcnt = {"sp": 0, "pool": 0}
        for op in self.order:
            if not op.inc:
                continue
            if op.dma:
                slots = ring[op.eng]
                sl = slots[rcnt[op.eng] % len(slots)]
                rcnt[op.eng] += 1
                if sl["last"] is not None:
                    op.deps.append(sl["last"])
                if sl["sem"] is None or sl["val"] + 16 > SEM_LIMIT:
                    sl["sem"] = self.free_sems.pop()
                    sl["val"] = 0
                sl["val"] += 16
                sl["last"] = op
                op.sem, op.val = sl["sem"], sl["val"]
            else:
                cc = cur.get(op.eng)
                if cc is None or cc[1] + 1 > SEM_LIMIT:
                    cc = [self.free_sems.pop(), 0]
                    cur[op.eng] = cc
                cc[1] += 1
                op.sem, op.val = cc[0], cc[1]

    def emit(self, eng_name, eng):
        known = {}
        for op in self.ops[eng_name]:
            need = {}
            for p in op.deps:
                k = id(p.sem)
                if known.get(k, 0) >= p.val:
                    continue
                if k not in need or need[k][1] < p.val:
                    need[k] = (p.sem, p.val)
            for k, (sem, val) in need.items():
                eng.wait_ge(sem, val)
                known[k] = val
            ins = getattr(eng, op.meth)(*op.args, **op.kw)
            if op.inc:
                ins.then_inc(op.sem, 16 if op.dma else 1)


class Arena:
    def __init__(self, ap_f32, nbytes):
        self.ap = ap_f32
        self.nbytes = nbytes
        self.off = 0

    def seek(self, kb):
        self.off = int(kb * 1024)

    def check(self, kb):
        assert self.off <= kb * 1024, f"arena region overflow: {self.off} > {kb}KB"

    def alloc(self, shape, dtype, parts=128):
        esz = 4 if dtype == F32 else 2
        n = int(np.prod(shape))
        nb = (n * esz + 3) // 4 * 4
        assert self.off + nb <= self.nbytes, f"arena overflow: {self.off}+{nb} > {self.nbytes}"
        a = self.ap[0:parts, self.off // 4:(self.off + nb) // 4]
        self.off += nb
        if dtype != F32:
            a = a.bitcast(dtype)
            if a.shape[1] != n:
                a = a[:, 0:n]
        if len(shape) == 2:
            a = a.rearrange("p (a b) -> p a b", a=shape[0])
        elif len(shape) == 3:
            a = a.rearrange("p (a b c) -> p a b c", a=shape[0], b=shape[1])
        return a


def tiles_of(n, t):
    return [(i, min(t, n - i)) for i in range(0, n, t)]


def blocks_of(n):
    bl = [(i * 128, 128) for i in range(n // 128)]
    if n % 128:
        bl.append((n - 128, 128))
    return bl


class Rot:
    def __init__(self, items):
        self.items = list(items)
        self.i = 0

    def next(self):
        v = self.items[self.i % len(self.items)]
        self.i += 1
        return v


def build(cfg, debug=False, stages=("p1", "mix", "ffn")):
    c = cfg
    D, S, H, KC = c.D, c.S, c.H, c.KC
    TE, TU, TQ = c.TE, c.TU, c.TQ
    nc = bass.Bass("TRN2", target_bir_lowering=False)

    def din(name, shape, dt=F32):
        return nc.dram_tensor(name, list(shape), dt, kind="ExternalInput").ap()

    def dscr(name, shape, dt):
        return nc.dram_tensor(name, list(shape), dt, kind="ExternalOutput" if debug else "Internal").ap()

    xall = din("xall", [NSEQ * S, D])
    xe = din("xe", [NSEQ, TU, D])
    cosk = din("cosk", [S, 32]); sink = din("sink", [S, 32])
    cosq = din("cosq", [TE, 32]); sinq = din("sinq", [TE, 32])
    invcnt = din("invcnt", [128, 4, TE])
    valid = din("valid", [TE, 1])
    g_mix = din("g_mix", [128, KC]); g_ffn = din("g_ffn", [128, KC])
    g_qa = din("g_qa", [128, c.QC]); g_kva = din("g_kva", [128, c.KVC])
    g_q_rep = din("g_q_rep", [128, 2 * QK]); g_kr_rep = din("g_kr_rep", [128, ROPE]); g_kn = din("g_kn", [128, 1])
    pscale = din("pscale", [128, c.PC])
    convw = din("convw", [128, 3, 2 * c.FC]); convb = din("convb", [128, 2 * c.FC])
    ident_in = din("ident", [128, 128])
    w_in = din("w_in", [D, c.IN_COLS])
    pool_w = din("pool_w", [c.PW, c.PG])
    w_uq = din("w_uq", [c.QL, H * QK])
    w_ukv = din("w_ukv", [c.KVL, H * 256])
    w_bp = din("w_bp", [c.PW, D]); w_bm = din("w_bm", [c.MA, D])
    w_o = din("w_o", [D, D])
    w_up = din("w_up", [D, 2 * c.DFF]); w_down = din("w_down", [c.DFF, D])
    y = nc.dram_tensor("yout", [NSEQ, TQ, D], F32, kind="ExternalOutput").ap()
    KT = dscr("KT", [NSEQ, H, 128, S], BF16)
    KRT = dscr("KRT", [NSEQ, ROPE, S], BF16)
    VV = dscr("VV", [NSEQ, S, H * VD], BF16)
    GATE = dscr("GATE", [NSEQ, 2, KC, 128, TE], BF16)
    X1 = dscr("X1", [NSEQ, TE, D], F32)
    dummy_d = nc.dram_tensor("dummy_d", [1, 16], F32, kind="Internal").ap()
    NPAN = (c.FC + 1) // 2
    KG = 22
    FGROUPS = tiles_of(c.FC, KG)
    WUP2 = nc.dram_tensor("WUP2", [NPAN, 2, 128, KC * 256], BF16, kind="Internal").ap()
    WD2 = nc.dram_tensor("WD2", [D // 512, len(FGROUPS), 128, KG * 512], BF16, kind="Internal").ap()

    NBLK_ALL = NSEQ * S // 128
    NBE = (TE + 127) // 128
    ARENA_BYTES = 190 * 1024
    pers_sizes = dict(ones_f=128, ident=64, ones_b=64, rsk=NBLK_ALL * H, cells=32, gmix=KC, gffn=KC, gqa=c.QC,
                      gkva=c.KVC, gkr=ROPE, gkn=1, pscale=c.PC, convw=6 * c.FC, convb=2 * c.FC)
    PERS_F32 = sum(pers_sizes.values()) + 16

    with ExitStack() as es:
        pers = es.enter_context(nc.sbuf_tensor("pers", [128, PERS_F32], F32))
        arena_t = es.enter_context(nc.sbuf_tensor("arena", [128, ARENA_BYTES // 4], F32))
        psum = [es.enter_context(nc.psum_tensor(f"ps{i}", [128, 512], F32)) for i in range(8)]
        sems = [es.enter_context(nc.semaphore(f"sm{i}")) for i in range(96)]
        block = es.enter_context(nc.Block())

        P = Prog(sems)
        A = Arena(arena_t, ARENA_BYTES)
        pv = {}
        o = 0
        for k, n in pers_sizes.items():
            pv[k] = pers[:, o:o + n]
            o += n
        ones_f = pv["ones_f"]
        ident = pv["ident"].bitcast(BF16)
        ones_b = pv["ones_b"].bitcast(BF16)
        RSK = pv["rsk"].rearrange("p (b h) -> p b h", h=H)
        cl = pv["cells"]
        cells = {"act": cl[:, 0:1], "dve": cl[:, 1:2], "pool": cl[:, 2:3], "sp_dst": cl[0:1, 8:24], "sp_src": ident_in[0:1, 0:16]}
        gmix, gffn, gqa, gkva = pv["gmix"], pv["gffn"], pv["gqa"], pv["gkva"]
        gkr, gkn, psc = pv["gkr"], pv["gkn"], pv["pscale"]
        cw = pv["convw"].rearrange("p (t j) -> p t j", t=3)
        cb = pv["convb"]
        ps = [p[:] for p in psum]
        psb = [p[:].bitcast(BF16) for p in psum]
        PB = P.bufs("psum", 8, persistent=True)
        B_const = P.buf("const", persistent=True)
        B_rsk = P.buf("rsk", persistent=True)
        B_y = P.buf("y", persistent=True)
        B_wpre = P.buf("wpre", persistent=True)
        B_kv = P.bufs("kvscr", NSEQ, persistent=True)
        B_gate = P.bufs("gate", NSEQ, persistent=True)
        B_x1 = P.bufs("x1", NSEQ, persistent=True)

        A.seek(189)
        t_id = A.alloc([128], F32)
        bt = P.buf("t_id")
        P.dma("sp", t_id, ident_in, w=[bt])
        P.op("dve", "tensor_copy", ident, t_id, r=[bt], w=[B_const])
        P.op("dve", "memset", ones_f, 1.0, w=[B_const])
        P.op("dve", "memset", ones_b, 1.0, w=[B_const])
        P.op("dve", "memset", cl, 0.0, w=[B_const, P.cell_buf])
        for dst, src in ((gmix, g_mix), (gffn, g_ffn), (gqa, g_qa), (gkva, g_kva), (gkr, g_kr_rep),
                         (gkn, g_kn), (psc, pscale), (cw, convw), (cb, convb)):
            P.dma("sp", dst, src, w=[B_const])
        P.barrier(cells)
        P.drop_bufs()

        def alloc_norm_tmp():
            tm = []
            for i in range(2):
                xn_ = A.alloc([D], BF16); Bn_ = P.buf("xn")
                tm.append(dict(xb=A.alloc([D], F32), junk=xn_, xn=xn_, st=A.alloc([4], F32),
                               Bx=P.buf("xb"), Bj=Bn_, Bn=Bn_, Bs=P.buf("st"), Bm=P.buf("msk")))
            return tm

        def norm_transpose(src, r, gain, hT, col0, t, BhT, rsrc=(), mask_ap=None):
            xb, junk, xn, st = t["xb"], t["junk"], t["xn"], t["st"]
            Bx, Bj, Bn, Bs, Bm = t["Bx"], t["Bj"], t["Bn"], t["Bs"], t["Bm"]
            P.dma("sp", xb[0:r, :], src, r=list(rsrc), w=[Bx])
            if mask_ap is not None:
                P.dma("sp", st[0:r, 3:4], mask_ap, w=[Bm])
            P.op("act", "activation", out=junk[0:r, :], in_=xb[0:r, :], func=AF.Square, accum_out=st[0:r, 0:1], r=[Bx], w=[Bj, Bs])
            P.op("act", "activation", out=st[0:r, 1:2], in_=st[0:r, 0:1], func=AF.Sqrt, bias=EPS, scale=1.0 / D, r=[Bs], w=[Bs])
            P.op("dve", "reciprocal", st[0:r, 2:3], st[0:r, 1:2], r=[Bs], w=[Bs])
            if mask_ap is not None:
                P.op("dve", "tensor_tensor", st[0:r, 2:3], st[0:r, 2:3], st[0:r, 3:4], ALU.mult, r=[Bs, Bm], w=[Bs])
            P.op("dve", "tensor_scalar", xn[0:r, :], xb[0:r, :], st[0:r, 2:3], None, ALU.mult, r=[Bx, Bs], w=[Bn])
            for g4 in range(0, KC, 4):
                bank = (g4 // 4) % 2
                n4 = min(4, KC - g4)
                for i in range(n4):
                    kc = g4 + i
                    P.op("pe", "transpose", psb[bank][:, i * 128:i * 128 + r], xn[0:r, kc * 128:(kc + 1) * 128], ident[0:r, 0:r],
                         r=[Bn, B_const], w=[PB[bank]])
                for i in range(n4):
                    kc = g4 + i
                    if kc % 2 == 0:
                        P.op("act", "activation", out=hT[:, kc, col0:col0 + r], in_=psb[bank][:, i * 128:i * 128 + r], func=AF.Copy,
                             scale=gain[:, kc:kc + 1], r=[PB[bank], B_const], w=[BhT])
                    else:
                        P.op("dve", "tensor_scalar", hT[:, kc, col0:col0 + r], psb[bank][:, i * 128:i * 128 + r], gain[:, kc:kc + 1],
                             None, ALU.mult, r=[PB[bank], B_const], w=[BhT])

        def wdma(dst, src_rows, col0, ncols, B, row0=0, nrows=None):
            nrows = src_rows.shape[0] - row0 if nrows is None else nrows
            src = src_rows[row0:row0 + nrows, col0:col0 + ncols].rearrange("(k p) c -> p k c", p=128)
            P.dma("pool", dst, src, w=[B], max_dma_last_dim=4096)

        def phase1():
            SBT = c.SBT
            NB = SBT // 128
            KVC = c.KVC
            A.seek(0)
            Wckv = A.alloc([KC, c.KVL], BF16); Wkr = A.alloc([KC, ROPE], BF16); Wukv = A.alloc([KVC, H * 256], BF16)
            BW = P.buf("p1w")
            wdma(Wckv, w_in, c.OFF_CKV, c.KVL, BW)
            wdma(Wkr, w_in, c.OFF_KR, ROPE, BW)
            for h0 in range(0, H * 256, 1024):
                n = min(1024, H * 256 - h0)
                wdma(Wukv[:, :, h0:h0 + n], w_ukv, h0, n, BW)
            for pan in range(NPAN):
                ncol = min(256, c.DFF - pan * 256)
                for gv in range(2):
                    dst = WUP2[pan, gv].rearrange("p (k c) -> p k c", k=KC)[:, :, 0:ncol]
                    src = w_up[:, gv * c.DFF + pan * 256:gv * c.DFF + pan * 256 + ncol].rearrange("(k p) c -> p k c", p=128)
                    P.dma("pool", dst, src, wa=[B_wpre], max_dma_last_dim=4096)
            for n_ in range(D // 512):
                for gi, (f0, nf) in enumerate(FGROUPS):
                    dst = WD2[n_, gi].rearrange("p (k c) -> p k c", k=KG)[:, 0:nf, :]
                    src = w_down[f0 * 128:(f0 + nf) * 128, n_ * 512:(n_ + 1) * 512].rearrange("(k p) c -> p k c", p=128)
                    P.dma("pool", dst, src, wa=[B_wpre], max_dma_last_dim=4096)
            tmps = alloc_norm_tmp()
            hT = [A.alloc([KC, SBT], BF16) for _ in range(2)]
            BhT = P.bufs("hT", 2)
            ckvf = A.alloc([KVC, SBT], F32); Bckvf = P.bufs("ckvf", KVC)
            sq = [A.alloc([SBT], F32) for _ in range(2)]; Bsq = P.bufs("sq", 2)
            rstdkv = A.alloc([SBT], F32); Brkv = P.buf("rstdkv")
            ckvn = A.alloc([KVC, SBT], BF16); Bckvn = P.buf("ckvn")
            krs = A.alloc([NB, 8], F32); Bkrs = P.buf("krs")
            krg = A.alloc([ROPE], F32); Bkrg = P.buf("krg")
            cs = [A.alloc([2, 32], F32) for _ in range(2)]; Bcs = P.bufs("cs", 2)
            rt = A.alloc([4, 32], F32); Brt = P.buf("rt")
            krb = A.alloc([ROPE], BF16); Bkrb = P.buf("krb")
            ssqk = A.alloc([NB, H], F32); Bssqk = P.buf("ssqk")
            Vst = A.alloc([NB, H * VD], BF16); BVst = P.buf("Vst")
            KTst = A.alloc([H, SBT], BF16); BKT = P.buf("KTst")
            KRst = A.alloc([SBT], BF16); BKR = P.buf("KRst")
            junk2s = [A.alloc([QK], BF16) for _ in range(4)]; Bj2s = P.bufs("junk2", 4); jr = Rot([0, 1, 2, 3])
            nsb = S // SBT
            Bser = P.buf("ser")
            blk_i = 0
            for s in range(NSEQ):
                Bscr = B_kv[s]
                for sb in range(nsb):
                    it = s * nsb + sb
                    hTc = hT[it % 2]; BhTc = BhT[it % 2]
                    t0 = sb * SBT
                    for b in range(NB):
                        t = tmps[blk_i % 2]; blk_i += 1
                        row0 = s * S + t0 + b * 128
                        norm_transpose(xall[row0:row0 + 128, :], 128, gmix, hTc, b * 128, t, BhTc)
                    if P1_STOP == 1:
                        continue
                    for j in range(KVC):
                        bank = 2 + (j % 2)
                        for kc in range(KC):
                            P.op("pe", "matmul", ps[bank][:, 0:SBT], Wckv[:, kc, j * 128:(j + 1) * 128], hTc[:, kc, :],
                                 start=(kc == 0), stop=(kc == KC - 1), r=[BW, BhTc], w=[PB[bank]])
                        P.op("act", "activation", out=ckvf[:, j, :], in_=ps[bank][:, 0:SBT], func=AF.Copy, r=[PB[bank]], w=[Bckvf[j]])
                        P.op("dve", "tensor_tensor", sq[j % 2], ps[bank][:, 0:SBT], ckvf[:, j, :], ALU.mult,
                             r=[PB[bank], Bckvf[j]], w=[Bsq[j % 2]])
                        P.op("pe", "matmul", ps[4][:, 0:SBT], ones_f, sq[j % 2], start=(j == 0), stop=(j == KVC - 1),
                             r=[Bsq[j % 2], B_const], w=[PB[4]])
                    P.op("act", "activation", out=rstdkv, in_=ps[4][:, 0:SBT], func=AF.Sqrt, bias=EPS, scale=1.0 / c.KVL,
                         r=[PB[4]], w=[Brkv])
                    P.op("dve", "reciprocal", rstdkv, rstdkv, r=[Brkv], w=[Brkv])
                    for j in range(KVC):
                        P.op("dve", "scalar_tensor_tensor", ckvn[:, j, :], ckvf[:, j, :], gkva[:, j:j + 1], rstdkv, ALU.mult, ALU.mult,
                             r=[Bckvf[j], Brkv, B_const], w=[Bckvn])
                    if P1_STOP == 2:
                        continue
                    for b in range(NB):
                        for kc in range(KC):
                            P.op("pe", "matmul", ps[5][:, b * ROPE:(b + 1) * ROPE], hTc[:, kc, b * 128:(b + 1) * 128], Wkr[:, kc, :],
                                 start=(kc == 0), stop=(kc == KC - 1), r=[BW, BhTc], w=[PB[5]])
                    for b in range(NB):
                        pk = ps[5][:, b * ROPE:(b + 1) * ROPE]
                        pos0 = t0 + b * 128
                        csb = cs[b % 2]; Bcsb = Bcs[b % 2]
                        P.dma("sp", csb[:, 0, :], cosk[pos0:pos0 + 128, :], w=[Bcsb])
                        P.dma("sp", csb[:, 1, :], sink[pos0:pos0 + 128, :], w=[Bcsb])
                        ji = jr.next()
                        P.op("act", "activation", out=junk2s[ji][:, 0:ROPE], in_=pk, func=AF.Square, accum_out=krs[:, b, 0:1],
                             r=[PB[5]], w=[Bj2s[ji], Bkrs])
                        P.op("dve", "tensor_tensor", krg, pk, gkr, ALU.mult, r=[PB[5], B_const], w=[Bkrg])
                        x1 = krg[:, 0:32]; x2 = krg[:, 32:64]
                        P.op("dve", "tensor_tensor", rt[:, 0, :], x1, csb[:, 0, :], ALU.mult, r=[Bkrg, Bcsb], w=[Brt])
                        P.op("dve", "tensor_tensor", rt[:, 1, :], x2, csb[:, 1, :], ALU.mult, r=[Bkrg, Bcsb], w=[Brt])
                        P.op("dve", "tensor_tensor", rt[:, 2, :], x2, csb[:, 0, :], ALU.mult, r=[Bkrg, Bcsb], w=[Brt])
                        P.op("dve", "tensor_tensor", rt[:, 3, :], x1, csb[:, 1, :], ALU.mult, r=[Bkrg, Bcsb], w=[Brt])
                        P.op("dve", "tensor_tensor", krb[:, 0:32], rt[:, 0, :], rt[:, 1, :], ALU.subtract, r=[Brt], w=[Bkrb])
                        P.op("dve", "tensor_tensor", krb[:, 32:64], rt[:, 2, :], rt[:, 3, :], ALU.add, r=[Brt], w=[Bkrb])
                        P.op("pe", "transpose", psb[6][0:ROPE, b * 128:(b + 1) * 128], krb, ident, r=[Bkrb, B_const], w=[PB[6]])
                    P.op("act", "activation", out=KRst[0:ROPE, :], in_=psb[6][0:ROPE, 0:SBT], func=AF.Copy, r=[PB[6]], w=[BKR])
                    P.dma("sp", KRT[s, :, t0:t0 + SBT], KRst[0:ROPE, :], r=[BKR], wa=[Bscr])
                    if P1_STOP == 3:
                        continue
                    for b in range(NB):
                        for n in range(H // 2):
                            bank = (n % 2)
                            for kc in range(KVC):
                                P.op("pe", "matmul", ps[bank][:, :], ckvn[:, kc, b * 128:(b + 1) * 128], Wukv[:, kc, n * 512:(n + 1) * 512],
                                     start=(kc == 0), stop=(kc == KVC - 1), r=[Bckvn, BW], w=[PB[bank]])
                            pvw = ps[bank][:, :].rearrange("p (h t d) -> p h t d", h=2, t=2)
                            if P1_STOP == 31:
                                continue
                            P.op("dve", "tensor_copy", Vst[:, b, n * 256:(n + 1) * 256].rearrange("p (h d) -> p h d", h=2), pvw[:, :, 1, :],
                                 r=[PB[bank]], w=[BVst, Bser])
                            if P1_STOP == 32:
                                continue
                            for hh in range(2):
                                ji = jr.next()
                                if P1_STOP in (35, 36):
                                    P.op("act", "activation", out=junk2s[ji][:, 0:128], in_=ps[bank][:, hh * 256:hh * 256 + 128], func=AF.Square,
                                         r=[PB[bank]], w=[Bj2s[ji], Bssqk])
                                    continue
                                P.op("act", "activation", out=junk2s[ji][:, 0:128], in_=ps[bank][:, hh * 256:hh * 256 + 128], func=AF.Square,
                                     accum_out=ssqk[:, b, 2 * n + hh:2 * n + hh + 1], r=[PB[bank], Bser], w=[Bj2s[ji], Bssqk])
                    if P1_STOP in (31, 32, 33, 35, 36):
                        continue
                    for b in range(NB):
                        gb = (s * S + t0) // 128 + b
                        P.op("dve", "tensor_scalar", ssqk[:, b, :], ssqk[:, b, :], krs[:, b, 0:1], EPS * QK, ALU.add, ALU.add, r=[Bssqk, Bkrs], w=[Bssqk])
                        P.op("act", "activation", out=RSK[:, gb, :], in_=ssqk[:, b, :], func=AF.Sqrt,
                             r=[Bssqk], w=[B_rsk])
                        P.op("dve", "reciprocal", RSK[:, gb, :], RSK[:, gb, :], r=[B_rsk], w=[B_rsk])
                    if P1_STOP == 34:
                        continue
                    P.dma("sp", VV[s, t0:t0 + SBT, :].rearrange("(b p) f -> p b f", p=128), Vst, r=[BVst], wa=[Bscr])
                    if P1_STOP == 4:
                        continue
                    for h in range(H):
                        bank = 6 + (h % 2)
                        for kc in range(KVC):
                            P.op("pe", "matmul", ps[bank][:, 0:SBT], Wukv[:, kc, h * 256:h * 256 + 128], ckvn[:, kc, :],
                                 start=(kc == 0), stop=(kc == KVC - 1), r=[BW, Bckvn], w=[PB[bank]])
                        if h % 2 == 0:
                            P.op("act", "activation", out=KTst[:, h, :], in_=ps[bank][:, 0:SBT], func=AF.Copy, scale=gkn[:, 0:1],
                                 r=[PB[bank], B_const], w=[BKT])
                        else:
                            P.op("dve", "tensor_scalar", KTst[:, h, :], ps[bank][:, 0:SBT], gkn[:, 0:1], None, ALU.mult,
                                 r=[PB[bank], B_const], w=[BKT])
                    P.dma("sp", KT[s, :, :, t0:t0 + SBT].rearrange("h p t -> p h t"), KTst, r=[BKT], wa=[Bscr])

        tilesE = tiles_of(TE, 512)
        tilesU = tiles_of(TU, 512)
        blocksE = blocks_of(TE)

        def mixer(s):
            PC, PGC, QC, MAC = c.PC, c.PGC, c.QC, c.MAC
            A.seek(0)
            aT = A.alloc([PC, TE], BF16); BaT = P.buf("aT")
            A.check(33)
            A.seek(33)
            hT = A.alloc([KC, TU], BF16); BhT = P.buf("hT")
            A.check(99)
            A.seek(99)
            tmps = alloc_norm_tmp()
            for bi, (i0, r) in enumerate(blocks_of(TU)):
                norm_transpose(xe[s, i0:i0 + r, :], r, gmix, hT, i0, tmps[bi % 2], BhT)
            P.barrier(cells)
            if MIX_STOP == 1:
                P.drop_bufs()
                return
            A.seek(99)
            Wg = [A.alloc([KC, 512], BF16) for _ in range(2)]; BWg = P.bufs("Wg", 2)
            sg = [A.alloc([TE], BF16) for _ in range(2)]; Bsg = P.bufs("sg", 2)
            banks = Rot([0, 1, 2, 3, 4, 5])
            cnt = 0
            for gi, off in enumerate((c.OFF_GP, c.OFF_GM)):
                for pi in range(D // 512):
                    W = Wg[cnt % 2]; BW_ = BWg[cnt % 2]; cnt += 1
                    wdma(W, w_in, off + pi * 512, 512, BW_)
                    for jj in range(4):
                        j = pi * 4 + jj
                        sgt = sg[j % 2]; Bs_ = Bsg[j % 2]
                        for (q0, n) in tilesE:
                            bank = banks.next()
                            for kc in range(KC):
                                P.op("pe", "matmul", ps[bank][:, 0:n], W[:, kc, jj * 128:(jj + 1) * 128], hT[:, kc, 8 + q0:8 + q0 + n],
                                     start=(kc == 0), stop=(kc == KC - 1), r=[BW_, BhT], w=[PB[bank]])
                            P.op("act", "activation", out=sgt[:, q0:q0 + n], in_=ps[bank][:, 0:n], func=AF.Sigmoid, r=[PB[bank]], w=[Bs_])
                        P.dma("sp", GATE[s, gi, j], sgt, r=[Bs_], wa=[B_gate[s]])
            P.barrier(cells)
            if MIX_STOP == 2:
                P.drop_bufs()
                return
            A.seek(99)
            Wp = [A.alloc([KC, 256], BF16) for _ in range(2)]; BWp = P.bufs("Wp", 2)
            usb = [A.alloc([TU], F32) for _ in range(2)]; Bu = P.bufs("usb", 2)
            ta = A.alloc([TU], F32); tb = A.alloc([TU], F32); tc = A.alloc([TU], F32); Bta = P.buf("ta"); Btb = P.buf("tb"); Btc = P.buf("tc")
            pooled = A.alloc([PGC, TE], BF16); Bpl = P.buf("pooled")
            plw = [A.alloc([PGC, c.PG], BF16) for _ in range(2)]; Bplw = P.bufs("plw", 2)
            icn = [A.alloc([TE], F32) for _ in range(2)]; Bicn = P.bufs("icn", 2)
            banks = Rot([0, 1, 2, 3, 4, 5])
            cnt = 0
            for g in range(4):
                w = WINDOWS[g]
                wdma(plw[g % 2], pool_w, 0, c.PG, Bplw[g % 2], row0=g * c.PG, nrows=c.PG)
                P.dma("sp", icn[g % 2], invcnt[:, g, :], w=[Bicn[g % 2]])
                for cc in range(PGC):
                    ch = g * PGC + cc
                    if ch % 2 == 0:
                        W = Wp[cnt % 2]; BW_ = BWp[cnt % 2]; cnt += 1
                        ncol = min(256, c.PW - ch * 128)
                        wdma(W[:, :, 0:ncol], w_in, ch * 128, ncol, BW_)
                    u = usb[ch % 2]; Bu_ = Bu[ch % 2]
                    for (u0, n) in tilesU:
                        bank = banks.next()
                        for kc in range(KC):
                            P.op("pe", "matmul", ps[bank][:, 0:n], W[:, kc, (ch % 2) * 128:(ch % 2) * 128 + 128], hT[:, kc, u0:u0 + n],
                                 start=(kc == 0), stop=(kc == KC - 1), r=[BW_, BhT], w=[PB[bank]])
                        P.op("act", "activation", out=u[:, u0:u0 + n], in_=ps[bank][:, 0:n], func=AF.Copy, r=[PB[bank]], w=[Bu_])
                    if w == 2:
                        P.op("dve", "tensor_tensor", tc[:, 8:8 + TE], u[:, 7:7 + TE], u[:, 8:8 + TE], ALU.add, r=[Bu_], w=[Btc])
                    else:
                        P.op("dve", "tensor_tensor", ta[:, 0:TU - 1], u[:, 0:TU - 1], u[:, 1:TU], ALU.add, r=[Bu_], w=[Bta])
                        if w == 4:
                            P.op("dve", "tensor_tensor", tc[:, 8:8 + TE], ta[:, 6:6 + TE], ta[:, 8:8 + TE], ALU.add, r=[Bta], w=[Btc])
                        else:
                            P.op("dve", "tensor_tensor", tb[:, 0:TU - 3], ta[:, 0:TU - 3], ta[:, 2:TU - 1], ALU.add, r=[Bta], w=[Btb])
                            if w == 8:
                                P.op("dve", "tensor_tensor", tc[:, 8:8 + TE], tb[:, 4:4 + TE], tb[:, 8:8 + TE], ALU.add, r=[Btb], w=[Btc])
                            else:
                                P.op("dve", "tensor_tensor", ta[:, 0:TU - 7], tb[:, 0:TU - 7], tb[:, 4:TU - 3], ALU.add, r=[Btb], w=[Bta])
                                P.op("dve", "tensor_tensor", tc[:, 8:8 + TE], ta[:, 0:TE], ta[:, 8:8 + TE], ALU.add, r=[Bta], w=[Btc])
                    P.op("dve", "tensor_tensor", tc[:, 8:8 + TE], tc[:, 8:8 + TE], icn[g % 2], ALU.mult, r=[Btc, Bicn[g % 2]], w=[Btc])
                    P.op("dve", "tensor_tensor", pooled[:, cc, :], tc[:, 8:8 + TE], u[:, 8:8 + TE], ALU.subtract, r=[Btc, Bu_], w=[Bpl])
                for oc in range(PGC):
                    ch = g * PGC + oc
                    for (q0, n) in tilesE:
                        bank = banks.next()
                        for kc in range(PGC):
                            P.op("pe", "matmul", ps[bank][:, 0:n], plw[g % 2][:, kc, oc * 128:(oc + 1) * 128], pooled[:, kc, q0:q0 + n],
                                 start=(kc == 0), stop=(kc == PGC - 1), r=[Bplw[g % 2], Bpl], w=[PB[bank]])
                        P.op("act", "activation", out=aT[:, ch, q0:q0 + n], in_=ps[bank][:, 0:n], func=AF.Copy, scale=psc[:, ch:ch + 1],
                             r=[PB[bank], B_const], w=[BaT])
            P.barrier(cells)
            if MIX_STOP == 3:
                P.drop_bufs()
                return
            A.seek(99)
            cqnT = A.alloc([QC, TE], BF16); Bcqn = P.buf("cqnT")
            Wcq = [A.alloc([KC, 128], BF16) for _ in range(2)]; BWcq = P.bufs("Wcq", 2)
            cqf = A.alloc([QC, TE], F32); Bcqf = P.bufs("cqf", QC)
            sq = [A.alloc([512], F32) for _ in range(2)]; Bsq = P.bufs("sqq", 2)
            rstd = A.alloc([TE], F32); Brs = P.buf("rstdq")
            banks = Rot([0, 1, 2, 3])
            sqi = 0
            for j in range(QC):
                W = Wcq[j % 2]; BW_ = BWcq[j % 2]
                wdma(W, w_in, c.OFF_CQ + j * 128, 128, BW_)
                for ti, (q0, n) in enumerate(tilesE):
                    bank = banks.next()
                    for kc in range(KC):
                        P.op("pe", "matmul", ps[bank][:, 0:n], W[:, kc, :], hT[:, kc, 8 + q0:8 + q0 + n],
                             start=(kc == 0), stop=(kc == KC - 1), r=[BW_, BhT], w=[PB[bank]])
                    P.op("act", "activation", out=cqf[:, j, q0:q0 + n], in_=ps[bank][:, 0:n], func=AF.Copy, r=[PB[bank]], w=[Bcqf[j]])
                    sqt = sq[sqi % 2]; Bsq_ = Bsq[sqi % 2]; sqi += 1
                    P.op("dve", "tensor_tensor", sqt[:, 0:n], ps[bank][:, 0:n], cqf[:, j, q0:q0 + n], ALU.mult, r=[PB[bank], Bcqf[j]], w=[Bsq_])
                    P.op("pe", "matmul", ps[5 + ti][:, 0:n], ones_f, sqt[:, 0:n], start=(j == 0), stop=(j == QC - 1),
                         r=[Bsq_, B_const], w=[PB[5 + ti]])
            for ti, (q0, n) in enumerate(tilesE):
                P.op("act", "activation", out=rstd[:, q0:q0 + n], in_=ps[5 + ti][:, 0:n], func=AF.Sqrt, bias=EPS, scale=1.0 / c.QL,
                     r=[PB[5 + ti]], w=[Brs])
            P.op("dve", "reciprocal", rstd, rstd, r=[Brs], w=[Brs])
            for j in range(QC):
                P.op("dve", "scalar_tensor_tensor", cqnT[:, j, :], cqf[:, j, :], gqa[:, j:j + 1], rstd, ALU.mult, ALU.mult,
                     r=[Bcqf[j], Brs, B_const], w=[Bcqn])
            P.barrier(cells)
            if MIX_STOP == 4:
                P.drop_bufs()
                return
            A.seek(33)
            QTn = A.alloc([H, TE], BF16); QTr = A.alloc([H, TE], BF16); BQT = P.buf("QT")
            A.check(98)
            A.seek(99 + 2 * QC * TE / 1024 + 1)
            NP5 = 2 if H >= 4 else 1
            HH = H // NP5
            Wuq = A.alloc([QC, HH * QK], BF16); BWuq = P.buf("Wuq")
            gq = A.alloc([2 * QK], F32); csq = A.alloc([NBE, 2, 32], F32); Bc5 = P.buf("c5")
            P.dma("sp", gq, g_q_rep, w=[Bc5])
            for bi, (e0, r) in enumerate(blocksE):
                P.dma("sp", csq[0:r, bi, 0, :], cosq[e0:e0 + r, :], w=[Bc5])
                P.dma("sp", csq[0:r, bi, 1, :], sinq[e0:e0 + r, :], w=[Bc5])
            qsbf = A.alloc([HH * QK], F32); qsb = qsbf.rearrange("p (h d) -> p h d", h=HH); Bqsb = P.buf("qsb")
            ssqq = A.alloc([HH], F32); rsq = A.alloc([HH], F32); Bssq = P.buf("ssqq"); Brq = P.buf("rsq")
            qn = A.alloc([HH, NOPE], BF16); Bqn = P.buf("qn")
            rtq = A.alloc([4, HH, 32], F32); Brtq = P.buf("rtq")
            qrf = A.alloc([HH, ROPE], F32); Bqrf = P.buf("qrf")
            qr = A.alloc([HH, ROPE], BF16); Bqr = P.buf("qr")
            junk3s = [A.alloc([QK], BF16) for _ in range(4)]; Bj3s = P.bufs("junk3", 4); jr3 = Rot([0, 1, 2, 3])
            banks = Rot([0, 1, 2, 3])
            tb_ = Rot([4, 5, 6, 7])
            for hf in range(NP5):
                hb = hf * HH
                for h0 in range(0, HH * QK, 768):
                    n = min(768, HH * QK - h0)
                    wdma(Wuq[:, :, h0:h0 + n], w_uq, hb * QK + h0, n, BWuq)
                for bi, (e0, r) in enumerate(blocksE):
                    for hp in range(HH // 2):
                        bank = banks.next()
                        for kc in range(QC):
                            P.op("pe", "matmul", ps[bank][0:r, 0:2 * QK], cqnT[:, kc, e0:e0 + r], Wuq[:, kc, hp * 2 * QK:(hp + 1) * 2 * QK],
                                 start=(kc == 0), stop=(kc == QC - 1), r=[Bcqn, BWuq], w=[PB[bank]])
                        for hh in range(2):
                            ji = jr3.next()
                            P.op("act", "activation", out=junk3s[ji][0:r, :], in_=ps[bank][0:r, hh * QK:(hh + 1) * QK], func=AF.Square,
                                 accum_out=ssqq[0:r, 2 * hp + hh:2 * hp + hh + 1], r=[PB[bank]], w=[Bj3s[ji], Bssq])
                        P.op("dve", "tensor_tensor", qsbf[0:r, hp * 2 * QK:(hp + 1) * 2 * QK], ps[bank][0:r, 0:2 * QK], gq[0:r, :],
                             ALU.mult, r=[PB[bank], Bc5, Bssq], w=[Bqsb])
                    P.op("act", "activation", out=rsq[0:r, :], in_=ssqq[0:r, :], func=AF.Sqrt, bias=EPS, scale=1.0 / QK, r=[Bssq], w=[Brq])
                    P.op("dve", "reciprocal", rsq[0:r, :], rsq[0:r, :], r=[Brq], w=[Brq])
                    P.op("dve", "tensor_tensor", qn[0:r], qsb[0:r, :, 0:NOPE], rsq[0:r, :].unsqueeze(2).broadcast_to([r, HH, NOPE]), ALU.mult,
                         r=[Bqsb, Brq], w=[Bqn])
                    x1 = qsb[0:r, :, NOPE:NOPE + 32]; x2 = qsb[0:r, :, NOPE + 32:QK]
                    cb_ = csq[0:r, bi, 0, :].unsqueeze(1).broadcast_to([r, HH, 32])
                    sb_ = csq[0:r, bi, 1, :].unsqueeze(1).broadcast_to([r, HH, 32])
                    P.op("dve", "tensor_tensor", rtq[0:r, 0], x1, cb_, ALU.mult, r=[Bqsb, Bc5], w=[Brtq])
                    P.op("dve", "tensor_tensor", rtq[0:r, 1], x2, sb_, ALU.mult, r=[Bqsb, Bc5], w=[Brtq])
                    P.op("dve", "tensor_tensor", rtq[0:r, 2], x2, cb_, ALU.mult, r=[Bqsb, Bc5], w=[Brtq])
                    P.op("dve", "tensor_tensor", rtq[0:r, 3], x1, sb_, ALU.mult, r=[Bqsb, Bc5], w=[Brtq])
                    P.op("dve", "tensor_tensor", qrf[0:r, :, 0:32], rtq[0:r, 0], rtq[0:r, 1], ALU.subtract, r=[Brtq], w=[Bqrf])
                    P.op("dve", "tensor_tensor", qrf[0:r, :, 32:64], rtq[0:r, 2], rtq[0:r, 3], ALU.add, r=[Brtq], w=[Bqrf])
                    P.op("dve", "tensor_tensor", qr[0:r], qrf[0:r], rsq[0:r, :].unsqueeze(2).broadcast_to([r, HH, ROPE]), ALU.mult,
                         r=[Bqrf, Brq], w=[Bqr])
                    for h0 in range(0, HH, 4):
                        nh = min(4, HH - h0)
                        bank = tb_.next()
                        for i in range(nh):
                            P.op("pe", "transpose", psb[bank][:, i * 128:i * 128 + r], qn[0:r, h0 + i, :], ident[0:r, 0:r], r=[Bqn, B_const], w=[PB[bank]])
                        P.op("act", "activation", out=QTn[:, hb + h0:hb + h0 + nh, e0:e0 + r],
                             in_=psb[bank][:, 0:nh * 128].rearrange("p (h t) -> p h t", h=nh)[:, :, 0:r], func=AF.Copy, r=[PB[bank]], w=[BQT])
                    for h0 in range(0, HH, 8):
                        nh = min(8, HH - h0)
                        bank = tb_.next()
                        for i in range(nh):
                            P.op("pe", "transpose", psb[bank][0:ROPE, i * 128:i * 128 + r], qr[0:r, h0 + i, :], ident[0:r, 0:r], r=[Bqr, B_const], w=[PB[bank]])
                        P.op("dve", "tensor_copy", QTr[0:ROPE, hb + h0:hb + h0 + nh, e0:e0 + r],
                             psb[bank][0:ROPE, 0:nh * 128].rearrange("p (h t) -> p h t", h=nh)[:, :, 0:r], r=[PB[bank]], w=[BQT])
            P.barrier(cells)
            if MIX_STOP == 5:
                P.drop_bufs()
                return
            A.seek(98)
            bT = A.alloc([H, TE], BF16); BbT = P.buf("bT")
            A.check(131)
            KCH = c.KCH
            NKB = KCH // 128
            Kn = [A.alloc([KCH], BF16) for _ in range(2)]; BKn = P.bufs("Kn", 2)
            Vh = [A.alloc([NKB, VD], BF16) for _ in range(2)]; BVh = P.bufs("Vh", 2)
            Kr = A.alloc([S], BF16); BKr = P.buf("Kr")
            PT = [A.alloc([512], BF16) for _ in range(4)]; BPT = P.bufs("PT", 4)
            rc = A.alloc([512], F32); Brc = P.buf("rc")
            P.dma("sp", Kr[0:ROPE, :], KRT[s], r=[B_kv[s]], w=[BKr])
            NQT = len(tilesE)
            assert NQT <= 3
            ci = 0
            pti = 0
            sbk = Rot([0, 1])
            for h in range(H):
                for k0 in range(0, S, KCH):
                    Kn_, BKn_ = Kn[ci % 2], BKn[ci % 2]
                    Vh_, BVh_ = Vh[ci % 2], BVh[ci % 2]
                    ci += 1
                    P.dma("sp", Kn_, KT[s, h, :, k0:k0 + KCH], r=[B_kv[s]], w=[BKn_])
                    P.dma("sp", Vh_, VV[s, k0:k0 + KCH, h * VD:(h + 1) * VD].rearrange("(b p) d -> p b d", p=128), r=[B_kv[s]], w=[BVh_])
                    for kb in range(NKB):
                        kg = (k0 // 128) + kb
                        first = (kg == 0)
                        last = (kg == S // 128 - 1)
                        for qi, (q0, n) in enumerate(tilesE):
                            sbank = sbk.next()
                            P.op("pe", "matmul", ps[sbank][:, 0:n], Kn_[:, kb * 128:(kb + 1) * 128], QTn[:, h, q0:q0 + n], start=True, stop=False,
                                 r=[BKn_, BQT], w=[PB[sbank]])
                            P.op("pe", "matmul", ps[sbank][:, 0:n], Kr[0:ROPE, kg * 128:(kg + 1) * 128], QTr[0:ROPE, h, q0:q0 + n], start=False, stop=True,
                                 r=[BKr, BQT], w=[PB[sbank]])
                            pt = PT[pti % 4]; Bpt = BPT[pti % 4]; pti += 1
                            P.op("act", "activation", out=pt[:, 0:n], in_=ps[sbank][:, 0:n], func=AF.Exp,
                                 scale=RSK[:, s * (S // 128) + kg, h:h + 1], r=[PB[sbank], B_rsk], w=[Bpt])
                            P.op("pe", "matmul", ps[2 + 2 * qi][:, 0:n], Vh_[:, kb, :], pt[:, 0:n], start=first, stop=last, r=[BVh_, Bpt], w=[PB[2 + 2 * qi]])
                            P.op("pe", "matmul", ps[3 + 2 * qi][:, 0:n], ones_b, pt[:, 0:n], start=first, stop=last, r=[B_const, Bpt], w=[PB[3 + 2 * qi]])
                for qi, (q0, n) in enumerate(tilesE):
                    P.op("dve", "reciprocal", rc[:, 0:n], ps[3 + 2 * qi][:, 0:n], r=[PB[3 + 2 * qi]], w=[Brc])
                    P.op("dve", "tensor_tensor", bT[:, h, q0:q0 + n], ps[2 + 2 * qi][:, 0:n], rc[:, 0:n], ALU.mult, r=[PB[2 + 2 * qi], Brc], w=[BbT])
            P.barrier(cells)
            if MIX_STOP == 6:
                P.drop_bufs()
                return
            A.seek(33)
            mT = A.alloc([KC, TE], BF16); BmT = P.buf("mT")
            A.check(98)
            A.seek(131)
            Wbp = [A.alloc([PC, 256], BF16) for _ in range(2)]; BWbp = P.bufs("Wbp", 2)
            Wbm = [A.alloc([MAC, 256], BF16) for _ in range(2)]; BWbm = P.bufs("Wbm", 2)
            sgp = [A.alloc([TE], BF16) for _ in range(2)]; sgm = [A.alloc([TE], BF16) for _ in range(2)]
            Bsgp = P.bufs("sgp", 2); Bsgm = P.bufs("sgm", 2)
            t1 = [A.alloc([512], F32) for _ in range(2)]; t2 = [A.alloc([512], F32) for _ in range(2)]
            Bt1 = P.bufs("t1", 2); Bt2 = P.bufs("t2", 2)
            pr = Rot([0, 1, 2, 3])
            ti_ = 0
            for j in range(KC):
                if j % 2 == 0:
                    k2 = (j // 2) % 2
                    ncol = min(256, D - j * 128)
                    wdma(Wbp[k2][:, :, 0:ncol], w_bp, j * 128, ncol, BWbp[k2])
                    wdma(Wbm[k2][:, :, 0:ncol], w_bm, j * 128, ncol, BWbm[k2])
                P.dma("sp", sgp[j % 2], GATE[s, 0, j], r=[B_gate[s]], w=[Bsgp[j % 2]])
                P.dma("sp", sgm[j % 2], GATE[s, 1, j], r=[B_gate[s]], w=[Bsgm[j % 2]])
                jc = (j % 2) * 128
                for (q0, n) in tilesE:
                    pi_ = pr.next()
                    b1, b2 = 2 * pi_, 2 * pi_ + 1
                    for kc in range(PC):
                        P.op("pe", "matmul", ps[b1][:, 0:n], Wbp[k2][:, kc, jc:jc + 128], aT[:, kc, q0:q0 + n], start=(kc == 0), stop=(kc == PC - 1),
                             r=[BWbp[k2], BaT], w=[PB[b1]])
                    for kc in range(MAC):
                        P.op("pe", "matmul", ps[b2][:, 0:n], Wbm[k2][:, kc, jc:jc + 128], bT[:, kc, q0:q0 + n], start=(kc == 0), stop=(kc == MAC - 1),
                             r=[BWbm[k2], BbT], w=[PB[b2]])
                    a1 = t1[ti_ % 2]; a2 = t2[ti_ % 2]; Ba1 = Bt1[ti_ % 2]; Ba2 = Bt2[ti_ % 2]; ti_ += 1
                    P.op("dve", "tensor_tensor", a1[:, 0:n], ps[b1][:, 0:n], sgp[j % 2][:, q0:q0 + n], ALU.mult, r=[PB[b1], Bsgp[j % 2]], w=[Ba1])
                    P.op("dve", "tensor_tensor", a2[:, 0:n], ps[b2][:, 0:n], sgm[j % 2][:, q0:q0 + n], ALU.mult, r=[PB[b2], Bsgm[j % 2]], w=[Ba2])
                    P.op("pool", "tensor_tensor", mT[:, j, q0:q0 + n], a1[:, 0:n], a2[:, 0:n], ALU.add, r=[Ba1, Ba2], w=[BmT])
            P.barrier(cells)
            if MIX_STOP == 7:
                P.drop_bufs()
                return
            A.seek(98)
            Wo = [A.alloc([KC, 512], BF16) for _ in range(2)]; BWo = P.bufs("Wo", 2)
            xs = [A.alloc([512], F32) for _ in range(3)]; Bxs = P.bufs("xs", 3)
            ost = [A.alloc([512], F32) for _ in range(3)]; Bost = P.bufs("ost", 3)
            banks = Rot(list(range(8)))
            xi = 0
            for n_ in range(D // 512):
                W = Wo[n_ % 2]; BW_ = BWo[n_ % 2]
                wdma(W, w_o, n_ * 512, 512, BW_)
                for (e0, r) in blocksE:
                    bank = banks.next()
                    x_ = xs[xi % 3]; Bx_ = Bxs[xi % 3]; o_ = ost[xi % 3]; Bo_ = Bost[xi % 3]; xi += 1
                    P.dma("sp", x_[0:r, :], xe[s, 8 + e0:8 + e0 + r, n_ * 512:(n_ + 1) * 512], w=[Bx_])
                    for kc in range(KC):
                        P.op("pe", "matmul", ps[bank][0:r, :], mT[:, kc, e0:e0 + r], W[:, kc, :], start=(kc == 0), stop=(kc == KC - 1),
                             r=[BmT, BW_], w=[PB[bank]])
                    P.op("dve", "tensor_tensor", o_[0:r, :], ps[bank][0:r, :], x_[0:r, :], ALU.add, r=[PB[bank], Bx_], w=[Bo_])
                    lo = (128 - TE % 128) if (e0 % 128 != 0) else 0
                    P.dma("sp", X1[s, e0 + lo:e0 + r, n_ * 512:(n_ + 1) * 512], o_[lo:r, :], r=[Bo_], wa=[B_x1[s]])
            P.barrier(cells)
            P.drop_bufs()

        def ffn(s, a0):
            TF, FC = c.TF, c.FC
            HF = TF // 2
            N = HF + 2
            A.seek(0)
            gT = A.alloc([FC, TF], BF16); BgT = P.buf("gT")
            mark0 = A.off
            h2T = A.alloc([KC, TF + 2], BF16); Bh2 = P.buf("h2T")
            mark = A.off / 1024
            tmps = alloc_norm_tmp()
            for bi, (i0, r) in enumerate(blocks_of(TF + 2)):
                norm_transpose(X1[s, a0 + i0:a0 + i0 + r, :], r, gffn, h2T, i0, tmps[bi % 2], Bh2, rsrc=[B_x1[s]],
                               mask_ap=valid[a0 + i0:a0 + i0 + r, :])
            P.barrier(cells)
            if FFN_STOP == 1:
                P.drop_bufs()
                return
            A.seek(mark)
            Wg = [A.alloc([KC, 256], BF16) for _ in range(2)]; Wv = [A.alloc([KC, 256], BF16) for _ in range(2)]
            BWg = P.bufs("Wug", 2); BWv = P.bufs("Wuv", 2)
            tg = [A.alloc([HF], F32) for _ in range(2)]; tv = [A.alloc([HF], F32) for _ in range(2)]; sg = [A.alloc([HF], F32) for _ in range(2)]
            Btg = P.bufs("tg", 2); Btv = P.bufs("tv", 2); Bsg = P.bufs("sgl", 2)
            pr = Rot([0, 1, 2, 3])
            ei = 0
            for f in range(FC):
                if f % 2 == 0:
                    k2 = (f // 2) % 2
                    ncol = min(256, c.DFF - f * 128)
                    P.dma("pool", Wg[k2][:, :, 0:ncol], WUP2[f // 2, 0].rearrange("p (k c) -> p k c", k=KC)[:, :, 0:ncol], r=[B_wpre], w=[BWg[k2]])
                    P.dma("pool", Wv[k2][:, :, 0:ncol], WUP2[f // 2, 1].rearrange("p (k c) -> p k c", k=KC)[:, :, 0:ncol], r=[B_wpre], w=[BWv[k2]])
                fc0 = (f % 2) * 128
                for half in range(2):
                    c0 = half * HF
                    pi_ = pr.next()
                    bg, bv = 2 * pi_, 2 * pi_ + 1
                    for kc in range(KC):
                        P.op("pe", "matmul", ps[bg][:, 0:N], Wg[k2][:, kc, fc0:fc0 + 128], h2T[:, kc, c0:c0 + N], start=(kc == 0), stop=(kc == KC - 1),
                             r=[BWg[k2], Bh2], w=[PB[bg]])
                    for kc in range(KC):
                        P.op("pe", "matmul", ps[bv][:, 0:N], Wv[k2][:, kc, fc0:fc0 + 128], h2T[:, kc, c0:c0 + N], start=(kc == 0), stop=(kc == KC - 1),
                             r=[BWv[k2], Bh2], w=[PB[bv]])
                    tg_, tv_, sg_ = tg[ei % 2], tv[ei % 2], sg[ei % 2]
                    Btg_, Btv_, Bsg_ = Btg[ei % 2], Btv[ei % 2], Bsg[ei % 2]
                    ei += 1
                    for (bank, t_, Bt_, ch) in ((bg, tg_, Btg_, f), (bv, tv_, Btv_, FC + f)):
                        P.op("act", "activation", out=t_, in_=ps[bank][:, 1:1 + HF], func=AF.Identity, bias=cb[:, ch:ch + 1], scale=cw[:, 1, ch:ch + 1],
                             r=[PB[bank], B_const], w=[Bt_])
                        P.op("dve", "scalar_tensor_tensor", t_, ps[bank][:, 0:HF], cw[:, 0, ch:ch + 1], t_, ALU.mult, ALU.add,
                             r=[PB[bank], B_const, Bt_], w=[Bt_])
                        P.op("dve", "scalar_tensor_tensor", t_, ps[bank][:, 2:2 + HF], cw[:, 2, ch:ch + 1], t_, ALU.mult, ALU.add,
                             r=[PB[bank], B_const, Bt_], w=[Bt_])
                    P.op("act", "activation", out=sg_, in_=tg_, func=AF.Silu, r=[Btg_], w=[Bsg_])
                    P.op("dve", "tensor_tensor", gT[:, f, c0:c0 + HF], sg_, tv_, ALU.mult, r=[Bsg_, Btv_], w=[BgT])
            P.barrier(cells)
            if FFN_STOP == 2:
                P.drop_bufs()
                return
            A.off = mark0
            groups = FGROUPS
            Wd = [A.alloc([KG, 512], BF16) for _ in range(2)]; BWd = P.bufs("Wd", 2)
            xs = [A.alloc([512], F32) for _ in range(4)]; Bxs = P.bufs("xs2", 4)
            ost = [A.alloc([512], F32) for _ in range(4)]; Bost = P.bufs("ost2", 4)
            NBk = TF // 128
            wi = 0
            xi = 0
            for n_ in range(D // 512):
                for gi, (f0, nf) in enumerate(groups):
                    W = Wd[wi % 2]; BW_ = BWd[wi % 2]; wi += 1
                    P.dma("pool", W[:, 0:nf, :], WD2[n_, gi].rearrange("p (k c) -> p k c", k=KG)[:, 0:nf, :], r=[B_wpre], w=[BW_])
                    for b in range(NBk):
                        bank = b + NBk * (n_ % (8 // NBk))
                        for fi in range(nf):
                            f = f0 + fi
                            P.op("pe", "matmul", ps[bank][:, :], gT[:, f, b * 128:(b + 1) * 128], W[:, fi, :], start=(f == 0), stop=(f == FC - 1),
                                 r=[BgT, BW_], w=[PB[bank]])
                for b in range(NBk):
                    bank = b + NBk * (n_ % (8 // NBk))
                    x_ = xs[xi % 4]; Bx_ = Bxs[xi % 4]; o_ = ost[xi % 4]; Bo_ = Bost[xi % 4]; xi += 1
                    if FFN_STOP == 3:
                        continue
                    P.dma("sp", x_, X1[s, a0 + 1 + b * 128:a0 + 1 + (b + 1) * 128, n_ * 512:(n_ + 1) * 512], r=[B_x1[s]], w=[Bx_])
                    P.op("dve", "tensor_tensor", o_, ps[bank][:, :], x_, ALU.add, r=[PB[bank], Bx_], w=[Bo_])
                    if FFN_STOP == 4:
                        continue
                    ydst = X1 if FFN_STOP == 5 else y
                    P.dma("sp", ydst[s, a0 + b * 128:a0 + (b + 1) * 128, n_ * 512:(n_ + 1) * 512], o_, r=[Bo_], wa=[B_y])
            P.barrier(cells)
            P.drop_bufs()

        if "p1" in stages:
            phase1()
            P.barrier(cells)
            P.drop_bufs()
        for s in range(NSEQ):
            if "mix" in stages:
                mixer(s)
            if "ffn" in stages:
                for a0 in range(0, TQ, c.TF):
                    ffn(s, a0)
        last = P._add("sp", "dma_start", (), dict(out=cells["sp_dst"], in_=cells["sp_src"]), [],
                      P.phase_bufs + B_kv + B_gate + B_x1 + [B_y, B_const, B_rsk, B_wpre, P.cell_buf], True)
        P.assign()
        n_ops = {e: len(v) for e, v in P.ops.items()}
        print("[build] ops per engine:", n_ops, "sems left", len(P.free_sems))

        @block.tensor
        def _(e):
            P.emit("pe", e)

        @block.scalar
        def _(e):
            P.emit("act", e)

        @block.vector
        def _(e):
            P.emit("dve", e)

        @block.gpsimd
        def _(e):
            P.emit("pool", e)

        @block.sync
        def _(e):
            P.emit("sp", e)
            e.wait_ge(last.sem, last.val)
    return nc


def rope_tables_np(n_pos):
    inv = (1.0 / (np.float32(10000.0) ** (np.arange(0, ROPE, 2, dtype=np.float32) / np.float32(ROPE)))).astype(np.float32)
    ang = (np.arange(n_pos, dtype=np.float32)[:, None] * inv[None, :]).astype(np.float32)
    return np.cos(ang).astype(np.float32), np.sin(ang).astype(np.float32)


def host_inputs(cfg, inp):
    c = cfg
    D, S, H = c.D, c.S, c.H
    f32 = np.float32
    xs = np.concatenate([np.asarray(inp["x_prompt"], f32), np.asarray(inp["x_sample"], f32)], 0)
    xall = np.ascontiguousarray(xs.reshape(NSEQ * S, D))

    def col(v, n):
        return np.ascontiguousarray(np.asarray(v, f32).reshape(n, 128).T)

    cosk, sink = rope_tables_np(S)
    qg = np.asarray(inp["q_norm_gain"], f32).reshape(QK)
    kg = np.asarray(inp["k_norm_gain"], f32).reshape(QK)
    cw = np.asarray(inp["conv_w"], f32).reshape(3, 2 * c.DFF)
    common = {
        "xall": xall, "cosk": cosk, "sink": sink,
        "g_mix": col(inp["norm_mix_gain"], c.KC), "g_ffn": col(inp["norm_ffn_gain"], c.KC),
        "g_qa": col(inp["q_a_norm_gain"], c.QC), "g_kva": col(inp["kv_a_norm_gain"], c.KVC),
        "g_q_rep": np.ascontiguousarray(np.broadcast_to(np.concatenate([qg, qg])[None, :], (128, 2 * QK))),
        "g_kr_rep": np.ascontiguousarray(np.broadcast_to(kg[None, NOPE:], (128, ROPE))),
        "g_kn": np.ascontiguousarray(kg[:NOPE].reshape(128, 1)),
        "pscale": col(inp["pool_scale"], c.PC),
        "convw": np.ascontiguousarray(cw.reshape(3, 2 * c.FC, 128).transpose(2, 0, 1)),
        "convb": col(inp["conv_b"], 2 * c.FC),
        "ident": np.eye(128, dtype=f32),
        "w_in": np.asarray(inp["w_in"], f32).reshape(D, c.IN_COLS),
        "pool_w": np.asarray(inp["pool_w"], f32).reshape(c.PW, c.PG),
        "w_uq": np.asarray(inp["w_uq"], f32).reshape(c.QL, H * QK),
        "w_ukv": np.asarray(inp["w_ukv"], f32).reshape(c.KVL, H * 256),
        "w_bp": np.asarray(inp["w_branch_pool"], f32).reshape(c.PW, D),
        "w_bm": np.asarray(inp["w_branch_mla"], f32).reshape(c.MA, D),
        "w_o": np.asarray(inp["w_o"], f32).reshape(D, D),
        "w_up": np.asarray(inp["w_up"], f32).reshape(D, 2 * c.DFF),
        "w_down": np.asarray(inp["w_down"], f32).reshape(c.DFF, D),
    }
    maps = []
    for core in range(c.NCORE):
        start = core * c.TQ
        lo = start - HALO
        xe_ = np.zeros((NSEQ, c.TU, D), f32)
        a, b = max(lo, 0), min(lo + c.TU, S)
        xe_[:, a - lo:b - lo, :] = xs[:, a:b, :]
        pos = np.arange(c.TE) + start - 1
        ok = (pos >= 0) & (pos < S)
        pc = np.clip(pos, 0, S - 1)
        inv = np.zeros((4, c.TE), f32)
        for gi, w in enumerate(WINDOWS):
            lo_w = np.clip(pos - w // 2, 0, S)
            hi_w = np.clip(pos + (w - w // 2), 0, S)
            cnt = np.maximum(hi_w - lo_w, 1)
            inv[gi] = (1.0 / cnt.astype(f32)).astype(f32)
        m = dict(common)
        m.update({
            "xe": xe_, "cosq": np.ascontiguousarray(cosk[pc]), "sinq": np.ascontiguousarray(sink[pc]),
            "invcnt": np.ascontiguousarray(np.broadcast_to(inv[None], (128, 4, c.TE))),
            "valid": ok.astype(f32).reshape(c.TE, 1),
        })
        maps.append(m)
    return maps


_FULL = None


def run(cfg, inp, debug=False):
    nc = build(cfg, debug)
    maps = host_inputs(cfg, inp)
    res = run_bass_kernel_spmd(nc, maps, core_ids=list(range(cfg.NCORE)))
    return res


def kernel(**inputs):
    cfg = Cfg()
    res = run(cfg, inputs)
    ys = np.stack([r["yout"] for r in res.results], 0)
    full = ys.transpose(1, 0, 2, 3).reshape(NSEQ, cfg.S, cfg.D)
    return (np.ascontiguousarray(full[:2]), np.ascontiguousarray(full[2:3]))
```

```python
import math
from contextlib import ExitStack

import numpy as np
import concourse.bass as bass
import concourse.mybir as mybir
from concourse.bass_utils import run_bass_kernel_spmd

F32 = mybir.dt.float32
BF16 = mybir.dt.bfloat16
AF = mybir.ActivationFunctionType
ALU = mybir.AluOpType
EPS = 1e-6
NSEQ = 3
ROPE = 64
NOPE = 128
VD = 128
QK = 192
WINDOWS = (2, 4, 8, 16)
HALO = 9


class Cfg:
    def __init__(self, D=4096, S=8192, H=16, DFF=11008, NCORE=8):
        self.D, self.S, self.H, self.DFF, self.NCORE = D, S, H, DFF, NCORE
        self.KC = D // 128
        self.PW = D // 2
        self.PG = self.PW // 4
        self.PC = self.PW // 128
        self.PGC = self.PG // 128
        self.QL = D // 4
        self.QC = self.QL // 128
        self.KVL = D // 8
        self.KVC = self.KVL // 128
        self.FC = DFF // 128
        self.TQ = S // NCORE
        self.TU = self.TQ + 2 * HALO
        self.TE = self.TQ + 2
        self.OFF_CQ = self.PW
        self.OFF_CKV = self.OFF_CQ + self.QL
        self.OFF_KR = self.OFF_CKV + self.KVL
        self.OFF_GP = self.OFF_KR + ROPE
        self.OFF_GM = self.OFF_GP + D
        self.IN_COLS = self.OFF_GM + D
        self.SBT = min(256, S)
        self.TF = min(512, self.TQ)
        self.KCH = min(4096, S)
        self.MA = H * VD
        self.MAC = self.MA // 128


class Buf:
    __slots__ = ("name", "w", "r", "dw", "dr")

    def __init__(self, name):
        self.name = name
        self.w = {}
        self.r = {}
        self.dw = []
        self.dr = []


class Op:
    __slots__ = ("eng", "meth", "args", "kw", "dma", "deps", "inc", "sem", "val")

    def __init__(self, eng, meth, args, kw, dma):
        self.eng, self.meth, self.args, self.kw, self.dma = eng, meth, args, kw, dma
        self.deps = []
        self.inc = False
        self.sem = None
        self.val = 0


SEM_LIMIT = 30000
import os
P1_STOP = int(os.environ.get("P1_STOP", "0"))
MIX_STOP = int(os.environ.get("MIX_STOP", "0"))
FFN_STOP = int(os.environ.get("FFN_STOP", "0"))
RING = 12


class Prog:
    ENGS = ("pe", "act", "dve", "pool", "sp")

    def __init__(self, sems):
        self.ops = {e: [] for e in self.ENGS}
        self.order = []
        self.free_sems = list(sems)
        self.phase_bufs = []
        self.cell_buf = Buf("cell")

    def buf(self, name, persistent=False):
        b = Buf(name)
        if not persistent:
            self.phase_bufs.append(b)
        return b

    def bufs(self, name, n, persistent=False):
        return [self.buf(f"{name}{i}", persistent) for i in range(n)]

    def _add(self, eng, meth, args, kw, r, w, dma):
        op = Op(eng, meth, args, kw, dma)
        deps = op.deps
        for b in r:
            for e, p in b.w.items():
                if e == eng and not dma and eng == "pe":
                    continue
                deps.append(p)
            deps.extend(b.dw)
        for b in w:
            for e, p in b.r.items():
                if e == eng and not dma and eng == "pe":
                    continue
                deps.append(p)
            for e, p in b.w.items():
                if e == eng and not dma and eng == "pe":
                    continue
                deps.append(p)
            deps.extend(b.dr)
            deps.extend(b.dw)
        for b in r:
            if dma:
                b.dr.append(op)
            else:
                b.r[eng] = op
        for b in w:
            if dma:
                b.w = {}
                b.r = {}
                b.dr = []
                b.dw = [op]
            else:
                b.w = {eng: op}
                b.r = {}
                b.dr = []
                b.dw = []
        for p in deps:
            p.inc = True
        if dma:
            op.inc = True
        self.ops[eng].append(op)
        self.order.append(op)
        return op

    def op(self, eng, meth, *args, r=(), w=(), **kw):
        return self._add(eng, meth, args, kw, r, w, False)

    def dma(self, q, out, in_, r=(), w=(), wa=(), **kw):
        kw = dict(kw)
        kw["out"] = out
        kw["in_"] = in_
        op = self._add(q, "dma_start", (), kw, r, w, True)
        for b in wa:
            b.dw.append(op)
        return op

    def barrier(self, cells):
        bl = list(self.phase_bufs) + [self.cell_buf]
        self._add("act", "memzero", (cells["act"],), {}, [], bl, False)
        for eng in ("dve", "pool"):
            self._add(eng, "memset", (cells[eng], 0.0), {}, [], bl, False)
        self._add("sp", "dma_start", (), dict(out=cells["sp_dst"], in_=cells["sp_src"]), [], bl, True)

    def drop_bufs(self):
        self.phase_bufs = []

    def assign(self):
        cur = {}
        ring = {q: [dict(sem=None, val=0, last=None) for _ in range(RING)] for q in ("sp", "pool")}
        rcnt = {"sp": 0, "pool": 0}
        for op in self.order:
            if not op.inc:
                continue
            if op.dma:
                slots = ring[op.eng]
                sl = slots[rcnt[op.eng] % len(slots)]
                rcnt[op.eng] += 1
                if sl["last"] is not None:
                    op.deps.append(sl["last"])
                if sl["sem"] is None or sl["val"] + 16 > SEM_LIMIT:
                    sl["sem"] = self.free_sems.pop()
                    sl["val"] = 0
                sl["val"] += 16
                sl["last"] = op
                op.sem, op.val = sl["sem"], sl["val"]
            else:
                cc = cur.get(op.eng)
                if cc is None or cc[1] + 1 > SEM_LIMIT:
                    cc = [self.free_sems.pop(), 0]
                    cur[op.eng] = cc
                cc[1] += 1
                op.sem, op.val = cc[0], cc[1]

    def emit(self, eng_name, eng):
        known = {}
        for op in self.ops[eng_name]:
            need = {}
            for p in op.deps:
                k = id(p.sem)
                if known.get(k, 0) >= p.val:
                    continue
                if k not in need or need[k][1] < p.val:
                    need[k] = (p.sem, p.val)
            for k, (sem, val) in need.items():
                eng.wait_ge(sem, val)
                known[k] = val
            ins = getattr(eng, op.meth)(*op.args, **op.kw)
            if op.inc:
                ins.then_inc(op.sem, 16 if op.dma else 1)


class Arena:
    def __init__(self, ap_f32, nbytes):
        self.ap = ap_f32
        self.nbytes = nbytes
        self.off = 0

    def seek(self, kb):
        self.off = int(kb * 1024)

    def check(self, kb):
        assert self.off <= kb * 1024, f"arena region overflow: {self.off} > {kb}KB"

    def alloc(self, shape, dtype, parts=128):
        esz = 4 if dtype == F32 else 2
        n = int(np.prod(shape))
        nb = (n * esz + 3) // 4 * 4
        assert self.off + nb <= self.nbytes, f"arena overflow: {self.off}+{nb} > {self.nbytes}"
        a = self.ap[0:parts, self.off // 4:(self.off + nb) // 4]
        self.off += nb
        if dtype != F32:
            a = a.bitcast(dtype)
            if a.shape[1] != n:
                a = a[:, 0:n]
        if len(shape) == 2:
            a = a.rearrange("p (a b) -> p a b", a=shape[0])
        elif len(shape) == 3:
            a = a.rearrange("p (a b c) -> p a b c", a=shape[0], b=shape[1])
        return a


def tiles_of(n, t):
    return [(i, min(t, n - i)) for i in range(0, n, t)]


def blocks_of(n):
    bl = [(i * 128, 128) for i in range(n // 128)]
    if n % 128:
        bl.append((n - 128, 128))
    return bl


class Rot:
    def __init__(self, items):
        self.items = list(items)
        self.i = 0

    def next(self):
        v = self.items[self.i % len(self.items)]
        self.i += 1
        return v


def build(cfg, debug=False, stages=("p1", "mix", "ffn")):
    c = cfg
    D, S, H, KC = c.D, c.S, c.H, c.KC
    TE, TU, TQ = c.TE, c.TU, c.TQ
    nc = bass.Bass("TRN2", target_bir_lowering=False)

    def din(name, shape, dt=F32):
        return nc.dram_tensor(name, list(shape), dt, kind="ExternalInput").ap()

    def dscr(name, shape, dt):
        return nc.dram_tensor(name, list(shape), dt, kind="ExternalOutput" if debug else "Internal").ap()

    xall = din("xall", [NSEQ * S, D])
    xe = din("xe", [NSEQ, TU, D])
    cosk = din("cosk", [S, 32]); sink = din("sink", [S, 32])
    cosq = din("cosq", [TE, 32]); sinq = din("sinq", [TE, 32])
    invcnt = din("invcnt", [128, 4, TE])
    valid = din("valid", [TE, 1])
    g_mix = din("g_mix", [128, KC]); g_ffn = din("g_ffn", [128, KC])
    g_qa = din("g_qa", [128, c.QC]); g_kva = din("g_kva", [128, c.KVC])
    g_q_rep = din("g_q_rep", [128, 2 * QK]); g_kr_rep = din("g_kr_rep", [128, ROPE]); g_kn = din("g_kn", [128, 1])
    pscale = din("pscale", [128, c.PC])
    convw = din("convw", [128, 3, 2 * c.FC]); convb = din("convb", [128, 2 * c.FC])
    ident_in = din("ident", [128, 128])
    w_in = din("w_in", [D, c.IN_COLS])
    pool_w = din("pool_w", [c.PW, c.PG])
    w_uq = din("w_uq", [c.QL, H * QK])
    w_ukv = din("w_ukv", [c.KVL, H * 256])
    w_bp = din("w_bp", [c.PW, D]); w_bm = din("w_bm", [c.MA, D])
    w_o = din("w_o", [D, D])
    w_up = din("w_up", [D, 2 * c.DFF]); w_down = din("w_down", [c.DFF, D])
    y = nc.dram_tensor("yout", [NSEQ, TQ, D], F32, kind="ExternalOutput").ap()
    KT = dscr("KT", [NSEQ, H, 128, S], BF16)
    KRT = dscr("KRT", [NSEQ, ROPE, S], BF16)
    VV = dscr("VV", [NSEQ, S, H * VD], BF16)
    GATE = dscr("GATE", [NSEQ, 2, KC, 128, TE], BF16)
    X1 = dscr("X1", [NSEQ, TE, D], F32)
    dummy_d = nc.dram_tensor("dummy_d", [1, 16], F32, kind="Internal").ap()
    NPAN = (c.FC + 1) // 2
    KG = 22
    FGROUPS = tiles_of(c.FC, KG)
    WUP2 = nc.dram_tensor("WUP2", [NPAN, 2, 128, KC * 256], BF16, kind="Internal").ap()
    WD2 = nc.dram_tensor("WD2", [D // 512, len(FGROUPS), 128, KG * 512], BF16, kind="Internal").ap()

    NBLK_ALL = NSEQ * S // 128
    NBE = (TE + 127) // 128
    ARENA_BYTES = 190 * 1024
    pers_sizes = dict(ones_f=128, ident=64, ones_b=64, rsk=NBLK_ALL * H, cells=32, gmix=KC, gffn=KC, gqa=c.QC,
                      gkva=c.KVC, gkr=ROPE, gkn=1, pscale=c.PC, convw=6 * c.FC, convb=2 * c.FC)
    PERS_F32 = sum(pers_sizes.values()) + 16

    with ExitStack() as es:
        pers = es.enter_context(nc.sbuf_tensor("pers", [128, PERS_F32], F32))
        arena_t = es.enter_context(nc.sbuf_tensor("arena", [128, ARENA_BYTES // 4], F32))
        psum = [es.enter_context(nc.psum_tensor(f"ps{i}", [128, 512], F32)) for i in range(8)]
        sems = [es.enter_context(nc.semaphore(f"sm{i}")) for i in range(96)]
        block = es.enter_context(nc.Block())

        P = Prog(sems)
        A = Arena(arena_t, ARENA_BYTES)
        pv = {}
        o = 0
        for k, n in pers_sizes.items():
            pv[k] = pers[:, o:o + n]
            o += n
        ones_f = pv["ones_f"]
        ident = pv["ident"].bitcast(BF16)
        ones_b = pv["ones_b"].bitcast(BF16)
        RSK = pv["rsk"].rearrange("p (b h) -> p b h", h=H)
        cl = pv["cells"]
        cells = {"act": cl[:, 0:1], "dve": cl[:, 1:2], "pool": cl[:, 2:3], "sp_dst": cl[0:1, 8:24], "sp_src": ident_in[0:1, 0:16]}
        gmix, gffn, gqa, gkva = pv["gmix"], pv["gffn"], pv["gqa"], pv["gkva"]
        gkr, gkn, psc = pv["gkr"], pv["gkn"], pv["pscale"]
        cw = pv["convw"].rearrange("p (t j) -> p t j", t=3)
        cb = pv["convb"]
        ps = [p[:] for p in psum]
        psb = [p[:].bitcast(BF16) for p in psum]
        PB = P.bufs("psum", 8, persistent=True)
        B_const = P.buf("const", persistent=True)
        B_rsk = P.buf("rsk", persistent=True)
        B_y = P.buf("y", persistent=True)
        B_wpre = P.buf("wpre", persistent=True)
        B_kv = P.bufs("kvscr", NSEQ, persistent=True)
        B_gate = P.bufs("gate", NSEQ, persistent=True)
        B_x1 = P.bufs("x1", NSEQ, persistent=True)

        A.seek(189)
        t_id = A.alloc([128], F32)
        bt = P.buf("t_id")
        P.dma("sp", t_id, ident_in, w=[bt])
        P.op("dve", "tensor_copy", ident, t_id, r=[bt], w=[B_const])
        P.op("dve", "memset", ones_f, 1.0, w=[B_const])
        P.op("dve", "memset", ones_b, 1.0, w=[B_const])
        P.op("dve", "memset", cl, 0.0, w=[B_const, P.cell_buf])
        for dst, src in ((gmix, g_mix), (gffn, g_ffn), (gqa, g_qa), (gkva, g_kva), (gkr, g_kr_rep),
                         (gkn, g_kn), (psc, pscale), (cw, convw), (cb, convb)):
            P.dma("sp", dst, src, w=[B_const])
        P.barrier(cells)
        P.drop_bufs()

        def alloc_norm_tmp():
            tm = []
            for i in range(2):
                xn_ = A.alloc([D], BF16); Bn_ = P.buf("xn")
                tm.append(dict(xb=A.alloc([D], F32), junk=xn_, xn=xn_, st=A.alloc([4], F32),
                               Bx=P.buf("xb"), Bj=Bn_, Bn=Bn_, Bs=P.buf("st"), Bm=P.buf("msk")))
            return tm

        def norm_transpose(src, r, gain, hT, col0, t, BhT, rsrc=(), mask_ap=None):
            xb, junk, xn, st = t["xb"], t["junk"], t["xn"], t["st"]
            Bx, Bj, Bn, Bs, Bm = t["Bx"], t["Bj"], t["Bn"], t["Bs"], t["Bm"]
            P.dma("sp", xb[0:r, :], src, r=list(rsrc), w=[Bx])
            if mask_ap is not None:
                P.dma("sp", st[0:r, 3:4], mask_ap, w=[Bm])
            P.op("act", "activation", out=junk[0:r, :], in_=xb[0:r, :], func=AF.Square, accum_out=st[0:r, 0:1], r=[Bx], w=[Bj, Bs])
            P.op("act", "activation", out=st[0:r, 1:2], in_=st[0:r, 0:1], func=AF.Sqrt, bias=EPS, scale=1.0 / D, r=[Bs], w=[Bs])
            P.op("dve", "reciprocal", st[0:r, 2:3], st[0:r, 1:2], r=[Bs], w=[Bs])
            if mask_ap is not None:
                P.op("dve", "tensor_tensor", st[0:r, 2:3], st[0:r, 2:3], st[0:r, 3:4], ALU.mult, r=[Bs, Bm], w=[Bs])
            P.op("dve", "tensor_scalar", xn[0:r, :], xb[0:r, :], st[0:r, 2:3], None, ALU.mult, r=[Bx, Bs], w=[Bn])
            for g4 in range(0, KC, 4):
                bank = (g4 // 4) % 2
                n4 = min(4, KC - g4)
                for i in range(n4):
                    kc = g4 + i
                    P.op("pe", "transpose", psb[bank][:, i * 128:i * 128 + r], xn[0:r, kc * 128:(kc + 1) * 128], ident[0:r, 0:r],
                         r=[Bn, B_const], w=[PB[bank]])
                for i in range(n4):
                    kc = g4 + i
                    if kc % 2 == 0:
                        P.op("act", "activation", out=hT[:, kc, col0:col0 + r], in_=psb[bank][:, i * 128:i * 128 + r], func=AF.Copy,
                             scale=gain[:, kc:kc + 1], r=[PB[bank], B_const], w=[BhT])
                    else:
                        P.op("dve", "tensor_scalar", hT[:, kc, col0:col0 + r], psb[bank][:, i * 128:i * 128 + r], gain[:, kc:kc + 1],
                             None, ALU.mult, r=[PB[bank], B_const], w=[BhT])

        def wdma(dst, src_rows, col0, ncols, B, row0=0, nrows=None):
            nrows = src_rows.shape[0] - row0 if nrows is None else nrows
            src = src_rows[row0:row0 + nrows, col0:col0 + ncols].rearrange("(k p) c -> p k c", p=128)
            P.dma("pool", dst, src, w=[B], max_dma_last_dim=4096)

        def phase1():
            SBT = c.SBT
            NB = SBT // 128
            KVC = c.KVC
            A.seek(0)
            Wckv = A.alloc([KC, c.KVL], BF16); Wkr = A.alloc([KC, ROPE], BF16); Wukv = A.alloc([KVC, H * 256], BF16)
            BW = P.buf("p1w")
            wdma(Wckv, w_in, c.OFF_CKV, c.KVL, BW)
            wdma(Wkr, w_in, c.OFF_KR, ROPE, BW)
            for h0 in range(0, H * 256, 1024):
                n = min(1024, H * 256 - h0)
                wdma(Wukv[:, :, h0:h0 + n], w_ukv, h0, n, BW)
            for pan in range(NPAN):
                ncol = min(256, c.DFF - pan * 256)
                for gv in range(2):
                    dst = WUP2[pan, gv].rearrange("p (k c) -> p k c", k=KC)[:, :, 0:ncol]
                    src = w_up[:, gv * c.DFF + pan * 256:gv * c.DFF + pan * 256 + ncol].rearrange("(k p) c -> p k c", p=128)
                    P.dma("pool", dst, src, wa=[B_wpre], max_dma_last_dim=4096)
            for n_ in range(D // 512):
                for gi, (f0, nf) in enumerate(FGROUPS):
                    dst = WD2[n_, gi].rearrange("p (k c) -> p k c", k=KG)[:, 0:nf, :]
                    src = w_down[f0 * 128:(f0 + nf) * 128, n_ * 512:(n_ + 1) * 512].rearrange("(k p) c -> p k c", p=128)
                    P.dma("pool", dst, src, wa=[B_wpre], max_dma_last_dim=4096)
            tmps = alloc_norm_tmp()
            hT = [A.alloc([KC, SBT], BF16) for _ in range(2)]
            BhT = P.bufs("hT", 2)
            ckvf = A.alloc([KVC, SBT], F32); Bckvf = P.bufs("ckvf", KVC)
            sq = [A.alloc([SBT], F32) for _ in range(2)]; Bsq = P.bufs("sq", 2)
            rstdkv = A.alloc([SBT], F32); Brkv = P.buf("rstdkv")
            ckvn = A.alloc([KVC, SBT], BF16); Bckvn = P.buf("ckvn")
            krs = A.alloc([NB, 8], F32); Bkrs = P.buf("krs")
            krg = A.alloc([ROPE], F32); Bkrg = P.buf("krg")
            cs = [A.alloc([2, 32], F32) for _ in range(2)]; Bcs = P.bufs("cs", 2)
            rt = A.alloc([4, 32], F32); Brt = P.buf("rt")
            krb = A.alloc([ROPE], BF16); Bkrb = P.buf("krb")
            ssqk = A.alloc([NB, H], F32); Bssqk = P.buf("ssqk")
            Vst = A.alloc([NB, H * VD], BF16); BVst = P.buf("Vst")
            KTst = A.alloc([H, SBT], BF16); BKT = P.buf("KTst")
            KRst = A.alloc([SBT], BF16); BKR = P.buf("KRst")
            junk2s = [A.alloc([QK], BF16) for _ in range(4)]; Bj2s = P.bufs("junk2", 4); jr = Rot([0, 1, 2, 3])
            nsb = S // SBT
            Bser = P.buf("ser")
            blk_i = 0
            for s in range(NSEQ):
                Bscr = B_kv[s]
                for sb in range(nsb):
                    it = s * nsb + sb
                    hTc = hT[it % 2]; BhTc = BhT[it % 2]
                    t0 = sb * SBT
                    for b in range(NB):
                        t = tmps[blk_i % 2]; blk_i += 1
                        row0 = s * S + t0 + b * 128
                        norm_transpose(xall[row0:row0 + 128, :], 128, gmix, hTc, b * 128, t, BhTc)
                    if P1_STOP == 1:
                        continue
                    for j in range(KVC):
                        bank = 2 + (j % 2)
                        for kc in range(KC):
                            P.op("pe", "matmul", ps[bank][:, 0:SBT], Wckv[:, kc, j * 128:(j + 1) * 128], hTc[:, kc, :],
                                 start=(kc == 0), stop=(kc == KC - 1), r=[BW, BhTc], w=[PB[bank]])
                        P.op("act", "activation", out=ckvf[:, j, :], in_=ps[bank][:, 0:SBT], func=AF.Copy, r=[PB[bank]], w=[Bckvf[j]])
                        P.op("dve", "tensor_tensor", sq[j % 2], ps[bank][:, 0:SBT], ckvf[:, j, :], ALU.mult,
                             r=[PB[bank], Bckvf[j]], w=[Bsq[j % 2]])
                        P.op("pe", "matmul", ps[4][:, 0:SBT], ones_f, sq[j % 2], start=(j == 0), stop=(j == KVC - 1),
                             r=[Bsq[j % 2], B_const], w=[PB[4]])
                    P.op("act", "activation", out=rstdkv, in_=ps[4][:, 0:SBT], func=AF.Sqrt, bias=EPS, scale=1.0 / c.KVL,
                         r=[PB[4]], w=[Brkv])
                    P.op("dve", "reciprocal", rstdkv, rstdkv, r=[Brkv], w=[Brkv])
                    for j in range(KVC):
                        P.op("dve", "scalar_tensor_tensor", ckvn[:, j, :], ckvf[:, j, :], gkva[:, j:j + 1], rstdkv, ALU.mult, ALU.mult,
                             r=[Bckvf[j], Brkv, B_const], w=[Bckvn])
                    if P1_STOP == 2:
                        continue
                    for b in range(NB):
                        for kc in range(KC):
                            P.op("pe", "matmul", ps[5][:, b * ROPE:(b + 1) * ROPE], hTc[:, kc, b * 128:(b + 1) * 128], Wkr[:, kc, :],
                                 start=(kc == 0), stop=(kc == KC - 1), r=[BW, BhTc], w=[PB[5]])
                    for b in range(NB):
                        pk = ps[5][:, b * ROPE:(b + 1) * ROPE]
                        pos0 = t0 + b * 128
                        csb = cs[b % 2]; Bcsb = Bcs[b % 2]
                        P.dma("sp", csb[:, 0, :], cosk[pos0:pos0 + 128, :], w=[Bcsb])
                        P.dma("sp", csb[:, 1, :], sink[pos0:pos0 + 128, :], w=[Bcsb])
                        ji = jr.next()
                        P.op("act", "activation", out=junk2s[ji][:, 0:ROPE], in_=pk, func=AF.Square, accum_out=krs[:, b, 0:1],
                             r=[PB[5]], w=[Bj2s[ji], Bkrs])
                        P.op("dve", "tensor_tensor", krg, pk, gkr, ALU.mult, r=[PB[5], B_const], w=[Bkrg])
                        x1 = krg[:, 0:32]; x2 = krg[:, 32:64]
                        P.op("dve", "tensor_tensor", rt[:, 0, :], x1, csb[:, 0, :], ALU.mult, r=[Bkrg, Bcsb], w=[Brt])
                        P.op("dve", "tensor_tensor", rt[:, 1, :], x2, csb[:, 1, :], ALU.mult, r=[Bkrg, Bcsb], w=[Brt])
                        P.op("dve", "tensor_tensor", rt[:, 2, :], x2, csb[:, 0, :], ALU.mult, r=[Bkrg, Bcsb], w=[Brt])
                        P.op("dve", "tensor_tensor", rt[:, 3, :], x1, csb[:, 1, :], ALU.mult, r=[Bkrg, Bcsb], w=[Brt])
                        P.op("dve", "tensor_tensor", krb[:, 0:32], rt[:, 0, :], rt[:, 1, :], ALU.subtract, r=[Brt], w=[Bkrb])
                        P.op("dve", "tensor_tensor", krb[:, 32:64], rt[:, 2, :], rt[:, 3, :], ALU.add, r=[Brt], w=[Bkrb])
                        P.op("pe", "transpose", psb[6][0:ROPE, b * 128:(b + 1) * 128], krb, ident, r=[Bkrb, B_const], w=[PB[6]])
                    P.op("act", "activation", out=KRst[0:ROPE, :], in_=psb[6][0:ROPE, 0:SBT], func=AF.Copy, r=[PB[6]], w=[BKR])
                    P.dma("sp", KRT[s, :, t0:t0 + SBT], KRst[0:ROPE, :], r=[BKR], wa=[Bscr])
                    if P1_STOP == 3:
                        continue
                    for b in range(NB):
                        for n in range(H // 2):
                            bank = (n % 2)
                            for kc in range(KVC):
                                P.op("pe", "matmul", ps[bank][:, :], ckvn[:, kc, b * 128:(b + 1) * 128], Wukv[:, kc, n * 512:(n + 1) * 512],
                                     start=(kc == 0), stop=(kc == KVC - 1), r=[Bckvn, BW], w=[PB[bank]])
                            pvw = ps[bank][:, :].rearrange("p (h t d) -> p h t d", h=2, t=2)
                            if P1_STOP == 31:
                                continue
                            P.op("dve", "tensor_copy", Vst[:, b, n * 256:(n + 1) * 256].rearrange("p (h d) -> p h d", h=2), pvw[:, :, 1, :],
                                 r=[PB[bank]], w=[BVst, Bser])
                            if P1_STOP == 32:
                                continue
                            for hh in range(2):
                                ji = jr.next()
                                if P1_STOP in (35, 36):
                                    P.op("act", "activation", out=junk2s[ji][:, 0:128], in_=ps[bank][:, hh * 256:hh * 256 + 128], func=AF.Square,
                                         r=[PB[bank]], w=[Bj2s[ji], Bssqk])
                                    continue
                                P.op("act", "activation", out=junk2s[ji][:, 0:128], in_=ps[bank][:, hh * 256:hh * 256 + 128], func=AF.Square,
                                     accum_out=ssqk[:, b, 2 * n + hh:2 * n + hh + 1], r=[PB[bank], Bser], w=[Bj2s[ji], Bssqk])
                    if P1_STOP in (31, 32, 33, 35, 36):
                        continue
                    for b in range(NB):
                        gb = (s * S + t0) // 128 + b
                        P.op("dve", "tensor_scalar", ssqk[:, b, :], ssqk[:, b, :], krs[:, b, 0:1], EPS * QK, ALU.add, ALU.add, r=[Bssqk, Bkrs], w=[Bssqk])
                        P.op("act", "activation", out=RSK[:, gb, :], in_=ssqk[:, b, :], func=AF.Sqrt,
                             r=[Bssqk], w=[B_rsk])
                        P.op("dve", "reciprocal", RSK[:, gb, :], RSK[:, gb, :], r=[B_rsk], w=[B_rsk])
                    if P1_STOP == 34:
                        continue
                    P.dma("sp", VV[s, t0:t0 + SBT, :].rearrange("(b p) f -> p b f", p=128), Vst, r=[BVst], wa=[Bscr])
                    if P1_STOP == 4:
                        continue
                    for h in range(H):
                        bank = 6 + (h % 2)
                        for kc in range(KVC):
                            P.op("pe", "matmul", ps[bank][:, 0:SBT], Wukv[:, kc, h * 256:h * 256 + 128], ckvn[:, kc, :],
                                 start=(kc == 0), stop=(kc == KVC - 1), r=[BW, Bckvn], w=[PB[bank]])
                        if h % 2 == 0:
                            P.op("act", "activation", out=KTst[:, h, :], in_=ps[bank][:, 0:SBT], func=AF.Copy, scale=gkn[:, 0:1],
                                 r=[PB[bank], B_const], w=[BKT])
                        else:
                            P.op("dve", "tensor_scalar", KTst[:, h, :], ps[bank][:, 0:SBT], gkn[:, 0:1], None, ALU.mult,
                                 r=[PB[bank], B_const], w=[BKT])
                    P.dma("sp", KT[s, :, :, t0:t0 + SBT].rearrange("h p t -> p h t"), KTst, r=[BKT], wa=[Bscr])

        tilesE = tiles_of(TE, 512)
        tilesU = tiles_of(TU, 512)
        blocksE = blocks_of(TE)

        def mixer(s):
            PC, PGC, QC, MAC = c.PC, c.PGC, c.QC, c.MAC
            A.seek(0)
            aT = A.alloc([PC, TE], BF16); BaT = P.buf("aT")
            A.check(33)
            A.seek(33)
            hT = A.alloc([KC, TU], BF16); BhT = P.buf("hT")
            A.check(99)
            A.seek(99)
            tmps = alloc_norm_tmp()
            for bi, (i0, r) in enumerate(blocks_of(TU)):
                norm_transpose(xe[s, i0:i0 + r, :], r, gmix, hT, i0, tmps[bi % 2], BhT)
            P.barrier(cells)
            if MIX_STOP == 1:
                P.drop_bufs()
                return
            A.seek(99)
            Wg = [A.alloc([KC, 512], BF16) for _ in range(2)]; BWg = P.bufs("Wg", 2)
            sg = [A.alloc([TE], BF16) for _ in range(2)]; Bsg = P.bufs("sg", 2)
            banks = Rot([0, 1, 2, 3, 4, 5])
            cnt = 0
            for gi, off in enumerate((c.OFF_GP, c.OFF_GM)):
                for pi in range(D // 512):
                    W = Wg[cnt % 2]; BW_ = BWg[cnt % 2]; cnt += 1
                    wdma(W, w_in, off + pi * 512, 512, BW_)
                    for jj in range(4):
                        j = pi * 4 + jj
                        sgt = sg[j % 2]; Bs_ = Bsg[j % 2]
                        for (q0, n) in tilesE:
                            bank = banks.next()
                            for kc in range(KC):
                                P.op("pe", "matmul", ps[bank][:, 0:n], W[:, kc, jj * 128:(jj + 1) * 128], hT[:, kc, 8 + q0:8 + q0 + n],
                                     start=(kc == 0), stop=(kc == KC - 1), r=[BW_, BhT], w=[PB[bank]])
                            P.op("act", "activation", out=sgt[:, q0:q0 + n], in_=ps[bank][:, 0:n], func=AF.Sigmoid, r=[PB[bank]], w=[Bs_])
                        P.dma("sp", GATE[s, gi, j], sgt, r=[Bs_], wa=[B_gate[s]])
            P.barrier(cells)
            if MIX_STOP == 2:
                P.drop_bufs()
                return
            A.seek(99)
            Wp = [A.alloc([KC, 256], BF16) for _ in range(2)]; BWp = P.bufs("Wp", 2)
            usb = [A.alloc([TU], F32) for _ in range(2)]; Bu = P.bufs("usb", 2)
            ta = A.alloc([TU], F32); tb = A.alloc([TU], F32); tc = A.alloc([TU], F32); Bta = P.buf("ta"); Btb = P.buf("tb"); Btc = P.buf("tc")
            pooled = A.alloc([PGC, TE], BF16); Bpl = P.buf("pooled")
            plw = [A.alloc([PGC, c.PG], BF16) for _ in range(2)]; Bplw = P.bufs("plw", 2)
            icn = [A.alloc([TE], F32) for _ in range(2)]; Bicn = P.bufs("icn", 2)
            banks = Rot([0, 1, 2, 3, 4, 5])
            cnt = 0
            for g in range(4):
                w = WINDOWS[g]
                wdma(plw[g % 2], pool_w, 0, c.PG, Bplw[g % 2], row0=g * c.PG, nrows=c.PG)
                P.dma("sp", icn[g % 2], invcnt[:, g, :], w=[Bicn[g % 2]])
                for cc in range(PGC):
                    ch = g * PGC + cc
                    if ch % 2 == 0:
                        W = Wp[cnt % 2]; BW_ = BWp[cnt % 2]; cnt += 1
                        ncol = min(256, c.PW - ch * 128)
                        wdma(W[:, :, 0:ncol], w_in, ch * 128, ncol, BW_)
                    u = usb[ch % 2]; Bu_ = Bu[ch % 2]
                    for (u0, n) in tilesU:
                        bank = banks.next()
                        for kc in range(KC):
                            P.op("pe", "matmul", ps[bank][:, 0:n], W[:, kc, (ch % 2) * 128:(ch % 2) * 128 + 128], hT[:, kc, u0:u0 + n],
                                 start=(kc == 0), stop=(kc == KC - 1), r=[BW_, BhT], w=[PB[bank]])
                        P.op("act", "activation", out=u[:, u0:u0 + n], in_=ps[bank][:, 0:n], func=AF.Copy, r=[PB[bank]], w=[Bu_])
                    if w == 2:
                        P.op("dve", "tensor_tensor", tc[:, 8:8 + TE], u[:, 7:7 + TE], u[:, 8:8 + TE], ALU.add, r=[Bu_], w=[Btc])
                    else:
                        P.op("dve", "tensor_tensor", ta[:, 0:TU - 1], u[:, 0:TU - 1], u[:, 1:TU], ALU.add, r=[Bu_], w=[Bta])
                        if w == 4:
                            P.op("dve", "tensor_tensor", tc[:, 8:8 + TE], ta[:, 6:6 + TE], ta[:, 8:8 + TE], ALU.add, r=[Bta], w=[Btc])
                        else:
                            P.op("dve", "tensor_tensor", tb[:, 0:TU - 3], ta[:, 0:TU - 3], ta[:, 2:TU - 1], ALU.add, r=[Bta], w=[Btb])
                            if w == 8:
                                P.op("dve", "tensor_tensor", tc[:, 8:8 + TE], tb[:, 4:4 + TE], tb[:, 8:8 + TE], ALU.add, r=[Btb], w=[Btc])
                            else:
                                P.op("dve", "tensor_tensor", ta[:, 0:TU - 7], tb[:, 0:TU - 7], tb[:, 4:TU - 3], ALU.add, r=[Btb], w=[Bta])
                                P.op("dve", "tensor_tensor", tc[:, 8:8 + TE], ta[:, 0:TE], ta[:, 8:8 + TE], ALU.add, r=[Bta], w=[Btc])
                    P.op("dve", "tensor_tensor", tc[:, 8:8 + TE], tc[:, 8:8 + TE], icn[g % 2], ALU.mult, r=[Btc, Bicn[g % 2]], w=[Btc])
                    P.op("dve", "tensor_tensor", pooled[:, cc, :], tc[:, 8:8 + TE], u[:, 8:8 + TE], ALU.subtract, r=[Btc, Bu_], w=[Bpl])
                for oc in range(PGC):
                    ch = g * PGC + oc
                    for (q0, n) in tilesE:
                        bank = banks.next()
                        for kc in range(PGC):
                            P.op("pe", "matmul", ps[bank][:, 0:n], plw[g % 2][:, kc, oc * 128:(oc + 1) * 128], pooled[:, kc, q0:q0 + n],
                                 start=(kc == 0), stop=(kc == PGC - 1), r=[Bplw[g % 2], Bpl], w=[PB[bank]])
                        P.op("act", "activation", out=aT[:, ch, q0:q0 + n], in_=ps[bank][:, 0:n], func=AF.Copy, scale=psc[:, ch:ch + 1],
                             r=[PB[bank], B_const], w=[BaT])
            P.barrier(cells)
            if MIX_STOP == 3:
                P.drop_bufs()
                return
            A.seek(99)
            cqnT = A.alloc([QC, TE], BF16); Bcqn = P.buf("cqnT")
            Wcq = [A.alloc([KC, 128], BF16) for _ in range(2)]; BWcq = P.bufs("Wcq", 2)
            cqf = A.alloc([QC, TE], F32); Bcqf = P.bufs("cqf", QC)
            sq = [A.alloc([512], F32) for _ in range(2)]; Bsq = P.bufs("sqq", 2)
            rstd = A.alloc([TE], F32); Brs = P.buf("rstdq")
            banks = Rot([0, 1, 2, 3])
            sqi = 0
            for j in range(QC):
                W = Wcq[j % 2]; BW_ = BWcq[j % 2]
                wdma(W, w_in, c.OFF_CQ + j * 128, 128, BW_)
                for ti, (q0, n) in enumerate(tilesE):
                    bank = banks.next()
                    for kc in range(KC):
                        P.op("pe", "matmul", ps[bank][:, 0:n], W[:, kc, :], hT[:, kc, 8 + q0:8 + q0 + n],
                             start=(kc == 0), stop=(kc == KC - 1), r=[BW_, BhT], w=[PB[bank]])
                    P.op("act", "activation", out=cqf[:, j, q0:q0 + n], in_=ps[bank][:, 0:n], func=AF.Copy, r=[PB[bank]], w=[Bcqf[j]])
                    sqt = sq[sqi % 2]; Bsq_ = Bsq[sqi % 2]; sqi += 1
                    P.op("dve", "tensor_tensor", sqt[:, 0:n], ps[bank][:, 0:n], cqf[:, j, q0:q0 + n], ALU.mult, r=[PB[bank], Bcqf[j]], w=[Bsq_])
                    P.op("pe", "matmul", ps[5 + ti][:, 0:n], ones_f, sqt[:, 0:n], start=(j == 0), stop=(j == QC - 1),
                         r=[Bsq_, B_const], w=[PB[5 + ti]])
            for ti, (q0, n) in enumerate(tilesE):
                P.op("act", "activation", out=rstd[:, q0:q0 + n], in_=ps[5 + ti][:, 0:n], func=AF.Sqrt, bias=EPS, scale=1.0 / c.QL,
                     r=[PB[5 + ti]], w=[Brs])
            P.op("dve", "reciprocal", rstd, rstd, r=[Brs], w=[Brs])
            for j in range(QC):
                P.op("dve", "scalar_tensor_tensor", cqnT[:, j, :], cqf[:, j, :], gqa[:, j:j + 1], rstd, ALU.mult, ALU.mult,
                     r=[Bcqf[j], Brs, B_const], w=[Bcqn])
            P.barrier(cells)
            if MIX_STOP == 4:
                P.drop_bufs()
                return
            A.seek(33)
            QTn = A.alloc([H, TE], BF16); QTr = A.alloc([H, TE], BF16); BQT = P.buf("QT")
            A.check(98)
            A.seek(99 + 2 * QC * TE / 1024 + 1)
            NP5 = 2 if H >= 4 else 1
            HH = H // NP5
            Wuq = A.alloc([QC, HH * QK], BF16); BWuq = P.buf("Wuq")
            gq = A.alloc([2 * QK], F32); csq = A.alloc([NBE, 2, 32], F32); Bc5 = P.buf("c5")
            P.dma("sp", gq, g_q_rep, w=[Bc5])
            for bi, (e0, r) in enumerate(blocksE):
                P.dma("sp", csq[0:r, bi, 0, :], cosq[e0:e0 + r, :], w=[Bc5])
                P.dma("sp", csq[0:r, bi, 1, :], sinq[e0:e0 + r, :], w=[Bc5])
            qsbf = A.alloc([HH * QK], F32); qsb = qsbf.rearrange("p (h d) -> p h d", h=HH); Bqsb = P.buf("qsb")
            ssqq = A.alloc([HH], F32); rsq = A.alloc([HH], F32); Bssq = P.buf("ssqq"); Brq = P.buf("rsq")
            qn = A.alloc([HH, NOPE], BF16); Bqn = P.buf("qn")
            rtq = A.alloc([4, HH, 32], F32); Brtq = P.buf("rtq")
            qrf = A.alloc([HH, ROPE], F32); Bqrf = P.buf("qrf")
            qr = A.alloc([HH, ROPE], BF16); Bqr = P.buf("qr")
            junk3s = [A.alloc([QK], BF16) for _ in range(4)]; Bj3s = P.bufs("junk3", 4); jr3 = Rot([0, 1, 2, 3])
            banks = Rot([0, 1, 2, 3])
            tb_ = Rot([4, 5, 6, 7])
            for hf in range(NP5):
                hb = hf * HH
                for h0 in range(0, HH * QK, 768):
                    n = min(768, HH * QK - h0)
                    wdma(Wuq[:, :, h0:h0 + n], w_uq, hb * QK + h0, n, BWuq)
                for bi, (e0, r) in enumerate(blocksE):
                    for hp in range(HH // 2):
                        bank = banks.next()
                        for kc in range(QC):
                            P.op("pe", "matmul", ps[bank][0:r, 0:2 * QK], cqnT[:, kc, e0:e0 + r], Wuq[:, kc, hp * 2 * QK:(hp + 1) * 2 * QK],
                                 start=(kc == 0), stop=(kc == QC - 1), r=[Bcqn, BWuq], w=[PB[bank]])
                        for hh in range(2):
                            ji = jr3.next()
                            P.op("act", "activation", out=junk3s[ji][0:r, :], in_=ps[bank][0:r, hh * QK:(hh + 1) * QK], func=AF.Square,
                                 accum_out=ssqq[0:r, 2 * hp + hh:2 * hp + hh + 1], r=[PB[bank]], w=[Bj3s[ji], Bssq])
                        P.op("dve", "tensor_tensor", qsbf[0:r, hp * 2 * QK:(hp + 1) * 2 * QK], ps[bank][0:r, 0:2 * QK], gq[0:r, :],
                             ALU.mult, r=[PB[bank], Bc5, Bssq], w=[Bqsb])
                    P.op("act", "activation", out=rsq[0:r, :], in_=ssqq[0:r, :], func=AF.Sqrt, bias=EPS, scale=1.0 / QK, r=[Bssq], w=[Brq])
                    P.op("dve", "reciprocal", rsq[0:r, :], rsq[0:r, :], r=[Brq], w=[Brq])
                    P.op("dve", "tensor_tensor", qn[0:r], qsb[0:r, :, 0:NOPE], rsq[0:r, :].unsqueeze(2).broadcast_to([r, HH, NOPE]), ALU.mult,
                         r=[Bqsb, Brq], w=[Bqn])
                    x1 = qsb[0:r, :, NOPE:NOPE + 32]; x2 = qsb[0:r, :, NOPE + 32:QK]
                    cb_ = csq[0:r, bi, 0, :].unsqueeze(1).broadcast_to([r, HH, 32])
                    sb_ = csq[0:r, bi, 1, :].unsqueeze(1).broadcast_to([r, HH, 32])
                    P.op("dve", "tensor_tensor", rtq[0:r, 0], x1, cb_, ALU.mult, r=[Bqsb, Bc5], w=[Brtq])
                    P.op("dve", "tensor_tensor", rtq[0:r, 1], x2, sb_, ALU.mult, r=[Bqsb, Bc5], w=[Brtq])
                    P.op("dve", "tensor_tensor", rtq[0:r, 2], x2, cb_, ALU.mult, r=[Bqsb, Bc5], w=[Brtq])
                    P.op("dve", "tensor_tensor", rtq[0:r, 3], x1, sb_, ALU.mult, r=[Bqsb, Bc5], w=[Brtq])
                    P.op("dve", "tensor_tensor", qrf[0:r, :, 0:32], rtq[0:r, 0], rtq[0:r, 1], ALU.subtract, r=[Brtq], w=[Bqrf])
                    P.op("dve", "tensor_tensor", qrf[0:r, :, 32:64], rtq[0:r, 2], rtq[0:r, 3], ALU.add, r=[Brtq], w=[Bqrf])
                    P.op("dve", "tensor_tensor", qr[0:r], qrf[0:r], rsq[0:r, :].unsqueeze(2).broadcast_to([r, HH, ROPE]), ALU.mult,
                         r=[Bqrf, Brq], w=[Bqr])
                    for h0 in range(0, HH, 4):
                        nh = min(4, HH - h0)
                        bank = tb_.next()
                        for i in range(nh):
                            P.op("pe", "transpose", psb[bank][:, i * 128:i * 128 + r], qn[0:r, h0 + i, :], ident[0:r, 0:r], r=[Bqn, B_const], w=[PB[bank]])
                        P.op("act", "activation", out=QTn[:, hb + h0:hb + h0 + nh, e0:e0 + r],
                             in_=psb[bank][:, 0:nh * 128].rearrange("p (h t) -> p h t", h=nh)[:, :, 0:r], func=AF.Copy, r=[PB[bank]], w=[BQT])
                    for h0 in range(0, HH, 8):
                        nh = min(8, HH - h0)
                        bank = tb_.next()
                        for i in range(nh):
                            P.op("pe", "transpose", psb[bank][0:ROPE, i * 128:i * 128 + r], qr[0:r, h0 + i, :], ident[0:r, 0:r], r=[Bqr, B_const], w=[PB[bank]])
                        P.op("dve", "tensor_copy", QTr[0:ROPE, hb + h0:hb + h0 + nh, e0:e0 + r],
                             psb[bank][0:ROPE, 0:nh * 128].rearrange("p (h t) -> p h t", h=nh)[:, :, 0:r], r=[PB[bank]], w=[BQT])
            P.barrier(cells)
            if MIX_STOP == 5:
                P.drop_bufs()
                return
            A.seek(98)
            bT = A.alloc([H, TE], BF16); BbT = P.buf("bT")
            A.check(131)
            KCH = c.KCH
            NKB = KCH // 128
            Kn = [A.alloc([KCH], BF16) for _ in range(2)]; BKn = P.bufs("Kn", 2)
            Vh = [A.alloc([NKB, VD], BF16) for _ in range(2)]; BVh = P.bufs("Vh", 2)
            Kr = A.alloc([S], BF16); BKr = P.buf("Kr")
            PT = [A.alloc([512], BF16) for _ in range(4)]; BPT = P.bufs("PT", 4)
            rc = A.alloc([512], F32); Brc = P.buf("rc")
            P.dma("sp", Kr[0:ROPE, :], KRT[s], r=[B_kv[s]], w=[BKr])
            NQT = len(tilesE)
            assert NQT <= 3
            ci = 0
            pti = 0
            sbk = Rot([0, 1])
            iters = []
            for h in range(H):
                for k0 in range(0, S, KCH):
                    for kb in range(NKB):
                        for qi, (q0, n) in enumerate(tilesE):
                            iters.append((h, k0, kb, qi, q0, n))
            loaded = {}

            def emit_st(it):
                nonlocal ci
                h, k0, kb, qi, q0, n = it
                if (h, k0) not in loaded:
                    Kn_, BKn_ = Kn[ci % 2], BKn[ci % 2]
                    Vh_, BVh_ = Vh[ci % 2], BVh[ci % 2]
                    ci += 1
                    P.dma("sp", Kn_, KT[s, h, :, k0:k0 + KCH], r=[B_kv[s]], w=[BKn_])
                    P.dma("sp", Vh_, VV[s, k0:k0 + KCH, h * VD:(h + 1) * VD].rearrange("(b p) d -> p b d", p=128), r=[B_kv[s]], w=[BVh_])
                    loaded[(h, k0)] = (Kn_, BKn_, Vh_, BVh_)
                Kn_, BKn_, Vh_, BVh_ = loaded[(h, k0)]
                kg = (k0 // 128) + kb
                sbank = sbk.next()
                P.op("pe", "matmul", ps[sbank][:, 0:n], Kn_[:, kb * 128:(kb + 1) * 128], QTn[:, h, q0:q0 + n], start=True, stop=False,
                     r=[BKn_, BQT], w=[PB[sbank]])
                P.op("pe", "matmul", ps[sbank][:, 0:n], Kr[0:ROPE, kg * 128:(kg + 1) * 128], QTr[0:ROPE, h, q0:q0 + n], start=False, stop=True,
                     r=[BKr, BQT], w=[PB[sbank]])
                return sbank

            pending = emit_st(iters[0])
            for idx, it in enumerate(iters):
                h, k0, kb, qi, q0, n = it
                sbank = pending
                if idx + 1 < len(iters):
                    pending = emit_st(iters[idx + 1])
                Kn_, BKn_, Vh_, BVh_ = loaded[(h, k0)]
                kg = (k0 // 128) + kb
                first = (kg == 0)
                last = (kg == S // 128 - 1)
                pt = PT[pti % 4]; Bpt = BPT[pti % 4]; pti += 1
                P.op("act", "activation", out=pt[:, 0:n], in_=ps[sbank][:, 0:n], func=AF.Exp,
                     scale=RSK[:, s * (S // 128) + kg, h:h + 1], r=[PB[sbank], B_rsk], w=[Bpt])
                P.op("pe", "matmul", ps[2 + 2 * qi][:, 0:n], Vh_[:, kb, :], pt[:, 0:n], start=first, stop=last, r=[BVh_, Bpt], w=[PB[2 + 2 * qi]])
                P.op("pe", "matmul", ps[3 + 2 * qi][:, 0:n], ones_b, pt[:, 0:n], start=first, stop=last, r=[B_const, Bpt], w=[PB[3 + 2 * qi]])
                if last and qi == NQT - 1:
                    for qj, (p0, m_) in enumerate(tilesE):
                        P.op("dve", "reciprocal", rc[:, 0:m_], ps[3 + 2 * qj][:, 0:m_], r=[PB[3 + 2 * qj]], w=[Brc])
                        P.op("dve", "tensor_tensor", bT[:, h, p0:p0 + m_], ps[2 + 2 * qj][:, 0:m_], rc[:, 0:m_], ALU.mult, r=[PB[2 + 2 * qj], Brc], w=[BbT])
            P.barrier(cells)
            if MIX_STOP == 6:
                P.drop_bufs()
                return
            A.seek(33)
            mT = A.alloc([KC, TE], BF16); BmT = P.buf("mT")
            A.check(98)
            A.seek(131)
            Wbp = [A.alloc([PC, 256], BF16) for _ in range(2)]; BWbp = P.bufs("Wbp", 2)
            Wbm = [A.alloc([MAC, 256], BF16) for _ in range(2)]; BWbm = P.bufs("Wbm", 2)
            sgp = [A.alloc([TE], BF16) for _ in range(2)]; sgm = [A.alloc([TE], BF16) for _ in range(2)]
            Bsgp = P.bufs("sgp", 2); Bsgm = P.bufs("sgm", 2)
            t1 = [A.alloc([512], F32) for _ in range(2)]; t2 = [A.alloc([512], F32) for _ in range(2)]
            Bt1 = P.bufs("t1", 2); Bt2 = P.bufs("t2", 2)
            pr = Rot([0, 1, 2, 3])
            ti_ = 0
            for j in range(KC):
                if j % 2 == 0:
                    k2 = (j // 2) % 2
                    ncol = min(256, D - j * 128)
                    wdma(Wbp[k2][:, :, 0:ncol], w_bp, j * 128, ncol, BWbp[k2])
                    wdma(Wbm[k2][:, :, 0:ncol], w_bm, j * 128, ncol, BWbm[k2])
                P.dma("sp", sgp[j % 2], GATE[s, 0, j], r=[B_gate[s]], w=[Bsgp[j % 2]])
                P.dma("sp", sgm[j % 2], GATE[s, 1, j], r=[B_gate[s]], w=[Bsgm[j % 2]])
                jc = (j % 2) * 128
                for (q0, n) in tilesE:
                    pi_ = pr.next()
                    b1, b2 = 2 * pi_, 2 * pi_ + 1
                    for kc in range(PC):
                        P.op("pe", "matmul", ps[b1][:, 0:n], Wbp[k2][:, kc, jc:jc + 128], aT[:, kc, q0:q0 + n], start=(kc == 0), stop=(kc == PC - 1),
                             r=[BWbp[k2], BaT], w=[PB[b1]])
                    for kc in range(MAC):
                        P.op("pe", "matmul", ps[b2][:, 0:n], Wbm[k2][:, kc, jc:jc + 128], bT[:, kc, q0:q0 + n], start=(kc == 0), stop=(kc == MAC - 1),
                             r=[BWbm[k2], BbT], w=[PB[b2]])
                    a1 = t1[ti_ % 2]; a2 = t2[ti_ % 2]; Ba1 = Bt1[ti_ % 2]; Ba2 = Bt2[ti_ % 2]; ti_ += 1
                    P.op("dve", "tensor_tensor", a1[:, 0:n], ps[b1][:, 0:n], sgp[j % 2][:, q0:q0 + n], ALU.mult, r=[PB[b1], Bsgp[j % 2]], w=[Ba1])
                    P.op("dve", "tensor_tensor", a2[:, 0:n], ps[b2][:, 0:n], sgm[j % 2][:, q0:q0 + n], ALU.mult, r=[PB[b2], Bsgm[j % 2]], w=[Ba2])
                    P.op("pool", "tensor_tensor", mT[:, j, q0:q0 + n], a1[:, 0:n], a2[:, 0:n], ALU.add, r=[Ba1, Ba2], w=[BmT])
            P.barrier(cells)
            if MIX_STOP == 7:
                P.drop_bufs()
                return
            A.seek(98)
            Wo = [A.alloc([KC, 512], BF16) for _ in range(2)]; BWo = P.bufs("Wo", 2)
            xs = [A.alloc([512], F32) for _ in range(3)]; Bxs = P.bufs("xs", 3)
            ost = [A.alloc([512], F32) for _ in range(3)]; Bost = P.bufs("ost", 3)
            banks = Rot(list(range(8)))
            xi = 0
            for n_ in range(D // 512):
                W = Wo[n_ % 2]; BW_ = BWo[n_ % 2]
                wdma(W, w_o, n_ * 512, 512, BW_)
                for (e0, r) in blocksE:
                    bank = banks.next()
                    x_ = xs[xi % 3]; Bx_ = Bxs[xi % 3]; o_ = ost[xi % 3]; Bo_ = Bost[xi % 3]; xi += 1
                    P.dma("sp", x_[0:r, :], xe[s, 8 + e0:8 + e0 + r, n_ * 512:(n_ + 1) * 512], w=[Bx_])
                    for kc in range(KC):
                        P.op("pe", "matmul", ps[bank][0:r, :], mT[:, kc, e0:e0 + r], W[:, kc, :], start=(kc == 0), stop=(kc == KC - 1),
                             r=[BmT, BW_], w=[PB[bank]])
                    P.op("dve", "tensor_tensor", o_[0:r, :], ps[bank][0:r, :], x_[0:r, :], ALU.add, r=[PB[bank], Bx_], w=[Bo_])
                    lo = (128 - TE % 128) if (e0 % 128 != 0) else 0
                    P.dma("sp", X1[s, e0 + lo:e0 + r, n_ * 512:(n_ + 1) * 512], o_[lo:r, :], r=[Bo_], wa=[B_x1[s]])
            P.barrier(cells)
            P.drop_bufs()

        def ffn(s, a0):
            TF, FC = c.TF, c.FC
            HF = TF // 2
            N = HF + 2
            A.seek(0)
            gT = A.alloc([FC, TF], BF16); BgT = P.buf("gT")
            mark0 = A.off
            h2T = A.alloc([KC, TF + 2], BF16); Bh2 = P.buf("h2T")
            mark = A.off / 1024
            tmps = alloc_norm_tmp()
            for bi, (i0, r) in enumerate(blocks_of(TF + 2)):
                norm_transpose(X1[s, a0 + i0:a0 + i0 + r, :], r, gffn, h2T, i0, tmps[bi % 2], Bh2, rsrc=[B_x1[s]],
                               mask_ap=valid[a0 + i0:a0 + i0 + r, :])
            P.barrier(cells)
            if FFN_STOP == 1:
                P.drop_bufs()
                return
            A.seek(mark)
            Wg = [A.alloc([KC, 256], BF16) for _ in range(2)]; Wv = [A.alloc([KC, 256], BF16) for _ in range(2)]
            BWg = P.bufs("Wug", 2); BWv = P.bufs("Wuv", 2)
            tg = [A.alloc([HF], F32) for _ in range(2)]; tv = [A.alloc([HF], F32) for _ in range(2)]; sg = [A.alloc([HF], F32) for _ in range(2)]
            Btg = P.bufs("tg", 2); Btv = P.bufs("tv", 2); Bsg = P.bufs("sgl", 2)
            pr = Rot([0, 1, 2, 3])
            ei = 0
            for f in range(FC):
                if f % 2 == 0:
                    k2 = (f // 2) % 2
                    ncol = min(256, c.DFF - f * 128)
                    P.dma("pool", Wg[k2][:, :, 0:ncol], WUP2[f // 2, 0].rearrange("p (k c) -> p k c", k=KC)[:, :, 0:ncol], r=[B_wpre], w=[BWg[k2]])
                    P.dma("pool", Wv[k2][:, :, 0:ncol], WUP2[f // 2, 1].rearrange("p (k c) -> p k c", k=KC)[:, :, 0:ncol], r=[B_wpre], w=[BWv[k2]])
                fc0 = (f % 2) * 128
                for half in range(2):
                    c0 = half * HF
                    pi_ = pr.next()
                    bg, bv = 2 * pi_, 2 * pi_ + 1
                    for kc in range(KC):
                        P.op("pe", "matmul", ps[bg][:, 0:N], Wg[k2][:, kc, fc0:fc0 + 128], h2T[:, kc, c0:c0 + N], start=(kc == 0), stop=(kc == KC - 1),
                             r=[BWg[k2], Bh2], w=[PB[bg]])
                    for kc in range(KC):
                        P.op("pe", "matmul", ps[bv][:, 0:N], Wv[k2][:, kc, fc0:fc0 + 128], h2T[:, kc, c0:c0 + N], start=(kc == 0), stop=(kc == KC - 1),
                             r=[BWv[k2], Bh2], w=[PB[bv]])
                    tg_, tv_, sg_ = tg[ei % 2], tv[ei % 2], sg[ei % 2]
                    Btg_, Btv_, Bsg_ = Btg[ei % 2], Btv[ei % 2], Bsg[ei % 2]
                    ei += 1
                    for (bank, t_, Bt_, ch) in ((bg, tg_, Btg_, f), (bv, tv_, Btv_, FC + f)):
                        P.op("act", "activation", out=t_, in_=ps[bank][:, 1:1 + HF], func=AF.Identity, bias=cb[:, ch:ch + 1], scale=cw[:, 1, ch:ch + 1],
                             r=[PB[bank], B_const], w=[Bt_])
                        P.op("dve", "scalar_tensor_tensor", t_, ps[bank][:, 0:HF], cw[:, 0, ch:ch + 1], t_, ALU.mult, ALU.add,
                             r=[PB[bank], B_const, Bt_], w=[Bt_])
                        P.op("dve", "scalar_tensor_tensor", t_, ps[bank][:, 2:2 + HF], cw[:, 2, ch:ch + 1], t_, ALU.mult, ALU.add,
                             r=[PB[bank], B_const, Bt_], w=[Bt_])
                    P.op("act", "activation", out=sg_, in_=tg_, func=AF.Silu, r=[Btg_], w=[Bsg_])
                    P.op("dve", "tensor_tensor", gT[:, f, c0:c0 + HF], sg_, tv_, ALU.mult, r=[Bsg_, Btv_], w=[BgT])
            P.barrier(cells)
            if FFN_STOP == 2:
                P.drop_bufs()
                return
            A.off = mark0
            groups = FGROUPS
            Wd = [A.alloc([KG, 512], BF16) for _ in range(2)]; BWd = P.bufs("Wd", 2)
            xs = [A.alloc([512], F32) for _ in range(4)]; Bxs = P.bufs("xs2", 4)
            ost = [A.alloc([512], F32) for _ in range(4)]; Bost = P.bufs("ost2", 4)
            NBk = TF // 128
            wi = 0
            xi = 0
            for n_ in range(D // 512):
                for gi, (f0, nf) in enumerate(groups):
                    W = Wd[wi % 2]; BW_ = BWd[wi % 2]; wi += 1
                    P.dma("pool", W[:, 0:nf, :], WD2[n_, gi].rearrange("p (k c) -> p k c", k=KG)[:, 0:nf, :], r=[B_wpre], w=[BW_])
                    for b in range(NBk):
                        bank = b + NBk * (n_ % (8 // NBk))
                        for fi in range(nf):
                            f = f0 + fi
                            P.op("pe", "matmul", ps[bank][:, :], gT[:, f, b * 128:(b + 1) * 128], W[:, fi, :], start=(f == 0), stop=(f == FC - 1),
                                 r=[BgT, BW_], w=[PB[bank]])
                for b in range(NBk):
                    bank = b + NBk * (n_ % (8 // NBk))
                    x_ = xs[xi % 4]; Bx_ = Bxs[xi % 4]; o_ = ost[xi % 4]; Bo_ = Bost[xi % 4]; xi += 1
                    if FFN_STOP == 3:
                        continue
                    P.dma("sp", x_, X1[s, a0 + 1 + b * 128:a0 + 1 + (b + 1) * 128, n_ * 512:(n_ + 1) * 512], r=[B_x1[s]], w=[Bx_])
                    P.op("dve", "tensor_tensor", o_, ps[bank][:, :], x_, ALU.add, r=[PB[bank], Bx_], w=[Bo_])
                    if FFN_STOP == 4:
                        continue
                    ydst = X1 if FFN_STOP == 5 else y
                    P.dma("sp", ydst[s, a0 + b * 128:a0 + (b + 1) * 128, n_ * 512:(n_ + 1) * 512], o_, r=[Bo_], wa=[B_y])
            P.barrier(cells)
            P.drop_bufs()

        if "p1" in stages:
            phase1()
            P.barrier(cells)
            P.drop_bufs()
        for s in range(NSEQ):
            if "mix" in stages:
                mixer(s)
            if "ffn" in stages:
                for a0 in range(0, TQ, c.TF):
                    ffn(s, a0)
        last = P._add("sp", "dma_start", (), dict(out=cells["sp_dst"], in_=cells["sp_src"]), [],
                      P.phase_bufs + B_kv + B_gate + B_x1 + [B_y, B_const, B_rsk, B_wpre, P.cell_buf], True)
        P.assign()
        n_ops = {e: len(v) for e, v in P.ops.items()}
        print("[build] ops per engine:", n_ops, "sems left", len(P.free_sems))

        @block.tensor
        def _(e):
            P.emit("pe", e)

        @block.scalar
        def _(e):
            P.emit("act", e)

        @block.vector
        def _(e):
            P.emit("dve", e)

        @block.gpsimd
        def _(e):
            P.emit("pool", e)

        @block.sync
        def _(e):
            P.emit("sp", e)
            e.wait_ge(last.sem, last.val)
    return nc


def rope_tables_np(n_pos):
    inv = (1.0 / (np.float32(10000.0) ** (np.arange(0, ROPE, 2, dtype=np.float32) / np.float32(ROPE)))).astype(np.float32)
    ang = (np.arange(n_pos, dtype=np.float32)[:, None] * inv[None, :]).astype(np.float32)
    return np.cos(ang).astype(np.float32), np.sin(ang).astype(np.float32)


def host_inputs(cfg, inp):
    c = cfg
    D, S, H = c.D, c.S, c.H
    f32 = np.float32
    xs = np.concatenate([np.asarray(inp["x_prompt"], f32), np.asarray(inp["x_sample"], f32)], 0)
    xall = np.ascontiguousarray(xs.reshape(NSEQ * S, D))

    def col(v, n):
        return np.ascontiguousarray(np.asarray(v, f32).reshape(n, 128).T)

    cosk, sink = rope_tables_np(S)
    qg = np.asarray(inp["q_norm_gain"], f32).reshape(QK)
    kg = np.asarray(inp["k_norm_gain"], f32).reshape(QK)
    cw = np.asarray(inp["conv_w"], f32).reshape(3, 2 * c.DFF)
    common = {
        "xall": xall, "cosk": cosk, "sink": sink,
        "g_mix": col(inp["norm_mix_gain"], c.KC), "g_ffn": col(inp["norm_ffn_gain"], c.KC),
        "g_qa": col(inp["q_a_norm_gain"], c.QC), "g_kva": col(inp["kv_a_norm_gain"], c.KVC),
        "g_q_rep": np.ascontiguousarray(np.broadcast_to(np.concatenate([qg, qg])[None, :], (128, 2 * QK))),
        "g_kr_rep": np.ascontiguousarray(np.broadcast_to(kg[None, NOPE:], (128, ROPE))),
        "g_kn": np.ascontiguousarray(kg[:NOPE].reshape(128, 1)),
        "pscale": col(inp["pool_scale"], c.PC),
        "convw": np.ascontiguousarray(cw.reshape(3, 2 * c.FC, 128).transpose(2, 0, 1)),
        "convb": col(inp["conv_b"], 2 * c.FC),
        "ident": np.eye(128, dtype=f32),
        "w_in": np.asarray(inp["w_in"], f32).reshape(D, c.IN_COLS),
        "pool_w": np.asarray(inp["pool_w"], f32).reshape(c.PW, c.PG),
        "w_uq": np.asarray(inp["w_uq"], f32).reshape(c.QL, H * QK),
        "w_ukv": np.asarray(inp["w_ukv"], f32).reshape(c.KVL, H * 256),
        "w_bp": np.asarray(inp["w_branch_pool"], f32).reshape(c.PW, D),
        "w_bm": np.asarray(inp["w_branch_mla"], f32).reshape(c.MA, D),
        "w_o": np.asarray(inp["w_o"], f32).reshape(D, D),
        "w_up": np.asarray(inp["w_up"], f32).reshape(D, 2 * c.DFF),
        "w_down": np.asarray(inp["w_down"], f32).reshape(c.DFF, D),
    }
    maps = []
    for core in range(c.NCORE):
        start = core * c.TQ
        lo = start - HALO
        xe_ = np.zeros((NSEQ, c.TU, D), f32)
        a, b = max(lo, 0), min(lo + c.TU, S)
        xe_[:, a - lo:b - lo, :] = xs[:, a:b, :]
        pos = np.arange(c.TE) + start - 1
        ok = (pos >= 0) & (pos < S)
        pc = np.clip(pos, 0, S - 1)
        inv = np.zeros((4, c.TE), f32)
        for gi, w in enumerate(WINDOWS):
            lo_w = np.clip(pos - w // 2, 0, S)
            hi_w = np.clip(pos + (w - w // 2), 0, S)
            cnt = np.maximum(hi_w - lo_w, 1)
            inv[gi] = (1.0 / cnt.astype(f32)).astype(f32)
        m = dict(common)
        m.update({
            "xe": xe_, "cosq": np.ascontiguousarray(cosk[pc]), "sinq": np.ascontiguousarray(sink[pc]),
            "invcnt": np.ascontiguousarray(np.broadcast_to(inv[None], (128, 4, c.TE))),
            "valid": ok.astype(f32).reshape(c.TE, 1),
        })
        maps.append(m)
    return maps


_FULL = None


def run(cfg, inp, debug=False):
    nc = build(cfg, debug)
    maps = host_inputs(cfg, inp)
    res = run_bass_kernel_spmd(nc, maps, core_ids=list(range(cfg.NCORE)))
    return res


def kernel(**inputs):
    cfg = Cfg()
    res = run(cfg, inputs)
    ys = np.stack([r["yout"] for r in res.results], 0)
    full = ys.transpose(1, 0, 2, 3).reshape(NSEQ, cfg.S, cfg.D)
    return (np.ascontiguousarray(full[:2]), np.ascontiguousarray(full[2:3]))
```

```python
import math
from contextlib import ExitStack

import numpy as np
import concourse.bass as bass
import concourse.mybir as mybir
from concourse.bass_utils import run_bass_kernel_spmd

F32 = mybir.dt.float32
BF16 = mybir.dt.bfloat16
AF = mybir.ActivationFunctionType
ALU = mybir.AluOpType
EPS = 1e-6
NSEQ = 3
ROPE = 64
NOPE = 128
VD = 128
QK = 192
WINDOWS = (2, 4, 8, 16)
HALO = 9


class Cfg:
    def __init__(self, D=4096, S=8192, H=16, DFF=11008, NCORE=8):
        self.D, self.S, self.H, self.DFF, self.NCORE = D, S, H, DFF, NCORE
        self.KC = D // 128
        self.PW = D // 2
        self.PG = self.PW // 4
        self.PC = self.PW // 128
        self.PGC = self.PG // 128
        self.QL = D // 4
        self.QC = self.QL // 128
        self.KVL = D // 8
        self.KVC = self.KVL // 128
        self.FC = DFF // 128
        self.TQ = S // NCORE
        self.TU = self.TQ + 2 * HALO
        self.TE = self.TQ + 2
        self.OFF_CQ = self.PW
        self.OFF_CKV = self.OFF_CQ + self.QL
        self.OFF_KR = self.OFF_CKV + self.KVL
        self.OFF_GP = self.OFF_KR + ROPE
        self.OFF_GM = self.OFF_GP + D
        self.IN_COLS = self.OFF_GM + D
        self.SBT = min(256, S)
        self.TF = min(512, self.TQ)
        self.KCH = min(4096, S)
        self.MA = H * VD
        self.MAC = self.MA // 128


class Buf:
    __slots__ = ("name", "w", "r", "dw", "dr")

    def __init__(self, name):
        self.name = name
        self.w = {}
        self.r = {}
        self.dw = []
        self.dr = []


class Op:
    __slots__ = ("eng", "meth", "args", "kw", "dma", "deps", "inc", "sem", "val")

    def __init__(self, eng, meth, args, kw, dma):
        self.eng, self.meth, self.args, self.kw, self.dma = eng, meth, args, kw, dma
        self.deps = []
        self.inc = False
        self.sem = None
        self.val = 0


SEM_LIMIT = 30000
import os
P1_STOP = int(os.environ.get("P1_STOP", "0"))
MIX_STOP = int(os.environ.get("MIX_STOP", "0"))
FFN_STOP = int(os.environ.get("FFN_STOP", "0"))
RING = 12


class Prog:
    ENGS = ("pe", "act", "dve", "pool", "sp")

    def __init__(self, sems):
        self.ops = {e: [] for e in self.ENGS}
        self.order = []
        self.free_sems = list(sems)
        self.phase_bufs = []
        self.cell_buf = Buf("cell")

    def buf(self, name, persistent=False):
        b = Buf(name)
        if not persistent:
            self.phase_bufs.append(b)
        return b

    def bufs(self, name, n, persistent=False):
        return [self.buf(f"{name}{i}", persistent) for i in range(n)]

    def _add(self, eng, meth, args, kw, r, w, dma):
        op = Op(eng, meth, args, kw, dma)
        deps = op.deps
        for b in r:
            for e, p in b.w.items():
                if e == eng and not dma and eng == "pe":
                    continue
                deps.append(p)
            deps.extend(b.dw)
        for b in w:
            for e, p in b.r.items():
                if e == eng and not dma and eng == "pe":
                    continue
                deps.append(p)
            for e, p in b.w.items():
                if e == eng and not dma and eng == "pe":
                    continue
                deps.append(p)
            deps.extend(b.dr)
            deps.extend(b.dw)
        for b in r:
            if dma:
                b.dr.append(op)
            else:
                b.r[eng] = op
        for b in w:
            if dma:
                b.w = {}
                b.r = {}
                b.dr = []
                b.dw = [op]
            else:
                b.w = {eng: op}
                b.r = {}
                b.dr = []
                b.dw = []
        for p in deps:
            p.inc = True
        if dma:
            op.inc = True
        self.ops[eng].append(op)
        self.order.append(op)
        return op

    def op(self, eng, meth, *args, r=(), w=(), **kw):
        return self._add(eng, meth, args, kw, r, w, False)

    def dma(self, q, out, in_, r=(), w=(), wa=(), **kw):
        kw = dict(kw)
        kw["out"] = out
        kw["in_"] = in_
        op = self._add(q, "dma_start", (), kw, r, w, True)
        for b in wa:
            b.dw.append(op)
        return op

    def barrier(self, cells):
        bl = list(self.phase_bufs) + [self.cell_buf]
        self._add("act", "memzero", (cells["act"],), {}, [], bl, False)
        for eng in ("dve", "pool"):
            self._add(eng, "memset", (cells[eng], 0.0), {}, [], bl, False)
        self._add("sp", "dma_start", (), dict(out=cells["sp_dst"], in_=cells["sp_src"]), [], bl, True)

    def drop_bufs(self):
        self.phase_bufs = []

    def assign(self):
        cur = {}
        ring = {q: [dict(sem=None, val=0, last=None) for _ in range(RING)] for q in ("sp", "pool")}
        rcnt = {"sp": 0, "pool": 0}
        for op in self.order:
            if not op.inc:
                continue
            if op.dma:
                slots = ring[op.eng]
                sl = slots[rcnt[op.eng] % len(slots)]
                rcnt[op.eng] += 1
                if sl["last"] is not None:
                    op.deps.append(sl["last"])
                if sl["sem"] is None or sl["val"] + 16 > SEM_LIMIT:
                    sl["sem"] = self.free_sems.pop()
                    sl["val"] = 0
                sl["val"] += 16
                sl["last"] = op
                op.sem, op.val = sl["sem"], sl["val"]
            else:
                cc = cur.get(op.eng)
                if cc is None or cc[1] + 1 > SEM_LIMIT:
                    cc = [self.free_sems.pop(), 0]
                    cur[op.eng] = cc
                cc[1] += 1
                op.sem, op.val = cc[0], cc[1]

    def emit(self, eng_name, eng):
        known = {}
        for op in self.ops[eng_name]:
            need = {}
            for p in op.deps:
                k = id(p.sem)
                if known.get(k, 0) >= p.val:
                    continue
                if k not in need or need[k][1] < p.val:
                    need[k] = (p.sem, p.val)
            for k, (sem, val) in need.items():
                eng.wait_ge(sem, val)
                known[k] = val
            ins = getattr(eng, op.meth)(*op.args, **op.kw)
            if op.inc:
                ins.then_inc(op.sem, 16 if op.dma else 1)


class Arena:
    def __init__(self, ap_f32, nbytes):
        self.ap = ap_f32
        self.nbytes = nbytes
        self.off = 0

    def seek(self, kb):
        self.off = int(kb * 1024)

    def check(self, kb):
        assert self.off <= kb * 1024, f"arena region overflow: {self.off} > {kb}KB"

    def alloc(self, shape, dtype, parts=128):
        esz = 4 if dtype == F32 else 2
        n = int(np.prod(shape))
        nb = (n * esz + 3) // 4 * 4
        assert self.off + nb <= self.nbytes, f"arena overflow: {self.off}+{nb} > {self.nbytes}"
        a = self.ap[0:parts, self.off // 4:(self.off + nb) // 4]
        self.off += nb
        if dtype != F32:
            a = a.bitcast(dtype)
            if a.shape[1] != n:
                a = a[:, 0:n]
        if len(shape) == 2:
            a = a.rearrange("p (a b) -> p a b", a=shape[0])
        elif len(shape) == 3:
            a = a.rearrange("p (a b c) -> p a b c", a=shape[0], b=shape[1])
        return a


def tiles_of(n, t):
    return [(i, min(t, n - i)) for i in range(0, n, t)]


def blocks_of(n):
    bl = [(i * 128, 128) for i in range(n // 128)]
    if n % 128:
        bl.append((n - 128, 128))
    return bl


class Rot:
    def __init__(self, items):
        self.items = list(items)
        self.i = 0

    def next(self):
        v = self.items[self.i % len(self.items)]
        self.i += 1
        return v


def build(cfg, debug=False, stages=("p1", "mix", "ffn")):
    c = cfg
    D, S, H, KC = c.D, c.S, c.H, c.KC
    TE, TU, TQ = c.TE, c.TU, c.TQ
    nc = bass.Bass("TRN2", target_bir_lowering=False)

    def din(name, shape, dt=F32):
        return nc.dram_tensor(name, list(shape), dt, kind="ExternalInput").ap()

    def dscr(name, shape, dt):
        return nc.dram_tensor(name, list(shape), dt, kind="ExternalOutput" if debug else "Internal").ap()

    xall = din("xall", [NSEQ * S, D])
    xe = din("xe", [NSEQ, TU, D])
    cosk = din("cosk", [S, 32]); sink = din("sink", [S, 32])
    cosq = din("cosq", [TE, 32]); sinq = din("sinq", [TE, 32])
    invcnt = din("invcnt", [128, 4, TE])
    valid = din("valid", [TE, 1])
    g_mix = din("g_mix", [128, KC]); g_ffn = din("g_ffn", [128, KC])
    g_qa = din("g_qa", [128, c.QC]); g_kva = din("g_kva", [128, c.KVC])
    g_q_rep = din("g_q_rep", [128, 2 * QK]); g_kr_rep = din("g_kr_rep", [128, ROPE]); g_kn = din("g_kn", [128, 1])
    pscale = din("pscale", [128, c.PC])
    convw = din("convw", [128, 3, 2 * c.FC]); convb = din("convb", [128, 2 * c.FC])
    ident_in = din("ident", [128, 128])
    w_in = din("w_in", [D, c.IN_COLS])
    pool_w = din("pool_w", [c.PW, c.PG])
    w_uq = din("w_uq", [c.QL, H * QK])
    w_ukv = din("w_ukv", [c.KVL, H * 256])
    w_bp = din("w_bp", [c.PW, D]); w_bm = din("w_bm", [c.MA, D])
    w_o = din("w_o", [D, D])
    w_up = din("w_up", [D, 2 * c.DFF]); w_down = din("w_down", [c.DFF, D])
    y = nc.dram_tensor("yout", [NSEQ, TQ, D], F32, kind="ExternalOutput").ap()
    KT = dscr("KT", [NSEQ, H, 128, S], BF16)
    KRT = dscr("KRT", [NSEQ, ROPE, S], BF16)
    VV = dscr("VV", [NSEQ, S, H * VD], BF16)
    GATE = dscr("GATE", [NSEQ, 2, KC, 128, TE], BF16)
    X1 = dscr("X1", [NSEQ, TE, D], F32)
    dummy_d = nc.dram_tensor("dummy_d", [1, 16], F32, kind="Internal").ap()
    NPAN = (c.FC + 1) // 2
    KG = 22
    FGROUPS = tiles_of(c.FC, KG)
    WUP2 = nc.dram_tensor("WUP2", [NPAN, 2, 128, KC * 256], BF16, kind="Internal").ap()
    WD2 = nc.dram_tensor("WD2", [D // 512, len(FGROUPS), 128, KG * 512], BF16, kind="Internal").ap()

    NBLK_ALL = NSEQ * S // 128
    NBE = (TE + 127) // 128
    ARENA_BYTES = 190 * 1024
    pers_sizes = dict(ones_f=128, ident=64, ones_b=64, rsk=NBLK_ALL * H, cells=32, gmix=KC, gffn=KC, gqa=c.QC,
                      gkva=c.KVC, gkr=ROPE, gkn=1, pscale=c.PC, convw=6 * c.FC, convb=2 * c.FC)
    PERS_F32 = sum(pers_sizes.values()) + 16

    with ExitStack() as es:
        pers = es.enter_context(nc.sbuf_tensor("pers", [128, PERS_F32], F32))
        arena_t = es.enter_context(nc.sbuf_tensor("arena", [128, ARENA_BYTES // 4], F32))
        psum = [es.enter_context(nc.psum_tensor(f"ps{i}", [128, 512], F32)) for i in range(8)]
        sems = [es.enter_context(nc.semaphore(f"sm{i}")) for i in range(96)]
        block = es.enter_context(nc.Block())

        P = Prog(sems)
        A = Arena(arena_t, ARENA_BYTES)
        pv = {}
        o = 0
        for k, n in pers_sizes.items():
            pv[k] = pers[:, o:o + n]
            o += n
        ones_f = pv["ones_f"]
        ident = pv["ident"].bitcast(BF16)
        ones_b = pv["ones_b"].bitcast(BF16)
        RSK = pv["rsk"].rearrange("p (b h) -> p b h", h=H)
        cl = pv["cells"]
        cells = {"act": cl[:, 0:1], "dve": cl[:, 1:2], "pool": cl[:, 2:3], "sp_dst": cl[0:1, 8:24], "sp_src": ident_in[0:1, 0:16]}
        gmix, gffn, gqa, gkva = pv["gmix"], pv["gffn"], pv["gqa"], pv["gkva"]
        gkr, gkn, psc = pv["gkr"], pv["gkn"], pv["pscale"]
        cw = pv["convw"].rearrange("p (t j) -> p t j", t=3)
        cb = pv["convb"]
        ps = [p[:] for p in psum]
        psb = [p[:].bitcast(BF16) for p in psum]
        PB = P.bufs("psum", 8, persistent=True)
        B_const = P.buf("const", persistent=True)
        B_rsk = P.buf("rsk", persistent=True)
        B_y = P.buf("y", persistent=True)
        B_wpre = P.buf("wpre", persistent=True)
        B_kv = P.bufs("kvscr", NSEQ, persistent=True)
        B_gate = P.bufs("gate", NSEQ, persistent=True)
        B_x1 = P.bufs("x1", NSEQ, persistent=True)

        A.seek(189)
        t_id = A.alloc([128], F32)
        bt = P.buf("t_id")
        P.dma("sp", t_id, ident_in, w=[bt])
        P.op("dve", "tensor_copy", ident, t_id, r=[bt], w=[B_const])
        P.op("dve", "memset", ones_f, 1.0, w=[B_const])
        P.op("dve", "memset", ones_b, 1.0, w=[B_const])
        P.op("dve", "memset", cl, 0.0, w=[B_const, P.cell_buf])
        for dst, src in ((gmix, g_mix), (gffn, g_ffn), (gqa, g_qa), (gkva, g_kva), (gkr, g_kr_rep),
                         (gkn, g_kn), (psc, pscale), (cw, convw), (cb, convb)):
            P.dma("sp", dst, src, w=[B_const])
        P.barrier(cells)
        P.drop_bufs()

        def alloc_norm_tmp():
            tm = []
            for i in range(2):
                xn_ = A.alloc([D], BF16); Bn_ = P.buf("xn")
                tm.append(dict(xb=A.alloc([D], F32), junk=xn_, xn=xn_, st=A.alloc([4], F32),
                               Bx=P.buf("xb"), Bj=Bn_, Bn=Bn_, Bs=P.buf("st"), Bm=P.buf("msk")))
            return tm

        def norm_transpose(src, r, gain, hT, col0, t, BhT, rsrc=(), mask_ap=None):
            xb, junk, xn, st = t["xb"], t["junk"], t["xn"], t["st"]
            Bx, Bj, Bn, Bs, Bm = t["Bx"], t["Bj"], t["Bn"], t["Bs"], t["Bm"]
            P.dma("sp", xb[0:r, :], src, r=list(rsrc), w=[Bx])
            if mask_ap is not None:
                P.dma("sp", st[0:r, 3:4], mask_ap, w=[Bm])
            P.op("act", "activation", out=junk[0:r, :], in_=xb[0:r, :], func=AF.Square, accum_out=st[0:r, 0:1], r=[Bx], w=[Bj, Bs])
            P.op("act", "activation", out=st[0:r, 1:2], in_=st[0:r, 0:1], func=AF.Sqrt, bias=EPS, scale=1.0 / D, r=[Bs], w=[Bs])
            P.op("dve", "reciprocal", st[0:r, 2:3], st[0:r, 1:2], r=[Bs], w=[Bs])
            if mask_ap is not None:
                P.op("dve", "tensor_tensor", st[0:r, 2:3], st[0:r, 2:3], st[0:r, 3:4], ALU.mult, r=[Bs, Bm], w=[Bs])
            P.op("dve", "tensor_scalar", xn[0:r, :], xb[0:r, :], st[0:r, 2:3], None, ALU.mult, r=[Bx, Bs], w=[Bn])
            for g4 in range(0, KC, 4):
                bank = (g4 // 4) % 2
                n4 = min(4, KC - g4)
                for i in range(n4):
                    kc = g4 + i
                    P.op("pe", "transpose", psb[bank][:, i * 128:i * 128 + r], xn[0:r, kc * 128:(kc + 1) * 128], ident[0:r, 0:r],
                         r=[Bn, B_const], w=[PB[bank]])
                for i in range(n4):
                    kc = g4 + i
                    if kc % 2 == 0:
                        P.op("act", "activation", out=hT[:, kc, col0:col0 + r], in_=psb[bank][:, i * 128:i * 128 + r], func=AF.Copy,
                             scale=gain[:, kc:kc + 1], r=[PB[bank], B_const], w=[BhT])
                    else:
                        P.op("dve", "tensor_scalar", hT[:, kc, col0:col0 + r], psb[bank][:, i * 128:i * 128 + r], gain[:, kc:kc + 1],
                             None, ALU.mult, r=[PB[bank], B_const], w=[BhT])

        def wdma(dst, src_rows, col0, ncols, B, row0=0, nrows=None):
            nrows = src_rows.shape[0] - row0 if nrows is None else nrows
            src = src_rows[row0:row0 + nrows, col0:col0 + ncols].rearrange("(k p) c -> p k c", p=128)
            P.dma("pool", dst, src, w=[B], max_dma_last_dim=4096)

        def phase1():
            SBT = c.SBT
            NB = SBT // 128
            KVC = c.KVC
            A.seek(0)
            Wckv = A.alloc([KC, c.KVL], BF16); Wkr = A.alloc([KC, ROPE], BF16); Wukv = A.alloc([KVC, H * 256], BF16)
            BW = P.buf("p1w")
            wdma(Wckv, w_in, c.OFF_CKV, c.KVL, BW)
            wdma(Wkr, w_in, c.OFF_KR, ROPE, BW)
            for h0 in range(0, H * 256, 1024):
                n = min(1024, H * 256 - h0)
                wdma(Wukv[:, :, h0:h0 + n], w_ukv, h0, n, BW)
            for pan in range(NPAN):
                ncol = min(256, c.DFF - pan * 256)
                for gv in range(2):
                    dst = WUP2[pan, gv].rearrange("p (k c) -> p k c", k=KC)[:, :, 0:ncol]
                    src = w_up[:, gv * c.DFF + pan * 256:gv * c.DFF + pan * 256 + ncol].rearrange("(k p) c -> p k c", p=128)
                    P.dma("pool", dst, src, wa=[B_wpre], max_dma_last_dim=4096)
            for n_ in range(D // 512):
                for gi, (f0, nf) in enumerate(FGROUPS):
                    dst = WD2[n_, gi].rearrange("p (k c) -> p k c", k=KG)[:, 0:nf, :]
                    src = w_down[f0 * 128:(f0 + nf) * 128, n_ * 512:(n_ + 1) * 512].rearrange("(k p) c -> p k c", p=128)
                    P.dma("pool", dst, src, wa=[B_wpre], max_dma_last_dim=4096)
            tmps = alloc_norm_tmp()
            hT = [A.alloc([KC, SBT], BF16) for _ in range(2)]
            BhT = P.bufs("hT", 2)
            ckvf = A.alloc([KVC, SBT], F32); Bckvf = P.bufs("ckvf", KVC)
            sq = [A.alloc([SBT], F32) for _ in range(2)]; Bsq = P.bufs("sq", 2)
            rstdkv = A.alloc([SBT], F32); Brkv = P.buf("rstdkv")
            ckvn = A.alloc([KVC, SBT], BF16); Bckvn = P.buf("ckvn")
            krs = A.alloc([NB, 8], F32); Bkrs = P.buf("krs")
            krg = A.alloc([ROPE], F32); Bkrg = P.buf("krg")
            cs = [A.alloc([2, 32], F32) for _ in range(2)]; Bcs = P.bufs("cs", 2)
            rt = A.alloc([4, 32], F32); Brt = P.buf("rt")
            krb = A.alloc([ROPE], BF16); Bkrb = P.buf("krb")
            ssqk = A.alloc([NB, H], F32); Bssqk = P.buf("ssqk")
            Vst = A.alloc([NB, H * VD], BF16); BVst = P.buf("Vst")
            KTst = A.alloc([H, SBT], BF16); BKT = P.buf("KTst")
            KRst = A.alloc([SBT], BF16); BKR = P.buf("KRst")
            junk2s = [A.alloc([QK], BF16) for _ in range(4)]; Bj2s = P.bufs("junk2", 4); jr = Rot([0, 1, 2, 3])
            nsb = S // SBT
            Bsers = P.bufs("ser", 4)
            blk_i = 0
            for s in range(NSEQ):
                Bscr = B_kv[s]
                for sb in range(nsb):
                    it = s * nsb + sb
                    hTc = hT[it % 2]; BhTc = BhT[it % 2]
                    t0 = sb * SBT
                    for b in range(NB):
                        t = tmps[blk_i % 2]; blk_i += 1
                        row0 = s * S + t0 + b * 128
                        norm_transpose(xall[row0:row0 + 128, :], 128, gmix, hTc, b * 128, t, BhTc)
                    if P1_STOP == 1:
                        continue
                    for j in range(KVC):
                        bank = 2 + (j % 2)
                        for kc in range(KC):
                            P.op("pe", "matmul", ps[bank][:, 0:SBT], Wckv[:, kc, j * 128:(j + 1) * 128], hTc[:, kc, :],
                                 start=(kc == 0), stop=(kc == KC - 1), r=[BW, BhTc], w=[PB[bank]])
                        P.op("act", "activation", out=ckvf[:, j, :], in_=ps[bank][:, 0:SBT], func=AF.Copy, r=[PB[bank]], w=[Bckvf[j]])
                        P.op("dve", "tensor_tensor", sq[j % 2], ps[bank][:, 0:SBT], ckvf[:, j, :], ALU.mult,
                             r=[PB[bank], Bckvf[j]], w=[Bsq[j % 2]])
                        P.op("pe", "matmul", ps[4][:, 0:SBT], ones_f, sq[j % 2], start=(j == 0), stop=(j == KVC - 1),
                             r=[Bsq[j % 2], B_const], w=[PB[4]])
                    P.op("act", "activation", out=rstdkv, in_=ps[4][:, 0:SBT], func=AF.Sqrt, bias=EPS, scale=1.0 / c.KVL,
                         r=[PB[4]], w=[Brkv])
                    P.op("dve", "reciprocal", rstdkv, rstdkv, r=[Brkv], w=[Brkv])
                    for j in range(KVC):
                        P.op("dve", "scalar_tensor_tensor", ckvn[:, j, :], ckvf[:, j, :], gkva[:, j:j + 1], rstdkv, ALU.mult, ALU.mult,
                             r=[Bckvf[j], Brkv, B_const], w=[Bckvn])
                    if P1_STOP == 2:
                        continue
                    for b in range(NB):
                        for kc in range(KC):
                            P.op("pe", "matmul", ps[5][:, b * ROPE:(b + 1) * ROPE], hTc[:, kc, b * 128:(b + 1) * 128], Wkr[:, kc, :],
                                 start=(kc == 0), stop=(kc == KC - 1), r=[BW, BhTc], w=[PB[5]])
                    for b in range(NB):
                        pk = ps[5][:, b * ROPE:(b + 1) * ROPE]
                        pos0 = t0 + b * 128
                        csb = cs[b % 2]; Bcsb = Bcs[b % 2]
                        P.dma("sp", csb[:, 0, :], cosk[pos0:pos0 + 128, :], w=[Bcsb])
                        P.dma("sp", csb[:, 1, :], sink[pos0:pos0 + 128, :], w=[Bcsb])
                        ji = jr.next()
                        P.op("act", "activation", out=junk2s[ji][:, 0:ROPE], in_=pk, func=AF.Square, accum_out=krs[:, b, 0:1],
                             r=[PB[5]], w=[Bj2s[ji], Bkrs])
                        P.op("dve", "tensor_tensor", krg, pk, gkr, ALU.mult, r=[PB[5], B_const], w=[Bkrg])
                        x1 = krg[:, 0:32]; x2 = krg[:, 32:64]
                        P.op("dve", "tensor_tensor", rt[:, 0, :], x1, csb[:, 0, :], ALU.mult, r=[Bkrg, Bcsb], w=[Brt])
                        P.op("dve", "tensor_tensor", rt[:, 1, :], x2, csb[:, 1, :], ALU.mult, r=[Bkrg, Bcsb], w=[Brt])
                        P.op("dve", "tensor_tensor", rt[:, 2, :], x2, csb[:, 0, :], ALU.mult, r=[Bkrg, Bcsb], w=[Brt])
                        P.op("dve", "tensor_tensor", rt[:, 3, :], x1, csb[:, 1, :], ALU.mult, r=[Bkrg, Bcsb], w=[Brt])
                        P.op("dve", "tensor_tensor", krb[:, 0:32], rt[:, 0, :], rt[:, 1, :], ALU.subtract, r=[Brt], w=[Bkrb])
                        P.op("dve", "tensor_tensor", krb[:, 32:64], rt[:, 2, :], rt[:, 3, :], ALU.add, r=[Brt], w=[Bkrb])
                        P.op("pe", "transpose", psb[6][0:ROPE, b * 128:(b + 1) * 128], krb, ident, r=[Bkrb, B_const], w=[PB[6]])
                    P.op("act", "activation", out=KRst[0:ROPE, :], in_=psb[6][0:ROPE, 0:SBT], func=AF.Copy, r=[PB[6]], w=[BKR])
                    P.dma("sp", KRT[s, :, t0:t0 + SBT], KRst[0:ROPE, :], r=[BKR], wa=[Bscr])
                    if P1_STOP == 3:
                        continue
                    for b in range(NB):
                        for n in range(H // 2):
                            bank = (n % 4)
                            Bser = Bsers[bank]
                            for kc in range(KVC):
                                P.op("pe", "matmul", ps[bank][:, :], ckvn[:, kc, b * 128:(b + 1) * 128], Wukv[:, kc, n * 512:(n + 1) * 512],
                                     start=(kc == 0), stop=(kc == KVC - 1), r=[Bckvn, BW], w=[PB[bank]])
                            pvw = ps[bank][:, :].rearrange("p (h t d) -> p h t d", h=2, t=2)
                            if P1_STOP == 31:
                                continue
                            P.op("dve", "tensor_copy", Vst[:, b, n * 256:(n + 1) * 256].rearrange("p (h d) -> p h d", h=2), pvw[:, :, 1, :],
                                 r=[PB[bank]], w=[BVst, Bser])
                            if P1_STOP == 32:
                                continue
                            for hh in range(2):
                                ji = jr.next()
                                if P1_STOP in (35, 36):
                                    P.op("act", "activation", out=junk2s[ji][:, 0:128], in_=ps[bank][:, hh * 256:hh * 256 + 128], func=AF.Square,
                                         r=[PB[bank]], w=[Bj2s[ji], Bssqk])
                                    continue
                                P.op("act", "activation", out=junk2s[ji][:, 0:128], in_=ps[bank][:, hh * 256:hh * 256 + 128], func=AF.Square,
                                     accum_out=ssqk[:, b, 2 * n + hh:2 * n + hh + 1], r=[PB[bank], Bser], w=[Bj2s[ji], Bssqk])
                    if P1_STOP in (31, 32, 33, 35, 36):
                        continue
                    for b in range(NB):
                        gb = (s * S + t0) // 128 + b
                        P.op("dve", "tensor_scalar", ssqk[:, b, :], ssqk[:, b, :], krs[:, b, 0:1], EPS * QK, ALU.add, ALU.add, r=[Bssqk, Bkrs], w=[Bssqk])
                        P.op("act", "activation", out=RSK[:, gb, :], in_=ssqk[:, b, :], func=AF.Sqrt,
                             r=[Bssqk], w=[B_rsk])
                        P.op("dve", "reciprocal", RSK[:, gb, :], RSK[:, gb, :], r=[B_rsk], w=[B_rsk])
                    if P1_STOP == 34:
                        continue
                    P.dma("sp", VV[s, t0:t0 + SBT, :].rearrange("(b p) f -> p b f", p=128), Vst, r=[BVst], wa=[Bscr])
                    if P1_STOP == 4:
                        continue
                    for h in range(H):
                        bank = 6 + (h % 2)
                        for kc in range(KVC):
                            P.op("pe", "matmul", ps[bank][:, 0:SBT], Wukv[:, kc, h * 256:h * 256 + 128], ckvn[:, kc, :],
                                 start=(kc == 0), stop=(kc == KVC - 1), r=[BW, Bckvn], w=[PB[bank]])
                        if h % 2 == 0:
                            P.op("act", "activation", out=KTst[:, h, :], in_=ps[bank][:, 0:SBT], func=AF.Copy, scale=gkn[:, 0:1],
                                 r=[PB[bank], B_const], w=[BKT])
                        else:
                            P.op("dve", "tensor_scalar", KTst[:, h, :], ps[bank][:, 0:SBT], gkn[:, 0:1], None, ALU.mult,
                                 r=[PB[bank], B_const], w=[BKT])
                    P.dma("sp", KT[s, :, :, t0:t0 + SBT].rearrange("h p t -> p h t"), KTst, r=[BKT], wa=[Bscr])

        tilesE = tiles_of(TE, 512)
        tilesU = tiles_of(TU, 512)
        blocksE = blocks_of(TE)

        def mixer(s):
            PC, PGC, QC, MAC = c.PC, c.PGC, c.QC, c.MAC
            A.seek(0)
            aT = A.alloc([PC, TE], BF16); BaT = P.buf("aT")
            A.check(33)
            A.seek(33)
            hT = A.alloc([KC, TU], BF16); BhT = P.buf("hT")
            A.check(99)
            A.seek(99)
            tmps = alloc_norm_tmp()
            for bi, (i0, r) in enumerate(blocks_of(TU)):
                norm_transpose(xe[s, i0:i0 + r, :], r, gmix, hT, i0, tmps[bi % 2], BhT)
            P.barrier(cells)
            if MIX_STOP == 1:
                P.drop_bufs()
                return
            A.seek(99)
            Wg = [A.alloc([KC, 512], BF16) for _ in range(2)]; BWg = P.bufs("Wg", 2)
            sg = [A.alloc([TE], BF16) for _ in range(2)]; Bsg = P.bufs("sg", 2)
            banks = Rot([0, 1, 2, 3, 4, 5])
            cnt = 0
            for gi, off in enumerate((c.OFF_GP, c.OFF_GM)):
                for pi in range(D // 512):
                    W = Wg[cnt % 2]; BW_ = BWg[cnt % 2]; cnt += 1
                    wdma(W, w_in, off + pi * 512, 512, BW_)
                    for jj in range(4):
                        j = pi * 4 + jj
                        sgt = sg[j % 2]; Bs_ = Bsg[j % 2]
                        for (q0, n) in tilesE:
                            bank = banks.next()
                            for kc in range(KC):
                                P.op("pe", "matmul", ps[bank][:, 0:n], W[:, kc, jj * 128:(jj + 1) * 128], hT[:, kc, 8 + q0:8 + q0 + n],
                                     start=(kc == 0), stop=(kc == KC - 1), r=[BW_, BhT], w=[PB[bank]])
                            P.op("act", "activation", out=sgt[:, q0:q0 + n], in_=ps[bank][:, 0:n], func=AF.Sigmoid, r=[PB[bank]], w=[Bs_])
                        P.dma("sp", GATE[s, gi, j], sgt, r=[Bs_], wa=[B_gate[s]])
            P.barrier(cells)
            if MIX_STOP == 2:
                P.drop_bufs()
                return
            A.seek(99)
            Wp = [A.alloc([KC, 256], BF16) for _ in range(2)]; BWp = P.bufs("Wp", 2)
            usb = [A.alloc([TU], F32) for _ in range(2)]; Bu = P.bufs("usb", 2)
            ta = A.alloc([TU], F32); tb = A.alloc([TU], F32); tc = A.alloc([TU], F32); Bta = P.buf("ta"); Btb = P.buf("tb"); Btc = P.buf("tc")
            pooled = A.alloc([PGC, TE], BF16); Bpl = P.buf("pooled")
            plw = [A.alloc([PGC, c.PG], BF16) for _ in range(2)]; Bplw = P.bufs("plw", 2)
            icn = [A.alloc([TE], F32) for _ in range(2)]; Bicn = P.bufs("icn", 2)
            banks = Rot([0, 1, 2, 3, 4, 5])
            cnt = 0
            for g in range(4):
                w = WINDOWS[g]
                wdma(plw[g % 2], pool_w, 0, c.PG, Bplw[g % 2], row0=g * c.PG, nrows=c.PG)
                P.dma("sp", icn[g % 2], invcnt[:, g, :], w=[Bicn[g % 2]])
                for cc in range(PGC):
                    ch = g * PGC + cc
                    if ch % 2 == 0:
                        W = Wp[cnt % 2]; BW_ = BWp[cnt % 2]; cnt += 1
                        ncol = min(256, c.PW - ch * 128)
                        wdma(W[:, :, 0:ncol], w_in, ch * 128, ncol, BW_)
                    u = usb[ch % 2]; Bu_ = Bu[ch % 2]
                    for (u0, n) in tilesU:
                        bank = banks.next()
                        for kc in range(KC):
                            P.op("pe", "matmul", ps[bank][:, 0:n], W[:, kc, (ch % 2) * 128:(ch % 2) * 128 + 128], hT[:, kc, u0:u0 + n],
                                 start=(kc == 0), stop=(kc == KC - 1), r=[BW_, BhT], w=[PB[bank]])
                        P.op("act", "activation", out=u[:, u0:u0 + n], in_=ps[bank][:, 0:n], func=AF.Copy, r=[PB[bank]], w=[Bu_])
                    if w == 2:
                        P.op("dve", "tensor_tensor", tc[:, 8:8 + TE], u[:, 7:7 + TE], u[:, 8:8 + TE], ALU.add, r=[Bu_], w=[Btc])
                    else:
                        P.op("dve", "tensor_tensor", ta[:, 0:TU - 1], u[:, 0:TU - 1], u[:, 1:TU], ALU.add, r=[Bu_], w=[Bta])
                        if w == 4:
                            P.op("dve", "tensor_tensor", tc[:, 8:8 + TE], ta[:, 6:6 + TE], ta[:, 8:8 + TE], ALU.add, r=[Bta], w=[Btc])
                        else:
                            P.op("dve", "tensor_tensor", tb[:, 0:TU - 3], ta[:, 0:TU - 3], ta[:, 2:TU - 1], ALU.add, r=[Bta], w=[Btb])
                            if w == 8:
                                P.op("dve", "tensor_tensor", tc[:, 8:8 + TE], tb[:, 4:4 + TE], tb[:, 8:8 + TE], ALU.add, r=[Btb], w=[Btc])
                            else:
                                P.op("dve", "tensor_tensor", ta[:, 0:TU - 7], tb[:, 0:TU - 7], tb[:, 4:TU - 3], ALU.add, r=[Btb], w=[Bta])
                                P.op("dve", "tensor_tensor", tc[:, 8:8 + TE], ta[:, 0:TE], ta[:, 8:8 + TE], ALU.add, r=[Bta], w=[Btc])
                    P.op("dve", "tensor_tensor", tc[:, 8:8 + TE], tc[:, 8:8 + TE], icn[g % 2], ALU.mult, r=[Btc, Bicn[g % 2]], w=[Btc])
                    P.op("dve", "tensor_tensor", pooled[:, cc, :], tc[:, 8:8 + TE], u[:, 8:8 + TE], ALU.subtract, r=[Btc, Bu_], w=[Bpl])
                for oc in range(PGC):
                    ch = g * PGC + oc
                    for (q0, n) in tilesE:
                        bank = banks.next()
                        for kc in range(PGC):
                            P.op("pe", "matmul", ps[bank][:, 0:n], plw[g % 2][:, kc, oc * 128:(oc + 1) * 128], pooled[:, kc, q0:q0 + n],
                                 start=(kc == 0), stop=(kc == PGC - 1), r=[Bplw[g % 2], Bpl], w=[PB[bank]])
                        P.op("act", "activation", out=aT[:, ch, q0:q0 + n], in_=ps[bank][:, 0:n], func=AF.Copy, scale=psc[:, ch:ch + 1],
                             r=[PB[bank], B_const], w=[BaT])
            P.barrier(cells)
            if MIX_STOP == 3:
                P.drop_bufs()
                return
            A.seek(99)
            cqnT = A.alloc([QC, TE], BF16); Bcqn = P.buf("cqnT")
            Wcq = [A.alloc([KC, 128], BF16) for _ in range(2)]; BWcq = P.bufs("Wcq", 2)
            cqf = A.alloc([QC, TE], F32); Bcqf = P.bufs("cqf", QC)
            sq = [A.alloc([512], F32) for _ in range(2)]; Bsq = P.bufs("sqq", 2)
            rstd = A.alloc([TE], F32); Brs = P.buf("rstdq")
            banks = Rot([0, 1, 2, 3])
            sqi = 0
            for j in range(QC):
                W = Wcq[j % 2]; BW_ = BWcq[j % 2]
                wdma(W, w_in, c.OFF_CQ + j * 128, 128, BW_)
                for ti, (q0, n) in enumerate(tilesE):
                    bank = banks.next()
                    for kc in range(KC):
                        P.op("pe", "matmul", ps[bank][:, 0:n], W[:, kc, :], hT[:, kc, 8 + q0:8 + q0 + n],
                             start=(kc == 0), stop=(kc == KC - 1), r=[BW_, BhT], w=[PB[bank]])
                    P.op("act", "activation", out=cqf[:, j, q0:q0 + n], in_=ps[bank][:, 0:n], func=AF.Copy, r=[PB[bank]], w=[Bcqf[j]])
                    sqt = sq[sqi % 2]; Bsq_ = Bsq[sqi % 2]; sqi += 1
                    P.op("dve", "tensor_tensor", sqt[:, 0:n], ps[bank][:, 0:n], cqf[:, j, q0:q0 + n], ALU.mult, r=[PB[bank], Bcqf[j]], w=[Bsq_])
                    P.op("pe", "matmul", ps[5 + ti][:, 0:n], ones_f, sqt[:, 0:n], start=(j == 0), stop=(j == QC - 1),
                         r=[Bsq_, B_const], w=[PB[5 + ti]])
            for ti, (q0, n) in enumerate(tilesE):
                P.op("act", "activation", out=rstd[:, q0:q0 + n], in_=ps[5 + ti][:, 0:n], func=AF.Sqrt, bias=EPS, scale=1.0 / c.QL,
                     r=[PB[5 + ti]], w=[Brs])
            P.op("dve", "reciprocal", rstd, rstd, r=[Brs], w=[Brs])
            for j in range(QC):
                P.op("dve", "scalar_tensor_tensor", cqnT[:, j, :], cqf[:, j, :], gqa[:, j:j + 1], rstd, ALU.mult, ALU.mult,
                     r=[Bcqf[j], Brs, B_const], w=[Bcqn])
            P.barrier(cells)
            if MIX_STOP == 4:
                P.drop_bufs()
                return
            A.seek(33)
            QTn = A.alloc([H, TE], BF16); QTr = A.alloc([H, TE], BF16); BQT = P.buf("QT")
            A.check(98)
            A.seek(99 + 2 * QC * TE / 1024 + 1)
            NP5 = 2 if H >= 4 else 1
            HH = H // NP5
            Wuq = A.alloc([QC, HH * QK], BF16); BWuq = P.buf("Wuq")
            gq = A.alloc([2 * QK], F32); csq = A.alloc([NBE, 2, 32], F32); Bc5 = P.buf("c5")
            P.dma("sp", gq, g_q_rep, w=[Bc5])
            for bi, (e0, r) in enumerate(blocksE):
                P.dma("sp", csq[0:r, bi, 0, :], cosq[e0:e0 + r, :], w=[Bc5])
                P.dma("sp", csq[0:r, bi, 1, :], sinq[e0:e0 + r, :], w=[Bc5])
            qsbf = A.alloc([HH * QK], F32); qsb = qsbf.rearrange("p (h d) -> p h d", h=HH); Bqsb = P.buf("qsb")
            ssqq = A.alloc([HH], F32); rsq = A.alloc([HH], F32); Bssq = P.buf("ssqq"); Brq = P.buf("rsq")
            qn = A.alloc([HH, NOPE], BF16); Bqn = P.buf("qn")
            rtq = A.alloc([4, HH, 32], F32); Brtq = P.buf("rtq")
            qrf = A.alloc([HH, ROPE], F32); Bqrf = P.buf("qrf")
            qr = A.alloc([HH, ROPE], BF16); Bqr = P.buf("qr")
            junk3s = [A.alloc([QK], BF16) for _ in range(4)]; Bj3s = P.bufs("junk3", 4); jr3 = Rot([0, 1, 2, 3])
            banks = Rot([0, 1, 2, 3])
            tb_ = Rot([4, 5, 6, 7])
            for hf in range(NP5):
                hb = hf * HH
                for h0 in range(0, HH * QK, 768):
                    n = min(768, HH * QK - h0)
                    wdma(Wuq[:, :, h0:h0 + n], w_uq, hb * QK + h0, n, BWuq)
                for bi, (e0, r) in enumerate(blocksE):
                    for hp in range(HH // 2):
                        bank = banks.next()
                        for kc in range(QC):
                            P.op("pe", "matmul", ps[bank][0:r, 0:2 * QK], cqnT[:, kc, e0:e0 + r], Wuq[:, kc, hp * 2 * QK:(hp + 1) * 2 * QK],
                                 start=(kc == 0), stop=(kc == QC - 1), r=[Bcqn, BWuq], w=[PB[bank]])
                        for hh in range(2):
                            ji = jr3.next()
                            P.op("act", "activation", out=junk3s[ji][0:r, :], in_=ps[bank][0:r, hh * QK:(hh + 1) * QK], func=AF.Square,
                                 accum_out=ssqq[0:r, 2 * hp + hh:2 * hp + hh + 1], r=[PB[bank]], w=[Bj3s[ji], Bssq])
                        P.op("dve", "tensor_tensor", qsbf[0:r, hp * 2 * QK:(hp + 1) * 2 * QK], ps[bank][0:r, 0:2 * QK], gq[0:r, :],
                             ALU.mult, r=[PB[bank], Bc5, Bssq], w=[Bqsb])
                    P.op("act", "activation", out=rsq[0:r, :], in_=ssqq[0:r, :], func=AF.Sqrt, bias=EPS, scale=1.0 / QK, r=[Bssq], w=[Brq])
                    P.op("dve", "reciprocal", rsq[0:r, :], rsq[0:r, :], r=[Brq], w=[Brq])
                    P.op("dve", "tensor_tensor", qn[0:r], qsb[0:r, :, 0:NOPE], rsq[0:r, :].unsqueeze(2).broadcast_to([r, HH, NOPE]), ALU.mult,
                         r=[Bqsb, Brq], w=[Bqn])
                    x1 = qsb[0:r, :, NOPE:NOPE + 32]; x2 = qsb[0:r, :, NOPE + 32:QK]
                    cb_ = csq[0:r, bi, 0, :].unsqueeze(1).broadcast_to([r, HH, 32])
                    sb_ = csq[0:r, bi, 1, :].unsqueeze(1).broadcast_to([r, HH, 32])
                    P.op("dve", "tensor_tensor", rtq[0:r, 0], x1, cb_, ALU.mult, r=[Bqsb, Bc5], w=[Brtq])
                    P.op("dve", "tensor_tensor", rtq[0:r, 1], x2, sb_, ALU.mult, r=[Bqsb, Bc5], w=[Brtq])
                    P.op("dve", "tensor_tensor", rtq[0:r, 2], x2, cb_, ALU.mult, r=[Bqsb, Bc5], w=[Brtq])
                    P.op("dve", "tensor_tensor", rtq[0:r, 3], x1, sb_, ALU.mult, r=[Bqsb, Bc5], w=[Brtq])
                    P.op("dve", "tensor_tensor", qrf[0:r, :, 0:32], rtq[0:r, 0], rtq[0:r, 1], ALU.subtract, r=[Brtq], w=[Bqrf])
                    P.op("dve", "tensor_tensor", qrf[0:r, :, 32:64], rtq[0:r, 2], rtq[0:r, 3], ALU.add, r=[Brtq], w=[Bqrf])
                    P.op("dve", "tensor_tensor", qr[0:r], qrf[0:r], rsq[0:r, :].unsqueeze(2).broadcast_to([r, HH, ROPE]), ALU.mult,
                         r=[Bqrf, Brq], w=[Bqr])
                    for h0 in range(0, HH, 4):
                        nh = min(4, HH - h0)
                        bank = tb_.next()
                        for i in range(nh):
                            P.op("pe", "transpose", psb[bank][:, i * 128:i * 128 + r], qn[0:r, h0 + i, :], ident[0:r, 0:r], r=[Bqn, B_const], w=[PB[bank]])
                        P.op("act", "activation", out=QTn[:, hb + h0:hb + h0 + nh, e0:e0 + r],
                             in_=psb[bank][:, 0:nh * 128].rearrange("p (h t) -> p h t", h=nh)[:, :, 0:r], func=AF.Copy, r=[PB[bank]], w=[BQT])
                    for h0 in range(0, HH, 8):
                        nh = min(8, HH - h0)
                        bank = tb_.next()
                        for i in range(nh):
                            P.op("pe", "transpose", psb[bank][0:ROPE, i * 128:i * 128 + r], qr[0:r, h0 + i, :], ident[0:r, 0:r], r=[Bqr, B_const], w=[PB[bank]])
                        P.op("dve", "tensor_copy", QTr[0:ROPE, hb + h0:hb + h0 + nh, e0:e0 + r],
                             psb[bank][0:ROPE, 0:nh * 128].rearrange("p (h t) -> p h t", h=nh)[:, :, 0:r], r=[PB[bank]], w=[BQT])
            P.barrier(cells)
            if MIX_STOP == 5:
                P.drop_bufs()
                return
            A.seek(98)
            bT = A.alloc([H, TE], BF16); BbT = P.buf("bT")
            A.check(131)
            KCH = c.KCH
            NKB = KCH // 128
            Kn = [A.alloc([KCH], BF16) for _ in range(2)]; BKn = P.bufs("Kn", 2)
            Vh = [A.alloc([NKB, VD], BF16) for _ in range(2)]; BVh = P.bufs("Vh", 2)
            Kr = A.alloc([S], BF16); BKr = P.buf("Kr")
            PT = [A.alloc([512], BF16) for _ in range(4)]; BPT = P.bufs("PT", 4)
            rc = A.alloc([512], F32); Brc = P.buf("rc")
            P.dma("sp", Kr[0:ROPE, :], KRT[s], r=[B_kv[s]], w=[BKr])
            NQT = len(tilesE)
            assert NQT <= 3
            ci = 0
            pti = 0
            sbk = Rot([0, 1])
            iters = []
            for h in range(H):
                for k0 in range(0, S, KCH):
                    for kb in range(NKB):
                        for qi, (q0, n) in enumerate(tilesE):
                            iters.append((h, k0, kb, qi, q0, n))
            loaded = {}

            def emit_st(it):
                nonlocal ci
                h, k0, kb, qi, q0, n = it
                if (h, k0) not in loaded:
                    Kn_, BKn_ = Kn[ci % 2], BKn[ci % 2]
                    Vh_, BVh_ = Vh[ci % 2], BVh[ci % 2]
                    ci += 1
                    P.dma("sp", Kn_, KT[s, h, :, k0:k0 + KCH], r=[B_kv[s]], w=[BKn_])
                    P.dma("sp", Vh_, VV[s, k0:k0 + KCH, h * VD:(h + 1) * VD].rearrange("(b p) d -> p b d", p=128), r=[B_kv[s]], w=[BVh_])
                    loaded[(h, k0)] = (Kn_, BKn_, Vh_, BVh_)
                Kn_, BKn_, Vh_, BVh_ = loaded[(h, k0)]
                kg = (k0 // 128) + kb
                sbank = sbk.next()
                P.op("pe", "matmul", ps[sbank][:, 0:n], Kn_[:, kb * 128:(kb + 1) * 128], QTn[:, h, q0:q0 + n], start=True, stop=False,
                     r=[BKn_, BQT], w=[PB[sbank]])
                P.op("pe", "matmul", ps[sbank][:, 0:n], Kr[0:ROPE, kg * 128:(kg + 1) * 128], QTr[0:ROPE, h, q0:q0 + n], start=False, stop=True,
                     r=[BKr, BQT], w=[PB[sbank]])
                return sbank

            pending = emit_st(iters[0])
            for idx, it in enumerate(iters):
                h, k0, kb, qi, q0, n = it
                sbank = pending
                if idx + 1 < len(iters):
                    pending = emit_st(iters[idx + 1])
                Kn_, BKn_, Vh_, BVh_ = loaded[(h, k0)]
                kg = (k0 // 128) + kb
                first = (kg == 0)
                last = (kg == S // 128 - 1)
                pt = PT[pti % 4]; Bpt = BPT[pti % 4]; pti += 1
                P.op("act", "activation", out=pt[:, 0:n], in_=ps[sbank][:, 0:n], func=AF.Exp,
                     scale=RSK[:, s * (S // 128) + kg, h:h + 1], r=[PB[sbank], B_rsk], w=[Bpt])
                P.op("pe", "matmul", ps[2 + 2 * qi][:, 0:n], Vh_[:, kb, :], pt[:, 0:n], start=first, stop=last, r=[BVh_, Bpt], w=[PB[2 + 2 * qi]])
                P.op("pe", "matmul", ps[3 + 2 * qi][:, 0:n], ones_b, pt[:, 0:n], start=first, stop=last, r=[B_const, Bpt], w=[PB[3 + 2 * qi]])
                if last and qi == NQT - 1:
                    for qj, (p0, m_) in enumerate(tilesE):
                        P.op("dve", "reciprocal", rc[:, 0:m_], ps[3 + 2 * qj][:, 0:m_], r=[PB[3 + 2 * qj]], w=[Brc])
                        P.op("dve", "tensor_tensor", bT[:, h, p0:p0 + m_], ps[2 + 2 * qj][:, 0:m_], rc[:, 0:m_], ALU.mult, r=[PB[2 + 2 * qj], Brc], w=[BbT])
            P.barrier(cells)
            if MIX_STOP == 6:
                P.drop_bufs()
                return
            A.seek(33)
            mT = A.alloc([KC, TE], BF16); BmT = P.buf("mT")
            A.check(98)
            A.seek(131)
            Wbp = [A.alloc([PC, 256], BF16) for _ in range(2)]; BWbp = P.bufs("Wbp", 2)
            Wbm = [A.alloc([MAC, 256], BF16) for _ in range(2)]; BWbm = P.bufs("Wbm", 2)
            sgp = [A.alloc([TE], BF16) for _ in range(2)]; sgm = [A.alloc([TE], BF16) for _ in range(2)]
            Bsgp = P.bufs("sgp", 2); Bsgm = P.bufs("sgm", 2)
            t1 = [A.alloc([512], F32) for _ in range(2)]; t2 = [A.alloc([512], F32) for _ in range(2)]
            Bt1 = P.bufs("t1", 2); Bt2 = P.bufs("t2", 2)
            pr = Rot([0, 1, 2, 3])
            ti_ = 0
            for j in range(KC):
                if j % 2 == 0:
                    k2 = (j // 2) % 2
                    ncol = min(256, D - j * 128)
                    wdma(Wbp[k2][:, :, 0:ncol], w_bp, j * 128, ncol, BWbp[k2])
                    wdma(Wbm[k2][:, :, 0:ncol], w_bm, j * 128, ncol, BWbm[k2])
                P.dma("sp", sgp[j % 2], GATE[s, 0, j], r=[B_gate[s]], w=[Bsgp[j % 2]])
                P.dma("sp", sgm[j % 2], GATE[s, 1, j], r=[B_gate[s]], w=[Bsgm[j % 2]])
                jc = (j % 2) * 128
                for (q0, n) in tilesE:
                    pi_ = pr.next()
                    b1, b2 = 2 * pi_, 2 * pi_ + 1
                    for kc in range(PC):
                        P.op("pe", "matmul", ps[b1][:, 0:n], Wbp[k2][:, kc, jc:jc + 128], aT[:, kc, q0:q0 + n], start=(kc == 0), stop=(kc == PC - 1),
                             r=[BWbp[k2], BaT], w=[PB[b1]])
                    for kc in range(MAC):
                        P.op("pe", "matmul", ps[b2][:, 0:n], Wbm[k2][:, kc, jc:jc + 128], bT[:, kc, q0:q0 + n], start=(kc == 0), stop=(kc == MAC - 1),
                             r=[BWbm[k2], BbT], w=[PB[b2]])
                    a1 = t1[ti_ % 2]; a2 = t2[ti_ % 2]; Ba1 = Bt1[ti_ % 2]; Ba2 = Bt2[ti_ % 2]; ti_ += 1
                    P.op("dve", "tensor_tensor", a1[:, 0:n], ps[b1][:, 0:n], sgp[j % 2][:, q0:q0 + n], ALU.mult, r=[PB[b1], Bsgp[j % 2]], w=[Ba1])
                    P.op("dve", "tensor_tensor", a2[:, 0:n], ps[b2][:, 0:n], sgm[j % 2][:, q0:q0 + n], ALU.mult, r=[PB[b2], Bsgm[j % 2]], w=[Ba2])
                    P.op("pool", "tensor_tensor", mT[:, j, q0:q0 + n], a1[:, 0:n], a2[:, 0:n], ALU.add, r=[Ba1, Ba2], w=[BmT])
            P.barrier(cells)
            if MIX_STOP == 7:
                P.drop_bufs()
                return
            A.seek(98)
            Wo = [A.alloc([KC, 512], BF16) for _ in range(2)]; BWo = P.bufs("Wo", 2)
            xs = [A.alloc([512], F32) for _ in range(3)]; Bxs = P.bufs("xs", 3)
            ost = [A.alloc([512], F32) for _ in range(3)]; Bost = P.bufs("ost", 3)
            banks = Rot(list(range(8)))
            xi = 0
            for n_ in range(D // 512):
                W = Wo[n_ % 2]; BW_ = BWo[n_ % 2]
                wdma(W, w_o, n_ * 512, 512, BW_)
                for (e0, r) in blocksE:
                    bank = banks.next()
                    x_ = xs[xi % 3]; Bx_ = Bxs[xi % 3]; o_ = ost[xi % 3]; Bo_ = Bost[xi % 3]; xi += 1
                    P.dma("sp", x_[0:r, :], xe[s, 8 + e0:8 + e0 + r, n_ * 512:(n_ + 1) * 512], w=[Bx_])
                    for kc in range(KC):
                        P.op("pe", "matmul", ps[bank][0:r, :], mT[:, kc, e0:e0 + r], W[:, kc, :], start=(kc == 0), stop=(kc == KC - 1),
                             r=[BmT, BW_], w=[PB[bank]])
                    P.op("dve", "tensor_tensor", o_[0:r, :], ps[bank][0:r, :], x_[0:r, :], ALU.add, r=[PB[bank], Bx_], w=[Bo_])
                    lo = (128 - TE % 128) if (e0 % 128 != 0) else 0
                    P.dma("sp", X1[s, e0 + lo:e0 + r, n_ * 512:(n_ + 1) * 512], o_[lo:r, :], r=[Bo_], wa=[B_x1[s]])
            P.barrier(cells)
            P.drop_bufs()

        def ffn(s, a0):
            TF, FC = c.TF, c.FC
            HF = TF // 2
            N = HF + 2
            A.seek(0)
            gT = A.alloc([FC, TF], BF16); BgT = P.buf("gT")
            mark0 = A.off
            h2T = A.alloc([KC, TF + 2], BF16); Bh2 = P.buf("h2T")
            mark = A.off / 1024
            tmps = alloc_norm_tmp()
            for bi, (i0, r) in enumerate(blocks_of(TF + 2)):
                norm_transpose(X1[s, a0 + i0:a0 + i0 + r, :], r, gffn, h2T, i0, tmps[bi % 2], Bh2, rsrc=[B_x1[s]],
                               mask_ap=valid[a0 + i0:a0 + i0 + r, :])
            P.barrier(cells)
            if FFN_STOP == 1:
                P.drop_bufs()
                return
            A.seek(mark)
            Wg = [A.alloc([KC, 256], BF16) for _ in range(2)]; Wv = [A.alloc([KC, 256], BF16) for _ in range(2)]
            BWg = P.bufs("Wug", 2); BWv = P.bufs("Wuv", 2)
            tg = [A.alloc([HF], F32) for _ in range(2)]; tv = [A.alloc([HF], F32) for _ in range(2)]; sg = [A.alloc([HF], F32) for _ in range(2)]
            Btg = P.bufs("tg", 2); Btv = P.bufs("tv", 2); Bsg = P.bufs("sgl", 2)
            pr = Rot([0, 1, 2, 3])
            ei = 0
            for f in range(FC):
                if f % 2 == 0:
                    k2 = (f // 2) % 2
                    ncol = min(256, c.DFF - f * 128)
                    P.dma("pool", Wg[k2][:, :, 0:ncol], WUP2[f // 2, 0].rearrange("p (k c) -> p k c", k=KC)[:, :, 0:ncol], r=[B_wpre], w=[BWg[k2]])
                    P.dma("pool", Wv[k2][:, :, 0:ncol], WUP2[f // 2, 1].rearrange("p (k c) -> p k c", k=KC)[:, :, 0:ncol], r=[B_wpre], w=[BWv[k2]])
                fc0 = (f % 2) * 128
                for half in range(2):
                    c0 = half * HF
                    pi_ = pr.next()
                    bg, bv = 2 * pi_, 2 * pi_ + 1
                    for kc in range(KC):
                        P.op("pe", "matmul", ps[bg][:, 0:N], Wg[k2][:, kc, fc0:fc0 + 128], h2T[:, kc, c0:c0 + N], start=(kc == 0), stop=(kc == KC - 1),
                             r=[BWg[k2], Bh2], w=[PB[bg]])
                    for kc in range(KC):
                        P.op("pe", "matmul", ps[bv][:, 0:N], Wv[k2][:, kc, fc0:fc0 + 128], h2T[:, kc, c0:c0 + N], start=(kc == 0), stop=(kc == KC - 1),
                             r=[BWv[k2], Bh2], w=[PB[bv]])
                    tg_, tv_, sg_ = tg[ei % 2], tv[ei % 2], sg[ei % 2]
                    Btg_, Btv_, Bsg_ = Btg[ei % 2], Btv[ei % 2], Bsg[ei % 2]
                    ei += 1
                    for (bank, t_, Bt_, ch) in ((bg, tg_, Btg_, f), (bv, tv_, Btv_, FC + f)):
                        P.op("act", "activation", out=t_, in_=ps[bank][:, 1:1 + HF], func=AF.Identity, bias=cb[:, ch:ch + 1], scale=cw[:, 1, ch:ch + 1],
                             r=[PB[bank], B_const], w=[Bt_])
                        P.op("dve", "scalar_tensor_tensor", t_, ps[bank][:, 0:HF], cw[:, 0, ch:ch + 1], t_, ALU.mult, ALU.add,
                             r=[PB[bank], B_const, Bt_], w=[Bt_])
                        P.op("dve", "scalar_tensor_tensor", t_, ps[bank][:, 2:2 + HF], cw[:, 2, ch:ch + 1], t_, ALU.mult, ALU.add,
                             r=[PB[bank], B_const, Bt_], w=[Bt_])
                    P.op("act", "activation", out=sg_, in_=tg_, func=AF.Silu, r=[Btg_], w=[Bsg_])
                    P.op("dve", "tensor_tensor", gT[:, f, c0:c0 + HF], sg_, tv_, ALU.mult, r=[Bsg_, Btv_], w=[BgT])
            P.barrier(cells)
            if FFN_STOP == 2:
                P.drop_bufs()
                return
            A.off = mark0
            groups = FGROUPS
            Wd = [A.alloc([KG, 512], BF16) for _ in range(2)]; BWd = P.bufs("Wd", 2)
            xs = [A.alloc([512], F32) for _ in range(4)]; Bxs = P.bufs("xs2", 4)
            ost = [A.alloc([512], F32) for _ in range(4)]; Bost = P.bufs("ost2", 4)
            NBk = TF // 128
            wi = 0
            xi = 0
            for n_ in range(D // 512):
                for gi, (f0, nf) in enumerate(groups):
                    W = Wd[wi % 2]; BW_ = BWd[wi % 2]; wi += 1
                    P.dma("pool", W[:, 0:nf, :], WD2[n_, gi].rearrange("p (k c) -> p k c", k=KG)[:, 0:nf, :], r=[B_wpre], w=[BW_])
                    for b in range(NBk):
                        bank = b + NBk * (n_ % (8 // NBk))
                        for fi in range(nf):
                            f = f0 + fi
                            P.op("pe", "matmul", ps[bank][:, :], gT[:, f, b * 128:(b + 1) * 128], W[:, fi, :], start=(f == 0), stop=(f == FC - 1),
                                 r=[BgT, BW_], w=[PB[bank]])
                for b in range(NBk):
                    bank = b + NBk * (n_ % (8 // NBk))
                    x_ = xs[xi % 4]; Bx_ = Bxs[xi % 4]; o_ = ost[xi % 4]; Bo_ = Bost[xi % 4]; xi += 1
                    if FFN_STOP == 3:
                        continue
                    P.dma("sp", x_, X1[s, a0 + 1 + b * 128:a0 + 1 + (b + 1) * 128, n_ * 512:(n_ + 1) * 512], r=[B_x1[s]], w=[Bx_])
                    P.op("dve", "tensor_tensor", o_, ps[bank][:, :], x_, ALU.add, r=[PB[bank], Bx_], w=[Bo_])
                    if FFN_STOP == 4:
                        continue
                    ydst = X1 if FFN_STOP == 5 else y
                    P.dma("sp", ydst[s, a0 + b * 128:a0 + (b + 1) * 128, n_ * 512:(n_ + 1) * 512], o_, r=[Bo_], wa=[B_y])
            P.barrier(cells)
            P.drop_bufs()

        if "p1" in stages:
            phase1()
            P.barrier(cells)
            P.drop_bufs()
        for s in range(NSEQ):
            if "mix" in stages:
                mixer(s)
            if "ffn" in stages:
                for a0 in range(0, TQ, c.TF):
                    ffn(s, a0)
        last = P._add("sp", "dma_start", (), dict(out=cells["sp_dst"], in_=cells["sp_src"]), [],
                      P.phase_bufs + B_kv + B_gate + B_x1 + [B_y, B_const, B_rsk, B_wpre, P.cell_buf], True)
        P.assign()
        n_ops = {e: len(v) for e, v in P.ops.items()}
        print("[build] ops per engine:", n_ops, "sems left", len(P.free_sems))

        @block.tensor
        def _(e):
            P.emit("pe", e)

        @block.scalar
        def _(e):
            P.emit("act", e)

        @block.vector
        def _(e):
            P.emit("dve", e)

        @block.gpsimd
        def _(e):
            P.emit("pool", e)

        @block.sync
        def _(e):
            P.emit("sp", e)
            e.wait_ge(last.sem, last.val)
    return nc


def rope_tables_np(n_pos):
    inv = (1.0 / (np.float32(10000.0) ** (np.arange(0, ROPE, 2, dtype=np.float32) / np.float32(ROPE)))).astype(np.float32)
    ang = (np.arange(n_pos, dtype=np.float32)[:, None] * inv[None, :]).astype(np.float32)
    return np.cos(ang).astype(np.float32), np.sin(ang).astype(np.float32)


def host_inputs(cfg, inp):
    c = cfg
    D, S, H = c.D, c.S, c.H
    f32 = np.float32
    xs = np.concatenate([np.asarray(inp["x_prompt"], f32), np.asarray(inp["x_sample"], f32)], 0)
    xall = np.ascontiguousarray(xs.reshape(NSEQ * S, D))

    def col(v, n):
        return np.ascontiguousarray(np.asarray(v, f32).reshape(n, 128).T)

    cosk, sink = rope_tables_np(S)
    qg = np.asarray(inp["q_norm_gain"], f32).reshape(QK)
    kg = np.asarray(inp["k_norm_gain"], f32).reshape(QK)
    cw = np.asarray(inp["conv_w"], f32).reshape(3, 2 * c.DFF)
    common = {
        "xall": xall, "cosk": cosk, "sink": sink,
        "g_mix": col(inp["norm_mix_gain"], c.KC), "g_ffn": col(inp["norm_ffn_gain"], c.KC),
        "g_qa": col(inp["q_a_norm_gain"], c.QC), "g_kva": col(inp["kv_a_norm_gain"], c.KVC),
        "g_q_rep": np.ascontiguousarray(np.broadcast_to(np.concatenate([qg, qg])[None, :], (128, 2 * QK))),
        "g_kr_rep": np.ascontiguousarray(np.broadcast_to(kg[None, NOPE:], (128, ROPE))),
        "g_kn": np.ascontiguousarray(kg[:NOPE].reshape(128, 1)),
        "pscale": col(inp["pool_scale"], c.PC),
        "convw": np.ascontiguousarray(cw.reshape(3, 2 * c.FC, 128).transpose(2, 0, 1)),
        "convb": col(inp["conv_b"], 2 * c.FC),
        "ident": np.eye(128, dtype=f32),
        "w_in": np.asarray(inp["w_in"], f32).reshape(D, c.IN_COLS),
        "pool_w": np.asarray(inp["pool_w"], f32).reshape(c.PW, c.PG),
        "w_uq": np.asarray(inp["w_uq"], f32).reshape(c.QL, H * QK),
        "w_ukv": np.asarray(inp["w_ukv"], f32).reshape(c.KVL, H * 256),
        "w_bp": np.asarray(inp["w_branch_pool"], f32).reshape(c.PW, D),
        "w_bm": np.asarray(inp["w_branch_mla"], f32).reshape(c.MA, D),
        "w_o": np.asarray(inp["w_o"], f32).reshape(D, D),
        "w_up": np.asarray(inp["w_up"], f32).reshape(D, 2 * c.DFF),
        "w_down": np.asarray(inp["w_down"], f32).reshape(c.DFF, D),
    }
    maps = []
    for core in range(c.NCORE):
        start = core * c.TQ
        lo = start - HALO
        xe_ = np.zeros((NSEQ, c.TU, D), f32)
        a, b = max(lo, 0), min(lo + c.TU, S)
        xe_[:, a - lo:b - lo, :] = xs[:, a:b, :]
        pos = np.arange(c.TE) + start - 1
        ok = (pos >= 0) & (pos < S)
        pc = np.clip(pos, 0, S - 1)
        inv = np.zeros((4, c.TE), f32)
        for gi, w in enumerate(WINDOWS):
            lo_w = np.clip(pos - w // 2, 0, S)
            hi_w = np.clip(pos + (w - w // 2), 0, S)
            cnt = np.maximum(hi_w - lo_w, 1)
            inv[gi] = (1.0 / cnt.astype(f32)).astype(f32)
        m = dict(common)
        m.update({
            "xe": xe_, "cosq": np.ascontiguousarray(cosk[pc]), "sinq": np.ascontiguousarray(sink[pc]),
            "invcnt": np.ascontiguousarray(np.broadcast_to(inv[None], (128, 4, c.TE))),
            "valid": ok.astype(f32).reshape(c.TE, 1),
        })
        maps.append(m)
    return maps


_FULL = None


def run(cfg, inp, debug=False):
    nc = build(cfg, debug)
    maps = host_inputs(cfg, inp)
    res = run_bass_kernel_spmd(nc, maps, core_ids=list(range(cfg.NCORE)))
    return res


def kernel(**inputs):
    cfg = Cfg()
    res = run(cfg, inputs)
    ys = np.stack([r["yout"] for r in res.results], 0)
    full = ys.transpose(1, 0, 2, 3).reshape(NSEQ, cfg.S, cfg.D)
    return (np.ascontiguousarray(full[:2]), np.ascontiguousarray(full[2:3]))
```
